# Optimizing a Trainium2 kernel written in Bass

```python
import jax, jax.numpy as jnp
from jax import lax
import numpy as np

D_MODEL = 1024
BATCH = 2
SEQ = 8192
DEPTH = 1
DEC_BATCH = 8
DEC_SEQ = 64
PAST_LEN = 1024

CHUNK = 64
SSD_EXPAND = 2
D_INNER = SSD_EXPAND * D_MODEL
SSD_HEAD_DIM = 64
SSD_HEADS = D_INNER // SSD_HEAD_DIM
SSD_GROUPS = 8
SSD_HPG = SSD_HEADS // SSD_GROUPS
D_STATE = 128
CONV_W = 4
CONV_DIM = D_INNER + 2 * SSD_GROUPS * D_STATE
D_POOL = D_MODEL
POOL_WINDOWS = (2, 4, 8, 16)
POOL_GROUPS = len(POOL_WINDOWS)
POOL_GDIM = D_POOL // POOL_GROUPS
POOL_MAX = max(POOL_WINDOWS)
N_BRANCH = 2
SPLIT_POINTS = (
    D_INNER,
    D_INNER + CONV_DIM,
    D_INNER + CONV_DIM + SSD_HEADS,
    D_INNER + CONV_DIM + SSD_HEADS + D_POOL,
    D_INNER + CONV_DIM + SSD_HEADS + 2 * D_POOL,
)
IN_COLS = SPLIT_POINTS[-1] + N_BRANCH * D_MODEL
EPS = 1e-6

kernel_name = "hybrid_ssd_pool_streaming_step"


def rmsnorm(x, g):
    xf = x.astype(jnp.float32)
    y = xf * lax.rsqrt(jnp.mean(xf * xf, axis=-1, keepdims=True) + EPS)
    return (y * g.astype(jnp.float32)).astype(x.dtype)


def gated_group_rmsnorm(y, z, g):
    yf = (y.astype(jnp.float32) * jax.nn.silu(z.astype(jnp.float32)))
    shp = yf.shape
    yg = yf.reshape(shp[:-1] + (SSD_GROUPS, D_INNER // SSD_GROUPS))
    yg = yg * lax.rsqrt(jnp.mean(yg * yg, axis=-1, keepdims=True) + EPS)
    return (yg.reshape(shp) * g.astype(jnp.float32)).astype(z.dtype)


def causal_dwconv(x_ext, w, b):
    out = lax.conv_general_dilated(
        x_ext, w[:, None, :].astype(x_ext.dtype), window_strides=(1,), padding='VALID',
        dimension_numbers=('NWC', 'WIO', 'NWC'), feature_group_count=x_ext.shape[-1])
    return out + b.astype(x_ext.dtype)


def ssd_chunk(h, x, dt, B, C, A):
    l = x.shape[1]
    acum = jnp.cumsum(dt * A, axis=1)
    causal = jnp.tril(jnp.ones((l, l), dtype=bool))[None, :, :, None, None]
    diff = acum[:, :, None] - acum[:, None, :]
    L = jnp.exp(jnp.where(causal, diff, -jnp.inf))
    CB = jnp.einsum('blgn,bsgn->blsg', C, B)
    xdt = x * dt[..., None]
    y_diag = jnp.einsum('blsgr,bsgrp->blgrp', CB[..., None] * L, xdt)
    y_off = jnp.einsum('blgn,bgrpn->blgrp', C, h) * jnp.exp(acum)[..., None]
    decay_end = jnp.exp(acum[:, -1:] - acum)
    h_new = (h * jnp.exp(acum[:, -1])[..., None, None]
             + jnp.einsum('blgn,blgrp->bgrpn', B, xdt * decay_end[..., None]))
    return y_diag + y_off, h_new


def ssd_scan(h, x, dt, B, C, A):
    b, t = x.shape[:2]
    if t <= CHUNK:
        return ssd_chunk(h, x, dt, B, C, A)
    nc = t // CHUNK

    def blocks(a):
        return jnp.swapaxes(a.reshape((b, nc, CHUNK) + a.shape[2:]), 0, 1)

    def step(carry, inp):
        xc, dtc, bc, cc = inp
        y, carry = ssd_chunk(carry, xc, dtc, bc, cc, A)
        return carry, y

    h_fin, ys = lax.scan(step, h, (blocks(x), blocks(dt), blocks(B), blocks(C)))
    y = jnp.swapaxes(ys, 0, 1).reshape((b, t) + x.shape[2:])
    return y, h_fin


def pool_mix(u_ext, pos0, pool_w, pool_scale):
    f32 = jnp.float32
    P = POOL_MAX - 1
    b, tot, _ = u_ext.shape
    t = tot - P
    uf = u_ext.astype(f32)
    cs = jnp.concatenate([jnp.zeros((b, 1, D_POOL), f32), jnp.cumsum(uf, axis=1)], axis=1)
    end = cs[:, P + 1:]
    cur = uf[:, P:]
    pos = pos0 + jnp.arange(t)
    outs = []
    for gi, w in enumerate(POOL_WINDOWS):
        sl = slice(gi * POOL_GDIM, (gi + 1) * POOL_GDIM)
        win = end[:, :, sl] - cs[:, P + 1 - w:P + 1 - w + t, sl]
        cnt = jnp.minimum(w, pos + 1).astype(f32)[None, :, None]
        outs.append(win / cnt - cur[:, :, sl])
    d = jnp.stack(outs, axis=2)
    o = jnp.einsum('btgc,gcd->btgd', d, pool_w.astype(f32)).reshape(b, t, D_POOL)
    return (o * pool_scale.astype(f32)).astype(u_ext.dtype)


def mixer_layer(x, conv_prev, pool_prev, h0, pos0, norm_g, w_in, conv_w, conv_b, dt_bias, a_log,
                d_skip, ssd_norm_g, w_branch_ssd, pool_w, pool_scale, w_branch_pool, gate_b, w_out):
    f32 = jnp.float32
    b, t, _ = x.shape
    n = rmsnorm(x, norm_g)
    proj = jnp.einsum('btd,de->bte', n, w_in)
    z, xbc, dt_raw, u, pool_gate, gate_logits = jnp.split(proj, SPLIT_POINTS, axis=-1)
    xbc_ext = jnp.concatenate([conv_prev.astype(xbc.dtype), xbc], axis=1)
    xbc_c = jax.nn.silu(causal_dwconv(xbc_ext, conv_w, conv_b))
    xs, bs, cs = jnp.split(xbc_c, [D_INNER, D_INNER + SSD_GROUPS * D_STATE], axis=-1)
    xs = xs.astype(f32).reshape(b, t, SSD_GROUPS, SSD_HPG, SSD_HEAD_DIM)
    bs = bs.astype(f32).reshape(b, t, SSD_GROUPS, D_STATE)
    cs = cs.astype(f32).reshape(b, t, SSD_GROUPS, D_STATE)
    dt = jax.nn.softplus(dt_raw.astype(f32) + dt_bias.astype(f32)).reshape(b, t, SSD_GROUPS, SSD_HPG)
    A = -jnp.exp(a_log.astype(f32)).reshape(SSD_GROUPS, SSD_HPG)
    h_init = h0.astype(f32).reshape(b, SSD_GROUPS, SSD_HPG, SSD_HEAD_DIM, D_STATE)
    y, h_fin = ssd_scan(h_init, xs, dt, bs, cs, A)
    y = y + d_skip.astype(f32).reshape(SSD_GROUPS, SSD_HPG)[:, :, None] * xs
    y = y.reshape(b, t, D_INNER)
    br_ssd = jnp.einsum('bti,id->btd', gated_group_rmsnorm(y, z, ssd_norm_g), w_branch_ssd)
    u_ext = jnp.concatenate([pool_prev.astype(u.dtype), u], axis=1)
    pooled = pool_mix(u_ext, pos0, pool_w, pool_scale)
    br_pool = jnp.einsum('btc,cd->btd', pooled * jax.nn.silu(pool_gate), w_branch_pool)
    gates = jax.nn.sigmoid((gate_logits + gate_b).astype(f32)).astype(x.dtype)
    g_ssd, g_pool = jnp.split(gates, N_BRANCH, axis=-1)
    out = jnp.einsum('btd,de->bte', g_ssd * br_ssd + g_pool * br_pool, w_out)
    new_h = h_fin.reshape(b, SSD_HEADS, SSD_HEAD_DIM, D_STATE)
    return x + out, xbc_ext[:, -(CONV_W - 1):], u_ext[:, -(POOL_MAX - 1):], new_h


def setup_inputs(seed: int = 0) -> dict:
    key = jax.random.key(seed)
    ks = jax.random.split(key, 24)
    f32 = jnp.float32
    nrm = lambda k, s: jax.random.normal(k, s, f32)
    dt0 = jnp.exp(jax.random.uniform(ks[7], (DEPTH, SSD_HEADS), f32, np.log(1e-3), np.log(1e-1)))
    return {
        "x_prompt": nrm(ks[0], (BATCH, SEQ, D_MODEL)),
        "x_sample": nrm(ks[1], (DEC_BATCH, DEC_SEQ, D_MODEL)),
        "state_ssd": 0.1 * nrm(ks[2], (DEPTH, DEC_BATCH, SSD_HEADS, SSD_HEAD_DIM, D_STATE)),
        "state_conv": nrm(ks[3], (DEPTH, DEC_BATCH, CONV_W - 1, CONV_DIM)),
        "state_pool": nrm(ks[4], (DEPTH, DEC_BATCH, POOL_MAX - 1, D_POOL)),
        "norm_g": 1.0 + 0.02 * nrm(ks[5], (DEPTH, D_MODEL)),
        "w_in": nrm(ks[6], (DEPTH, D_MODEL, IN_COLS)) * D_MODEL ** -0.5,
        "conv_w": 0.5 * nrm(ks[8], (DEPTH, CONV_W, CONV_DIM)),
        "conv_b": 0.02 * nrm(ks[9], (DEPTH, CONV_DIM)),
        "dt_bias": dt0 + jnp.log(-jnp.expm1(-dt0)),
        "a_log": jnp.log(jax.random.uniform(ks[10], (DEPTH, SSD_HEADS), f32, 1.0, 16.0)),
        "d_skip": 1.0 + 0.1 * nrm(ks[11], (DEPTH, SSD_HEADS)),
        "ssd_norm_g": 1.0 + 0.02 * nrm(ks[12], (DEPTH, D_INNER)),
        "w_branch_ssd": nrm(ks[13], (DEPTH, D_INNER, D_MODEL)) * D_INNER ** -0.5,
        "pool_w": nrm(ks[14], (DEPTH, POOL_GROUPS, POOL_GDIM, POOL_GDIM)) * POOL_GDIM ** -0.5,
        "pool_scale": 1.0 + 0.1 * nrm(ks[15], (DEPTH, D_POOL)),
        "w_branch_pool": nrm(ks[16], (DEPTH, D_POOL, D_MODEL)) * D_POOL ** -0.5,
        "gate_b": 0.02 * nrm(ks[17], (DEPTH, N_BRANCH * D_MODEL)),
        "w_out": nrm(ks[18], (DEPTH, D_MODEL, D_MODEL)) * D_MODEL ** -0.5,
        "final_g": 1.0 + 0.02 * nrm(ks[19], (D_MODEL,)),
    }


def reference(x_prompt, x_sample, state_ssd, state_conv, state_pool, norm_g, w_in, conv_w, conv_b,
              dt_bias, a_log, d_skip, ssd_norm_g, w_branch_ssd, pool_w, pool_scale, w_branch_pool,
              gate_b, w_out, final_g):
    bp = x_prompt.shape[0]
    hp, hs = x_prompt, x_sample
    ssd_p, ssd_s, conv_p, conv_s, pool_p, pool_s = [], [], [], [], [], []
    for l in range(DEPTH):
        lw = (norm_g[l], w_in[l], conv_w[l], conv_b[l], dt_bias[l], a_log[l], d_skip[l],
              ssd_norm_g[l], w_branch_ssd[l], pool_w[l], pool_scale[l], w_branch_pool[l],
              gate_b[l], w_out[l])
        zc = jnp.zeros((bp, CONV_W - 1, CONV_DIM), hp.dtype)
        zp = jnp.zeros((bp, POOL_MAX - 1, D_POOL), hp.dtype)
        zh = jnp.zeros((bp, SSD_HEADS, SSD_HEAD_DIM, D_STATE), jnp.float32)
        hp, c1, p1, s1 = mixer_layer(hp, zc, zp, zh, 0, *lw)
        hs, c2, p2, s2 = mixer_layer(hs, state_conv[l], state_pool[l], state_ssd[l], PAST_LEN, *lw)
        ssd_p.append(s1); ssd_s.append(s2)
        conv_p.append(c1); conv_s.append(c2)
        pool_p.append(p1); pool_s.append(p2)
    y_prompt = rmsnorm(hp, final_g)
    y_sample = rmsnorm(hs, final_g)
    return (y_prompt, y_sample, jnp.stack(ssd_p), jnp.stack(ssd_s), jnp.stack(conv_p),
            jnp.stack(conv_s), jnp.stack(pool_p), jnp.stack(pool_s))
```

```python
import contextlib
import numpy as np
import concourse.bass as bass
import concourse.mybir as mybir
from concourse.bass_utils import run_bass_kernel_spmd

F32 = mybir.dt.float32
BF16 = mybir.dt.bfloat16
AF = mybir.ActivationFunctionType
ALU = mybir.AluOpType

ENGS = ("pe", "act", "dve", "pool", "sp")
N_DMA_SEMS = 12
EPS = 1e-6


class Op:
    __slots__ = ("eng", "fn", "reads", "writes", "deps", "sig", "sig_idx", "dma_slot", "dma_cnt",
                 "is_dma", "idx", "barrier", "dma_snapshot")

    def __init__(self, eng, fn, reads, writes, is_dma, barrier=False):
        self.eng = eng
        self.fn = fn
        self.reads = tuple(reads)
        self.writes = tuple(writes)
        self.deps = []
        self.sig = False
        self.sig_idx = 0
        self.dma_slot = -1
        self.dma_cnt = 0
        self.is_dma = is_dma
        self.barrier = barrier
        self.dma_snapshot = None


class _Rec:
    def __init__(self):
        self.call = None

    def __getattr__(self, name):
        def f(*a, **k):
            self.call = (name, a, k)
            return self
        return f


def _eager(fn):
    r = _Rec()
    fn(r)
    name, a, k = r.call
    return lambda e: getattr(e, name)(*a, **k)


PSUM_BANK = {"pA0": "pA0", "pA1": "pA1", "pA2": "pA2", "pT": "pT", "pT2": "pT",
             "pL0a": "pL", "pL1a": "pL", "pL2a": "pL", "pL3a": "pL", "pSm": "pLb", "pL0b": "pLb", "pL1b": "pLb", "pL2b": "pLb", "pL3b": "pLb",
             "pY": "pY", "pZo": "pZS", "pS": "pZS", "pCB": "pZS", "pZ": "pCBt"}


def _flat(keys):
    out = []
    for k in keys:
        if isinstance(k, (list, tuple)):
            out.extend(_flat(k))
        else:
            out.append(k)
    return out


class Prog:
    def __init__(self):
        self.ops = []
        self.muted = False

    def add(self, eng, fn, reads=(), writes=()):
        if not self.muted:
            reads, writes = _flat(reads), _flat(writes)
            if eng != "pe":
                extra = ["bk_" + PSUM_BANK[k] for k in reads if k in PSUM_BANK]
            else:
                extra = ["bk_" + PSUM_BANK[k] for k in writes if k in PSUM_BANK]
            if extra:
                writes = list(writes) + sorted(set(extra))
            self.ops.append(Op(eng, _eager(fn), reads, writes, False))

    def dma(self, fn, reads=(), writes=(), eng="sp"):
        if not self.muted:
            self.ops.append(Op(eng, _eager(fn), reads, writes, True))

    def barrier(self):
        if not self.muted:
            self.ops.append(Op("all", None, (), (), False, barrier=True))

    def emit(self, nc):
        ops = self.ops
        last_w = {}
        readers = {}
        for i, op in enumerate(ops):
            op.idx = i
            if op.barrier:
                last_w = {}
                readers = {}
                continue
            deps = set()
            for k in op.reads:
                j = last_w.get(k)
                if j is not None:
                    deps.add((j, "raw"))
            for k in op.writes:
                j = last_w.get(k)
                if j is not None:
                    deps.add((j, "waw"))
                for j in readers.get(k, ()):
                    deps.add((j, "war"))
            for k in op.reads:
                lst = readers.setdefault(k, [])
                if not op.is_dma:
                    lst[:] = [j for j in lst if ops[j].is_dma or ops[j].eng != op.eng]
                lst.append(i)
            for k in op.writes:
                last_w[k] = i
                readers[k] = []
            best = set()
            for j, kind in deps:
                if j == i:
                    continue
                pj = ops[j]
                if pj.eng == op.eng and not pj.is_dma and not op.is_dma:
                    if op.eng == "pe":
                        continue
                    if kind != "raw":
                        continue
                best.add(j)
            op.deps = sorted(best)
            for j in op.deps:
                ops[j].sig = True
        cnt = {e: 0 for e in ENGS}
        dma_cnt = [0] * N_DMA_SEMS
        ndma = 0
        nsw = 0
        nhw = 0
        nbar = 0
        for op in ops:
            if op.barrier:
                op.dma_snapshot = list(dma_cnt)
                nbar += 1
                op.sig_idx = nbar
            elif op.is_dma:
                if op.eng == "pool":
                    op.dma_slot = N_DMA_SEMS // 2 + (nsw % (N_DMA_SEMS // 2))
                    nsw += 1
                else:
                    op.dma_slot = nhw % (N_DMA_SEMS // 2)
                    nhw += 1
                dma_cnt[op.dma_slot] += 1
                op.dma_cnt = dma_cnt[op.dma_slot]
                ndma += 1
            elif op.sig:
                cnt[op.eng] += 1
                op.sig_idx = cnt[op.eng]
        self.stats = dict(cnt, ndma=ndma, nops=len(ops), nbar=nbar)

        with contextlib.ExitStack() as es:
            esem = {e: es.enter_context(nc.semaphore("s_" + e)) for e in ("pe", "act", "dve", "pool")}
            dsem = [es.enter_context(nc.semaphore("s_dma%d" % i)) for i in range(N_DMA_SEMS)]
            bsem = es.enter_context(nc.semaphore("s_bar"))
            block = es.enter_context(nc.Block())

            def run(ename, eng):
                waited = {}

                def wait_dma(slot, v):
                    key = ("d", slot)
                    if v > 0 and waited.get(key, 0) < v:
                        eng.wait_ge(dsem[slot], v)
                        waited[key] = v

                for op in ops:
                    if op.barrier:
                        for s in range(N_DMA_SEMS):
                            wait_dma(s, 16 * op.dma_snapshot[s])
                        eng.drain().then_inc(bsem, 1)
                        eng.wait_ge(bsem, len(ENGS) * op.sig_idx)
                        continue
                    if op.eng != ename:
                        continue
                    if op.is_dma and op.dma_cnt > 1:
                        wait_dma(op.dma_slot, 16 * (op.dma_cnt - 1))
                    for j in op.deps:
                        pj = ops[j]
                        if pj.is_dma:
                            wait_dma(pj.dma_slot, 16 * pj.dma_cnt)
                        else:
                            key = ("e", pj.eng)
                            v = pj.sig_idx
                            if waited.get(key, 0) < v:
                                eng.wait_ge(esem[pj.eng], v)
                                waited[key] = v
                    ins = op.fn(eng)
                    if op.is_dma:
                        ins.then_inc(dsem[op.dma_slot], 16)
                    elif op.sig:
                        ins.then_inc(esem[op.eng], 1)
                if ename == "sp":
                    for s in range(N_DMA_SEMS):
                        wait_dma(s, 16 * dma_cnt[s])

            @block.tensor
            def _(e):
                run("pe", e)

            @block.scalar
            def _(e):
                run("act", e)

            @block.vector
            def _(e):
                run("dve", e)

            @block.gpsimd
            def _(e):
                run("pool", e)

            @block.sync
            def _(e):
                run("sp", e)


D = 1024
DI = 2048
NCOL = 10272
C_XBC = 2048
C_DT = 6144
C_U = 6176
C_PG = 7200
C_GS = 8224
C_GP = 9248
NPH = 1024
NTOKH = 16 + NPH + 64
NPRE = 6
POOL_W = (2, 4, 8, 16)


class _Stop(Exception):
    pass


def build_program(stop=None):
    nc = bass.Bass("TRN2", target_bir_lowering=False)

    def chk(name):
        if stop is not None and name == stop:
            P.muted = True

    di = lambda n, s: nc.dram_tensor(n, s, F32, kind="ExternalInput").ap()
    do = lambda n, s: nc.dram_tensor(n, s, F32, kind="ExternalOutput").ap()
    xin = di("xin", [2128, D])
    xpre = di("xpre", [NPRE * NPH + 16, D])
    pm_d = di("pm", [1, 8])
    w_in = di("w_in", [D, NCOL])
    w_bs = di("w_bs", [DI, D])
    w_bp = di("w_bp", [D, D])
    w_out = di("w_out", [D, D])
    pool_w = di("pool_w", [4, 256, 256])
    sst = di("sst", [128, 2048])
    scv = di("scv", [128, 32, 4])
    spl = di("spl", [128, 8, 15])
    ng_d = di("ng", [128, 8])
    cw_d = di("cw", [128, 32, 4])
    cb_d = di("cb", [128, 32])
    sng_d = di("sng", [128, 16])
    psc_d = di("psc", [128, 8])
    gb_d = di("gb", [128, 16])
    dtb_d = di("dtb", [1, 32])
    alog_d = di("alog", [1, 32])
    dsk_d = di("dsk", [1, 32])
    fg_d = di("fg", [1, D])
    pinv_d = di("pinv", [1, 64])
    yP = do("yP", [2048, D])
    yS = do("yS", [64, D])
    hPo = do("hPo", [128, 2048])
    hSo = do("hSo", [128, 2048])
    cvP = do("cvP", [128, 32, 3])
    cvS = do("cvS", [128, 32, 3])
    plP = do("plP", [128, 8, 15])
    plS = do("plS", [128, 8, 15])

    P = Prog()
    import os as _os2
    DBG = bool(_os2.environ.get("KDEBUG"))
    dbg_list = []

    def dbg_out(name, ap, shape, key):
        if not DBG:
            return
        t = nc.dram_tensor("dbg_" + name, list(shape), F32, kind="ExternalOutput").ap()
        dbg_list.append(name)
        P.dma(lambda e: e.dma_start(out=t, in_=ap), reads=[key], eng="pool")

    with contextlib.ExitStack() as es:
        def sb(name, shape, dt=F32, stack=es):
            return stack.enter_context(nc.sbuf_tensor(name, shape, dt))

        def ps(name, shape, dt=F32):
            return es.enter_context(nc.psum_tensor(name, shape, dt))

        pA = [ps("pA0", [128, 512]), ps("pA1", [128, 512])]
        pT = ps("pT", [128, 1024], BF16)
        pL = ps("pL", [128, 4, 128])
        pLb = ps("pLb", [128, 4, 128])
        pLs = [pL, pLb]
        pSm = pLb[:, :, :].rearrange("p a b -> p (a b)")[:, 0:256]
        pTb = pLb[:, :, :].rearrange("p a b -> p (a b)").bitcast(BF16)
        pLb_keys = ("pL0b", "pL1b", "pL2b", "pL3b", "pSm")
        pY = ps("pY", [128, 512])
        pZS = ps("pZS", [128, 512])
        pCBt = ps("pCBt", [128, 512])
        pa_ctr = [0]
        pa_ext = [False]
        pa_all = [(pA[0], ("pA0",)), (pA[1], ("pA1",)), (pCBt, ("pZ",)),
                  (pL[:, :, :].rearrange("p a b -> p (a b)"), ("pL0a", "pL1a", "pL2a", "pL3a")),
                  (pLb[:, :, :].rearrange("p a b -> p (a b)"), ("pL0b", "pL1b", "pL2b", "pL3b", "pSm")),
                  (pY, ("pY",))]

        def next_pA():
            n = 6 if pa_ext[0] else 2
            i = pa_ctr[0] % n
            pa_ctr[0] += 1
            return pa_all[i]

        tri = sb("tri", [128, 128])
        ones = sb("ones", [128, 128])
        identf = sb("identf", [128, 128])
        ident = sb("ident", [128, 128], BF16)
        trib = sb("trib", [128, 128], BF16)
        onesb = sb("onesb", [128, 128], BF16)
        a_hi = sb("a_hi", [128, 9 * 32], BF16)
        a_lo = sb("a_lo", [128, 9 * 32], BF16)
        ng = sb("ng_s", [128, 8])
        cw = sb("cw_s", [128, 32, 4])
        cb = sb("cb_s", [128, 32])
        sng = sb("sng_s", [128, 16])
        psc = sb("psc_s", [128, 8])
        gb = sb("gb_s", [128, 16])
        dtb = sb("dtb_s", [128, 32])
        Abc = sb("Abc", [128, 32])
        dsk = sb("dsk_s", [128, 32])
        fg = sb("fg_s", [128, D])
        pinv = sb("pinv_s", [128, 64])
        pm = sb("pm_s", [128, 8])
        scvb = sb("scvb", [128, 32, 4], BF16)
        splf = sb("splf", [128, 8, 15])
        Wdt = sb("Wdt", [128, 8, 32], BF16)
        cvPs = sb("cvPs", [128, 32, 3])
        cvSs = sb("cvSs", [128, 32, 3])
        plPs = sb("plPs", [128, 8, 15])
        plSs = sb("plSs", [128, 8, 15])
        hcar = sb("hcar", [128, 2048])
        NCH = 9
        dtn = ["dtx", "mx", "tA", "dt", "a", "nac", "e", "dtdec", "dec"]
        dta = {n: sb("dt_" + n, [128, NCH * 32]) for n in dtn}
        nT = sb("nT", [128, 8, NTOKH], BF16)
        MG = sb("MG", [128, 8, NPH + 64], BF16)

        P.add("pool", lambda e: e.memset(identf[:], 0.0), writes=["identf"])
        P.add("pool", lambda e: e.memset(tri[:], 1.0), writes=["tri"])
        P.add("pool", lambda e: e.memset(ones[:], 1.0), writes=["ones"])
        P.add("pool", lambda e: e.memset(hcar[:], 0.0), writes=["hcar"])
        P.add("pool", lambda e: e.affine_select(out=identf[:], in_=tri[:], pattern=[[-1, 128]], compare_op=ALU.is_equal,
                                                fill=0.0, base=0, channel_multiplier=1), reads=["tri"], writes=["identf"])
        P.add("pool", lambda e: e.affine_select(out=tri[:], in_=tri[:], pattern=[[1, 128]], compare_op=ALU.is_ge,
                                                fill=0.0, base=0, channel_multiplier=-1), reads=["tri", "identf"], writes=["tri"])
        P.add("dve", lambda e: e.tensor_copy(out=ident[:], in_=identf[:]), reads=["identf"], writes=["ident"])
        P.add("dve", lambda e: e.tensor_copy(out=trib[:], in_=tri[:]), reads=["tri"], writes=["trib"])
        P.add("dve", lambda e: e.tensor_copy(out=onesb[:], in_=ones[:]), reads=["ones"], writes=["onesb"])
        for dst, src, key in ((ng, ng_d, "ng"), (cw, cw_d, "cw"), (cb, cb_d, "cb"), (sng, sng_d, "sng"), (psc, psc_d, "psc"),
                              (gb, gb_d, "gb"), (splf, spl, "splf")):
            P.dma(lambda e, dst=dst, src=src: e.dma_start(out=dst[:], in_=src), writes=[key])
        for dst, src, key in ((dtb, dtb_d, "dtb"), (Abc, alog_d, "Abc"), (dsk, dsk_d, "dsk"), (fg, fg_d, "fg"), (pinv, pinv_d, "pinv"), (pm, pm_d, "pm")):
            P.dma(lambda e, dst=dst, src=src: e.dma_start(out=dst[:], in_=src[0:1, :].partition_broadcast(128)), writes=[key])
        P.dma(lambda e: e.dma_start(out=scvb[:], in_=scv), writes=["scvb"], eng="pool")
        P.dma(lambda e: e.dma_start(out=Wdt[:], in_=w_in[:, C_DT:C_DT + 32].rearrange("(k p) c -> p k c", p=128)), writes=["Wdt"], eng="pool")
        P.add("act", lambda e: e.activation(out=Abc[:], in_=Abc[:], func=AF.Exp), reads=["Abc"], writes=["Abc"])
        P.add("dve", lambda e: e.tensor_scalar(out=Abc[:], in0=Abc[:], scalar1=-1.0, scalar2=None, op0=ALU.mult), reads=["Abc"], writes=["Abc"])

        def wload(dst, src_rows_cols, key):
            P.dma(lambda e: e.dma_start(out=dst, in_=src_rows_cols.rearrange("(k p) c -> p k c", p=128)), writes=[key], eng="pool")

        class Prefetch:
            def __init__(self):
                self.th = []
                self.issued = 0

            def add(self, dst, src, key):
                self.th.append((dst, src, key))
                return len(self.th) - 1

            def need(self, i):
                while self.issued <= min(i + 2, len(self.th) - 1):
                    wload(*self.th[self.issued])
                    self.issued += 1

        def mm_group(out_ap, okey, pairs, rkeys):
            n = len(pairs)
            for i, (l, r) in enumerate(pairs):
                P.add("pe", lambda e, l=l, r=r, i=i: e.matmul(out_ap, lhsT=l, rhs=r, start=(i == 0), stop=(i == n - 1)),
                      reads=rkeys, writes=[okey])

        def rstd_from_ssq(ssq_ap, key, inv_n):
            P.add("dve", lambda e: e.tensor_scalar(out=ssq_ap, in0=ssq_ap, scalar1=inv_n, scalar2=EPS, op0=ALU.mult, op1=ALU.add),
                  reads=[key], writes=[key])
            P.add("act", lambda e: e.activation(out=ssq_ap, in_=ssq_ap, func=AF.Sqrt), reads=[key], writes=[key])
            P.add("dve", lambda e: e.reciprocal(out=ssq_ap, in_=ssq_ap), reads=[key], writes=[key])

        ys = contextlib.ExitStack()
        try:
          for blk in range(NPRE + 2):
            pre = blk < NPRE
            hf = -1 if pre else blk - NPRE
            xsrc, xrow0 = (xpre, blk * NPH) if pre else (xin, hf * NPH)
            has_s = (hf == 1)
            ncols = NTOKH if has_s else 16 + NPH
            ntok = NPH + 64 if has_s else NPH
            chunks = [(c, 16 + c * 128, 128, c * 128) for c in range(8)]
            if has_s:
                chunks.append((8, 16 + NPH, 64, NPH))
            fblocks = [(16, 512, 0), (528, 512, 512)]
            if has_s:
                fblocks.append((16 + NPH, 64, NPH))
            H = "pre_" if pre else "h%d_" % hf
            pa_ext[0] = pre

            if not pre:
                ys = contextlib.ExitStack()
                ynT = ys.enter_context(nc.sbuf_tensor("ynT%d" % hf, [128, 16, NPH + 64], BF16))
            if pre and blk > 0:
                new_alloc = False
            else:
                new_alloc = True
                sa = contextlib.ExitStack()
            if new_alloc:
                sba = lambda n, s, dt=F32, sa=sa, blk=blk: sb(n + "_b%d" % blk, s, dt, stack=sa)
                xt = [sba("xt%d" % i, [128, D]) for i in range(2)]
                xb = [sba("xb%d" % i, [128, D], BF16) for i in range(2)]
                ssq0 = [sba("ssq0_%d" % i, [128, 1]) for i in range(2)]
                Wx = [sba("Wx%d" % i, [128, 8, 512], BF16) for i in range(2)]
                Wz = [sba("Wz%d" % i, [128, 8, 256], BF16) for i in range(2)]
                rawP = sba("rawP", [128, 4, 4 + NPH], BF16)
                rawS = sba("rawS", [128, 4, 4 + 64], BF16)
                xsT = sba("xsT", [128, 2, NPH + 64], BF16)
                BT = sba("BT", [128, NPH + 64], BF16)
                CT = sba("CT", [128, NPH + 64], BF16)
                xBt = sba("xBt", [128, NCH, 384], BF16)
                hTa = sba("hTa", [128, NCH, 256], BF16)
                yza = sba("yza", [128, NCH, 256], BF16)
                ssqa = sba("ssqa", [128, NCH])
                dg = sba("dg", [128, 4, 4, 128], BF16)
                szs = sba("szs", [128, 256])
                CBm = [sba("CBm%d" % p_, [128, 128]) for p_ in range(2)]
                Eh = [[sba("E%d_%d" % (p_, i), [128, 128]) for i in range(4)] for p_ in range(2)]
                MT = [[sba("MT%d_%d" % (p_, i), [128, 128], BF16) for i in range(4)] for p_ in range(2)]
                t1 = [sba("t1_%d" % p_, [128, 256]) for p_ in range(2)]
                t3 = sba("t3", [128, 256])
                xdt = [sba("xdt%d" % p_, [128, 256], BF16) for p_ in range(2)]
                xd = [sba("xd%d" % p_, [128, 256], BF16) for p_ in range(2)]
                xdd = [sba("xdd%d" % p_, [128, 256], BF16) for p_ in range(2)]
                tmpH = sba("tmpH", [128, 256])
                hS = sba("hS", [128, 256])
                yn = [sba("yn%d" % p_, [128, 256], BF16) for p_ in range(2)]
                junk = sba("junkA", [128, 256], BF16)

            if True:
                def issue_w(g_):
                    wb_ = g_ % 2
                    wload(Wx[wb_][:, :, 0:256], w_in[:, C_XBC + g_ * 256: C_XBC + (g_ + 1) * 256], H + "Wx%da" % wb_)
                    wload(Wx[wb_][:, :, 256:384], w_in[:, C_XBC + 2048 + g_ * 128: C_XBC + 2048 + (g_ + 1) * 128], H + "Wx%db" % wb_)
                    if not pre:
                        wload(Wx[wb_][:, :, 384:512], w_in[:, C_XBC + 3072 + g_ * 128: C_XBC + 3072 + (g_ + 1) * 128], H + "Wx%dc" % wb_)
                        wload(Wz[wb_][:, :, :], w_in[:, g_ * 256:(g_ + 1) * 256], H + "Wz%d" % wb_)

                issue_w(0)
                for ti, j0 in enumerate(range(0, ncols, 128)):
                    R = min(128, ncols - j0)
                    b = ti % 2
                    xk, bk, sk = H + "xt%d" % b, H + "xb%d" % b, H + "ssq0%d" % b
                    if j0 + R <= 16 + NPH:
                        P.dma(lambda e, b=b, j0=j0, R=R: e.dma_start(out=xt[b][:R, :], in_=xsrc[xrow0 + j0: xrow0 + j0 + R, :]), writes=[xk])
                    else:
                        rp = 16 + NPH - j0
                        P.dma(lambda e, b=b, j0=j0, rp=rp: e.dma_start(out=xt[b][:rp, :], in_=xsrc[xrow0 + j0: xrow0 + j0 + rp, :]), writes=[xk])
                        P.dma(lambda e, b=b, rp=rp: e.dma_start(out=xt[b][rp:rp + 64, :], in_=xin[2064:2128, :]), writes=[xk])
                    P.add("act", lambda e, b=b, R=R: e.activation(out=xb[b][:R, :], in_=xt[b][:R, :], func=AF.Square, accum_out=ssq0[b][:R, :]),
                          reads=[xk], writes=[bk, sk])
                    rstd_from_ssq(ssq0[b][:R, :], sk, 1.0 / D)
                    P.add("dve", lambda e, b=b, R=R: e.tensor_scalar(out=xb[b][:R, :], in0=xt[b][:R, :], scalar1=ssq0[b][:R, 0:1], scalar2=None, op0=ALU.mult),
                          reads=[xk, sk], writes=[bk])
                    for k in range(8):
                        P.add("pe", lambda e, b=b, R=R, k=k: e.transpose(out=pT[:, k * 128:k * 128 + R], in_=xb[b][:R, k * 128:(k + 1) * 128], identity=ident[:R, :R]),
                              reads=[bk, "ident"], writes=["pT"])
                    P.add("dve", lambda e, R=R, j0=j0: e.tensor_tensor(out=nT[:, :, j0:j0 + R], in0=pT[:, :].rearrange("p (k t) -> p k t", k=8)[:, :, :R],
                                                                       in1=ng[:, :].unsqueeze(2).to_broadcast([128, 8, R]), op=ALU.mult),
                          reads=["pT", "ng"], writes=[H + "nT"])
                nTk = H + "nT"
                chk("S0_%d" % hf)

                import os as _os
                if "S0b" in _os.environ.get("KSKIP", ""):
                    P.muted = True
                dk = lambda n: H + "dt_" + n
                A3 = lambda n: dta[n][:, :].rearrange("p (c h) -> p c h", h=32)
                P.add("pool", lambda e: e.memset(dta["dtx"][:], 0.0), writes=[dk("dtx")])
                pD, pDk = next_pA()
                for (c, col0, T, tok0) in chunks:
                    mm_group(pD[:T, c * 32:(c + 1) * 32], pDk, [(nT[:, k, col0:col0 + T], Wdt[:, k, :]) for k in range(8)], [nTk, "Wdt"])
                    P.add("dve", lambda e, c=c, T=T: e.tensor_tensor(out=dta["dtx"][:T, c * 32:(c + 1) * 32], in0=pD[:T, c * 32:(c + 1) * 32], in1=dtb[:T, :], op=ALU.add),
                          reads=[pDk, "dtb"], writes=[dk("dtx")])
                chk("S0b1_%d" % hf)
                P.add("dve", lambda e: e.tensor_scalar(out=dta["mx"][:], in0=dta["dtx"][:], scalar1=0.0, scalar2=None, op0=ALU.max), reads=[dk("dtx")], writes=[dk("mx")])
                P.add("dve", lambda e: e.tensor_scalar(out=dta["tA"][:], in0=dta["dtx"][:], scalar1=0.0, scalar2=None, op0=ALU.min), reads=[dk("dtx")], writes=[dk("tA")])
                P.add("dve", lambda e: e.tensor_tensor(out=dta["tA"][:], in0=dta["tA"][:], in1=dta["mx"][:], op=ALU.subtract), reads=[dk("tA"), dk("mx")], writes=[dk("tA")])
                P.add("act", lambda e: e.activation(out=dta["tA"][:], in_=dta["tA"][:], func=AF.Exp), reads=[dk("tA")], writes=[dk("tA")])
                P.add("act", lambda e: e.activation(out=dta["tA"][:], in_=dta["tA"][:], func=AF.Ln, bias=1.0, scale=1.0), reads=[dk("tA")], writes=[dk("tA")])
                P.add("dve", lambda e: e.tensor_tensor(out=dta["dt"][:], in0=dta["mx"][:], in1=dta["tA"][:], op=ALU.add), reads=[dk("tA"), dk("mx")], writes=[dk("dt")])
                if pre:
                    P.add("dve", lambda e: e.tensor_scalar(out=dta["dt"][:], in0=dta["dt"][:], scalar1=pm[:, blk:blk + 1], scalar2=None, op0=ALU.mult),
                          reads=[dk("dt"), "pm"], writes=[dk("dt")])
                chk("S0b2_%d" % hf)
                P.add("dve", lambda e: e.tensor_tensor(out=A3("a"), in0=A3("dt"), in1=Abc[:, :].unsqueeze(1).to_broadcast([128, NCH, 32]), op=ALU.mult),
                      reads=[dk("dt"), "Abc"], writes=[dk("a")])
                chk("S0b3_%d" % hf)
                pC, pCk = next_pA()
                pE, pEk = next_pA()
                P.add("dve", lambda e: e.tensor_copy(out=a_hi[:], in_=dta["a"][:]), reads=[dk("a")], writes=[H + "a_hi"])
                P.add("dve", lambda e: e.tensor_tensor(out=a_lo[:], in0=dta["a"][:], in1=a_hi[:], op=ALU.subtract), reads=[dk("a"), H + "a_hi"], writes=[H + "a_lo"])
                ak = [H + "a_hi", H + "a_lo"]
                for pi, asrc in enumerate((a_hi, a_lo)):
                    P.add("pe", lambda e, asrc=asrc, pi=pi: e.matmul(pC[:, 0:256], lhsT=trib[:, :], rhs=asrc[:, 0:256], start=(pi == 0), stop=(pi == 1)), reads=["trib"] + ak, writes=[pCk])
                for pi, asrc in enumerate((a_hi, a_lo)):
                    P.add("pe", lambda e, asrc=asrc, pi=pi: e.matmul(pE[:, 0:256], lhsT=onesb[:, :], rhs=asrc[:, 0:256], start=(pi == 0), stop=(pi == 1)), reads=["onesb"] + ak, writes=[pEk])
                ncc = 256
                if has_s:
                    for pi, asrc in enumerate((a_hi, a_lo)):
                        P.add("pe", lambda e, asrc=asrc, pi=pi: e.matmul(pC[:64, 256:288], lhsT=trib[:64, :64], rhs=asrc[:64, 256:288], start=(pi == 0), stop=(pi == 1)), reads=["trib"] + ak, writes=[pCk])
                    for pi, asrc in enumerate((a_hi, a_lo)):
                        P.add("pe", lambda e, asrc=asrc, pi=pi: e.matmul(pE[:, 256:288], lhsT=onesb[:64, :], rhs=asrc[:64, 256:288], start=(pi == 0), stop=(pi == 1)), reads=["onesb"] + ak, writes=[pEk])
                    ncc = 288
                chk("S0b4_%d" % hf)
                P.add("dve", lambda e, ncc=ncc: e.tensor_scalar(out=dta["nac"][:, :ncc], in0=pC[:, :ncc], scalar1=-1.0, scalar2=None, op0=ALU.mult), reads=[pCk], writes=[dk("nac")])
                chk("S0b5_%d" % hf)
                P.add("act", lambda e, ncc=ncc: e.activation(out=dta["e"][:, :ncc], in_=dta["nac"][:, :ncc], func=AF.Exp, scale=-1.0), reads=[dk("nac")], writes=[dk("e")])
                chk("S0b6_%d" % hf)
                P.add("dve", lambda e, ncc=ncc: e.tensor_tensor(out=dta["dtdec"][:, :ncc], in0=pE[:, :ncc], in1=dta["nac"][:, :ncc], op=ALU.add), reads=[pEk, dk("nac")], writes=[dk("dtdec")])
                chk("S0b7_%d" % hf)
                P.add("act", lambda e, ncc=ncc: e.activation(out=dta["dtdec"][:, :ncc], in_=dta["dtdec"][:, :ncc], func=AF.Exp), reads=[dk("dtdec")], writes=[dk("dtdec")])
                P.add("dve", lambda e, ncc=ncc: e.tensor_tensor(out=dta["dtdec"][:, :ncc], in0=dta["dtdec"][:, :ncc], in1=dta["dt"][:, :ncc], op=ALU.mult), reads=[dk("dtdec"), dk("dt")], writes=[dk("dtdec")])
                chk("S0b8_%d" % hf)
                P.add("act", lambda e, ncc=ncc: e.activation(out=dta["dec"][:, :ncc], in_=pE[:, :ncc], func=AF.Exp), reads=[pEk], writes=[dk("dec")])

                if pre:
                    A3t = lambda n: dta[n][:, 0:256].rearrange("p (c h) -> p c h", h=32)
                    P.add("dve", lambda e: e.tensor_copy(out=dta["tA"][:, 0:256], in_=pE[:, 0:256]), reads=[pEk], writes=[dk("tA")])
                    P.add("pool", lambda e: e.memset(dta["mx"][:, 224:256], 0.0), reads=[dk("mx")], writes=[dk("mx")])
                    for c_ in range(6, -1, -1):
                        P.add("dve", lambda e: e.tensor_tensor(out=dta["mx"][:, c_ * 32:(c_ + 1) * 32], in0=dta["mx"][:, (c_ + 1) * 32:(c_ + 2) * 32],
                                                               in1=dta["tA"][:, (c_ + 1) * 32:(c_ + 2) * 32], op=ALU.add),
                              reads=[dk("mx"), dk("tA")], writes=[dk("mx")])
                    P.add("dve", lambda e: e.tensor_tensor(out=dta["e"][:, 0:32], in0=dta["mx"][:, 0:32], in1=dta["tA"][:, 0:32], op=ALU.add),
                          reads=[dk("mx"), dk("tA")], writes=[dk("e")])
                    P.add("act", lambda e: e.activation(out=dta["e"][:, 0:32], in_=dta["e"][:, 0:32], func=AF.Exp), reads=[dk("e")], writes=[dk("e")])
                    P.add("act", lambda e: e.activation(out=dta["mx"][:, 0:256], in_=dta["mx"][:, 0:256], func=AF.Exp), reads=[dk("mx")], writes=[dk("mx")])
                    P.add("dve", lambda e: e.tensor_tensor(out=dta["dtdec"][:, 0:256], in0=dta["dtdec"][:, 0:256], in1=dta["mx"][:, 0:256], op=ALU.mult),
                          reads=[dk("dtdec"), dk("mx")], writes=[dk("dtdec")])
                if "S0b" in _os.environ.get("KSKIP", ""):
                    P.muted = False
                if hf == 1:
                    for nm in ("dt", "a", "nac", "e", "dtdec", "dec"):
                        dbg_out(nm, dta[nm][:, :], [128, NCH * 32], dk(nm))
                chk("S0b_%d" % hf)
                ntile = 3 if pre else 4
                for g in range(8):
                    wb = g % 2
                    Wxk, Wzk = H + "Wx%d" % wb, H + "Wz%d" % wb
                    tiles = [2 * g, 2 * g + 1, 16 + g, 24 + g]
                    hd = slice(g * 4, g * 4 + 4)
                    for j in range(ntile):
                        for s_ in range(4):
                            P.add("dve", lambda e, j=j, s_=s_: e.tensor_scalar(out=dg[:, j, s_, :], in0=ident[:, :], scalar1=cw[:, tiles[j], s_:s_ + 1], scalar2=None, op0=ALU.mult),
                                  reads=["ident", "cw"], writes=[H + "dg%d" % j])

                    def proj(j):
                        tl = tiles[j]
                        wsl = lambda k: Wx[wb][:, k, j * 128:(j + 1) * 128]
                        rk = H + "rawP%d" % j
                        Wxk_j = Wxk + "aabc"[j]
                        rsk = H + "rawS%d" % j
                        pp, ppk = next_pA()
                        mm_group(pp[:, 0:16], ppk, [(wsl(k), nT[:, k, 0:16]) for k in range(8)], [nTk, Wxk_j])
                        P.add("act", lambda e: e.activation(out=rawP[:, j, 0:4], in_=pp[:, 12:16], func=AF.Copy), reads=[ppk], writes=[rk + "h"])
                        for bi in range(2):
                            pp, ppk = next_pA()
                            c0 = 16 + bi * 512
                            mm_group(pp[:, :], ppk, [(wsl(k), nT[:, k, c0:c0 + 512]) for k in range(8)], [nTk, Wxk_j])
                            P.add("act", lambda e: e.activation(out=rawP[:, j, 4 + bi * 512: 4 + (bi + 1) * 512], in_=pp[:, :], func=AF.Copy), reads=[ppk], writes=[rk + "b%d" % bi])
                            if has_s and bi == 1:
                                P.add("dve", lambda e: e.tensor_copy(out=cvPs[:, tl, :], in_=pp[:, 509:512]), reads=[ppk], writes=["cvPs"])
                        if has_s:
                            pp, ppk = next_pA()
                            c0 = 16 + NPH
                            mm_group(pp[:, 0:64], ppk, [(wsl(k), nT[:, k, c0:c0 + 64]) for k in range(8)], [nTk, Wxk_j])
                            P.add("act", lambda e: e.activation(out=rawS[:, j, 4:68], in_=pp[:, 0:64], func=AF.Copy), reads=[ppk], writes=[rsk])
                            P.add("dve", lambda e: e.tensor_copy(out=cvSs[:, tl, :], in_=pp[:, 61:64]), reads=[ppk], writes=["cvSs"])
                            P.add("pool", lambda e: e.tensor_copy(out=rawS[:, j, 0:4], in_=scvb[:, tl, :]), reads=["scvb"], writes=[rsk])

                    def conv(j):
                        tl = tiles[j]
                        rk = H + "rawP%d" % j
                        rsk = H + "rawS%d" % j
                        if j < 2:
                            dst = lambda a, b_: xsT[:, j, a:b_]
                            dkey = H + "xsT%d" % j
                        elif j == 2:
                            dst = lambda a, b_: BT[:, a:b_]
                            dkey = H + "BT"
                        else:
                            dst = lambda a, b_: CT[:, a:b_]
                            dkey = H + "CT"
                        for bi in range(2):
                            pp, ppk = next_pA()
                            rks = [rk + "h", rk + "b0"] if bi == 0 else [rk + "b0", rk + "b1"]
                            mm_group(pp[:, :], ppk, [(dg[:, j, s_, :], rawP[:, j, bi * 512 + s_ + 1: bi * 512 + s_ + 513]) for s_ in range(4)], [H + "dg%d" % j] + rks)
                            P.add("act", lambda e: e.activation(out=dst(bi * 512, (bi + 1) * 512), in_=pp[:, :], func=AF.Silu, bias=cb[:, tl:tl + 1], scale=1.0),
                                  reads=[ppk, "cb"], writes=[dkey + "_%d" % bi])
                        if has_s:
                            pp, ppk = next_pA()
                            mm_group(pp[:, 0:64], ppk, [(dg[:, j, s_, :], rawS[:, j, s_ + 1: s_ + 65]) for s_ in range(4)], [H + "dg%d" % j, rsk])
                            P.add("act", lambda e: e.activation(out=dst(NPH, NPH + 64), in_=pp[:, 0:64], func=AF.Silu, bias=cb[:, tl:tl + 1], scale=1.0),
                                  reads=[ppk, "cb"], writes=[dkey + "_2"])

                    proj(0)
                    for j in range(1, ntile):
                        proj(j)
                        conv(j - 1)
                    conv(ntile - 1)
                    if g + 1 < 8:
                        issue_w(g + 1)
                    chk("S1a_%d_%d" % (hf, g))

                    def ckeys(base, tok0):
                        return [base + "_%d" % (2 if tok0 >= NPH else tok0 // 512)]
                    for (c, col0, T, tok0) in chunks:
                        ptb = pTb if c % 2 else pT
                        pk = list(pLb_keys) if c % 2 else ["pT"]
                        for jj, (src, skey) in enumerate(((xsT[:, 0, tok0:tok0 + T], H + "xsT0"), (xsT[:, 1, tok0:tok0 + T], H + "xsT1"), (BT[:, tok0:tok0 + T], H + "BT"))):
                            P.add("pe", lambda e, src=src, jj=jj, T=T: e.transpose(out=ptb[:T, jj * 128:(jj + 1) * 128], in_=src, identity=ident[:, :]),
                                  reads=ckeys(skey, tok0) + ["ident"], writes=pk)
                        P.add("dve", lambda e, c=c, T=T: e.tensor_copy(out=xBt[:T, c, :], in_=ptb[:T, 0:384]), reads=pk, writes=[H + "xBt%d" % c])
                    hg = hcar[:, g * 256:(g + 1) * 256]
                    hgk = "hcar%d" % g
                    if pre:
                        for (c, col0, T, tok0) in chunks:
                            xk_ = H + "xBt%d" % c
                            xddb = xdd[c % 2]
                            xddk = H + "xdd%d" % (c % 2)
                            P.add("dve", lambda e: e.tensor_tensor(out=xddb[:T, :].rearrange("t (h p) -> t h p", h=4), in0=xBt[:T, c, 0:256].rearrange("t (h p) -> t h p", h=4),
                                                                   in1=A3("dtdec")[:T, c, hd].unsqueeze(2).to_broadcast([T, 4, 64]), op=ALU.mult),
                                  reads=[xk_, dk("dtdec")], writes=[xddk])
                            P.add("pe", lambda e: e.matmul(pZS[:, 256:512], lhsT=xBt[:T, c, 256:384], rhs=xddb[:T, :], start=(c == 0), stop=(c == 7)),
                                  reads=[xk_, xddk], writes=["pS"])
                        P.add("pool", lambda e: e.tensor_tensor(out=tmpH[:, :].rearrange("n (h p) -> n h p", h=4), in0=hg.rearrange("n (h p) -> n h p", h=4),
                                                                in1=dta["e"][:, hd].unsqueeze(2).to_broadcast([128, 4, 64]), op=ALU.mult),
                              reads=[hgk, dk("e")], writes=[H + "tmpH"])
                        P.add("dve", lambda e: e.tensor_tensor(out=hg, in0=pZS[:, 256:512], in1=tmpH[:, :], op=ALU.add), reads=["pS", H + "tmpH"], writes=[hgk])
                        continue
                    def state_ops(c, col0, T, tok0):
                        samp = (c == 8)
                        if samp:
                            P.dma(lambda e: e.dma_start(out=hS[:, :], in_=sst[:, g * 256:(g + 1) * 256]), writes=[H + "hS"])
                            hcur, hk = hS[:, :], H + "hS"
                        else:
                            hcur, hk = hg, hgk
                        xk_ = H + "xBt%d" % c
                        if not pre:
                            P.add("pool", lambda e, c=c, hcur=hcur: e.tensor_copy(out=hTa[:, c, :], in_=hcur), reads=[hk], writes=[H + "hTa%d" % c])
                        xddb = xdd[c % 2]
                        xddk = H + "xdd%d" % (c % 2)
                        P.add("dve", lambda e, c=c, T=T: e.tensor_tensor(out=xddb[:T, :].rearrange("t (h p) -> t h p", h=4), in0=xBt[:T, c, 0:256].rearrange("t (h p) -> t h p", h=4),
                                                                         in1=A3("dtdec")[:T, c, hd].unsqueeze(2).to_broadcast([T, 4, 64]), op=ALU.mult),
                              reads=[xk_, dk("dtdec")], writes=[xddk])
                        P.add("pe", lambda e, c=c, T=T: e.matmul(pSm, lhsT=xBt[:T, c, 256:384], rhs=xddb[:T, :], start=True, stop=True),
                              reads=[xk_, xddk], writes=["pSm"])
                        P.add("pool", lambda e, c=c, hcur=hcur: e.tensor_tensor(out=tmpH[:, :].rearrange("n (h p) -> n h p", h=4), in0=hcur.rearrange("n (h p) -> n h p", h=4),
                                                                                in1=A3("dec")[:, c, hd].unsqueeze(2).to_broadcast([128, 4, 64]), op=ALU.mult),
                              reads=[hk, dk("dec")], writes=[H + "tmpH"])
                        P.add("dve", lambda e, hcur=hcur: e.tensor_tensor(out=hcur, in0=pSm, in1=tmpH[:, :], op=ALU.add), reads=["pSm", H + "tmpH"], writes=[hk])
                        if samp:
                            P.dma(lambda e: e.dma_start(out=hSo[:, g * 256:(g + 1) * 256], in_=hS[:, :]), reads=[hk])
                        elif has_s and c == 7:
                            P.dma(lambda e: e.dma_start(out=hPo[:, g * 256:(g + 1) * 256], in_=hg), reads=[hk])
                    chk("S1b_%d_%d" % (hf, g))
                    if pre:
                        continue
                    pend_sq = [None]

                    def emit_sq(c, T):
                        P.add("act", lambda e: e.activation(out=junk[:T, :], in_=yza[:T, c, :], func=AF.Square, accum_out=ssqa[:T, c:c + 1]),
                              reads=[H + "yza%d" % c], writes=[H + "junk", H + "ssqa"])

                    for (c, col0, T, tok0) in chunks:
                        state_ops(c, col0, T, tok0)
                        xk_ = H + "xBt%d" % c
                        mm_group(pCBt[:T, 0:256], "pZ", [(nT[:, k, col0:col0 + T], Wz[wb][:, k, :]) for k in range(8)], [nTk, Wzk])
                        P.add("act", lambda e, T=T: e.activation(out=szs[:T, :], in_=pCBt[:T, 0:256], func=AF.Silu), reads=["pZ"], writes=[H + "sz"])
                        P.add("pe", lambda e, T=T, tok0=tok0: e.matmul(pZS[:T, 256:256 + T], lhsT=BT[:, tok0:tok0 + T], rhs=CT[:, tok0:tok0 + T], start=True, stop=True),
                              reads=ckeys(H + "BT", tok0) + ckeys(H + "CT", tok0), writes=["pCB"])
                        P.add("pe", lambda e, c=c, T=T, tok0=tok0: e.matmul(pZS[:T, 0:256], lhsT=CT[:, tok0:tok0 + T], rhs=hTa[:, c, :], start=True, stop=True),
                              reads=ckeys(H + "CT", tok0) + [H + "hTa%d" % c], writes=["pZo"])
                        for h in range(4):
                            for pi, asrc in enumerate((a_hi, a_lo)):
                                P.add("pe", lambda e, c=c, T=T, h=h, asrc=asrc, pi=pi: e.matmul(pL[:T, h, :T], lhsT=asrc[:T, c * 32 + g * 4 + h: c * 32 + g * 4 + h + 1].to_broadcast([T, T]),
                                                                                        rhs=trib[:T, :T], start=(pi == 0), stop=(pi == 1)),
                                      reads=[H + "a_hi", H + "a_lo", "trib"], writes=["pL%da" % h])
                        P.add("dve", lambda e, T=T: e.tensor_tensor(out=CBm[0][:T, :T], in0=pZS[:T, 256:256 + T], in1=tri[:T, :T], op=ALU.mult), reads=["pCB", "tri"], writes=[H + "CBm"])
                        P.add("pool", lambda e, c=c, T=T: e.tensor_tensor(out=xdt[0][:T, :].rearrange("t (h p) -> t h p", h=4), in0=xBt[:T, c, 0:256].rearrange("t (h p) -> t h p", h=4),
                                                                          in1=A3("dt")[:T, c, hd].unsqueeze(2).to_broadcast([T, 4, 64]), op=ALU.mult),
                              reads=[xk_, dk("dt")], writes=[H + "xdt"])
                        P.add("pool", lambda e, c=c, T=T: e.tensor_tensor(out=xd[0][:T, :].rearrange("t (h p) -> t h p", h=4), in0=xBt[:T, c, 0:256].rearrange("t (h p) -> t h p", h=4),
                                                                          in1=dsk[:T, hd].unsqueeze(2).to_broadcast([T, 4, 64]), op=ALU.mult),
                              reads=[xk_, "dsk"], writes=[H + "xd"])
                        for h in range(4):
                            P.add("act", lambda e, c=c, T=T, h=h: e.activation(out=Eh[0][h][:T, :T], in_=pL[:T, h, :T], func=AF.Exp,
                                                                               bias=dta["nac"][:T, c * 32 + g * 4 + h: c * 32 + g * 4 + h + 1], scale=1.0),
                                  reads=["pL%da" % h, dk("nac")], writes=[H + "E%d" % h])
                            P.add("dve", lambda e, T=T, h=h: e.scalar_tensor_tensor(out=MT[0][h][:T, :T], in0=Eh[0][h][:T, :T], scalar=1.0, in1=CBm[0][:T, :T], op0=ALU.min, op1=ALU.mult),
                                  reads=[H + "E%d" % h, H + "CBm"], writes=[H + "MT%d" % h])
                        P.add("pe", lambda e, T=T: e.matmul(pY[:T, 0:256], lhsT=ident[:T, :T], rhs=xd[0][:T, :], start=True, stop=False), reads=["ident", H + "xd"], writes=["pY"])
                        for h in range(4):
                            P.add("pe", lambda e, T=T, h=h: e.matmul(pY[:T, h * 64:(h + 1) * 64], lhsT=MT[0][h][:T, :T], rhs=xdt[0][:T, h * 64:(h + 1) * 64], start=False, stop=(h == 3)),
                                  reads=[H + "MT%d" % h, H + "xdt"], writes=["pY"])
                        P.add("dve", lambda e, c=c, T=T: e.tensor_tensor(out=t1[0][:T, :].rearrange("t (h p) -> t h p", h=4), in0=pZS[:T, 0:256].rearrange("t (h p) -> t h p", h=4),
                                                                         in1=A3("e")[:T, c, hd].unsqueeze(2).to_broadcast([T, 4, 64]), op=ALU.mult),
                              reads=["pZo", dk("e")], writes=[H + "t1"])
                        P.add("dve", lambda e, T=T: e.tensor_tensor(out=t3[:T, :], in0=pY[:T, 0:256], in1=t1[0][:T, :], op=ALU.add), reads=["pY", H + "t1"], writes=[H + "t3"])
                        P.add("pool", lambda e, c=c, T=T: e.tensor_tensor(out=yza[:T, c, :], in0=t3[:T, :], in1=szs[:T, :], op=ALU.mult), reads=[H + "t3", H + "sz"], writes=[H + "yza%d" % c])
                        if pend_sq[0] is not None:
                            emit_sq(*pend_sq[0])
                        pend_sq[0] = (c, T)
                    emit_sq(*pend_sq[0])
                    if hf == 1 and g == 0:
                        dbg_out("yza8", yza[:64, 8, :], [64, 256], H + "yza8")
                        dbg_out("yza0", yza[:, 0, :], [128, 256], H + "yza0")
                        dbg_out("ssqa", ssqa[:, :], [128, NCH], H + "ssqa")
                    nchk = len(chunks)
                    if has_s:
                        P.add("pool", lambda e: e.memset(ssqa[64:128, 8:9], 1.0), reads=[H + "ssqa"], writes=[H + "ssqa"])
                    rstd_from_ssq(ssqa[:, 0:nchk], H + "ssqa", 1.0 / 256)
                    for (c, col0, T, tok0) in chunks:
                        ynb = yn[c % 2]
                        ynk = H + "yn%d" % (c % 2)
                        P.add("dve", lambda e, c=c, T=T: e.tensor_scalar(out=ynb[:T, :], in0=yza[:T, c, :], scalar1=ssqa[:T, c:c + 1], scalar2=None, op0=ALU.mult),
                              reads=[H + "yza%d" % c, H + "ssqa"], writes=[ynk])
                        for jj in range(2):
                            P.add("pe", lambda e, T=T, jj=jj: e.transpose(out=pT[:, 512 + jj * 128: 512 + jj * 128 + T], in_=ynb[:T, jj * 128:(jj + 1) * 128], identity=ident[:T, :T]),
                                  reads=[ynk, "ident"], writes=["pT2"])
                        for jj in range(2):
                            P.add("act", lambda e, T=T, jj=jj, tok0=tok0: e.activation(out=ynT[:, 2 * g + jj, tok0:tok0 + T], in_=pT[:, 512 + jj * 128: 512 + jj * 128 + T], func=AF.Copy,
                                                                                      scale=sng[:, 2 * g + jj: 2 * g + jj + 1]),
                                  reads=["pT2", "sng"], writes=[H + "ynT"])
            if pre:
                if blk == NPRE - 1:
                    sa.close()
                    P.barrier()
                continue
            sa.close()
            chk("S1_%d" % hf)
            P.barrier()

            pa_ext[0] = True
            with contextlib.ExitStack() as sB:
                sbb = lambda n, s, dt=F32: sb(n + "_%d" % hf, s, dt, stack=sB)
                Wbs = [sbb("Wbs%d" % i, [128, 16, 128], BF16) for i in range(2)]
                Wg1 = [sbb("Wg1%d" % i, [128, 8, 128], BF16) for i in range(2)]
                gsb = [sbb("gsb%d" % i, [128, 512]) for i in range(2)]
                ctr = 0
                pf = Prefetch()
                pfi = {}
                for m in range(8):
                    wb = m % 2
                    pfi[("g1", m)] = pf.add(Wg1[wb][:, :, :], w_in[:, C_GS + m * 128: C_GS + (m + 1) * 128], H + "Wg1%d" % wb)
                    pfi[("bs", m)] = pf.add(Wbs[wb][:, :, :], w_bs[:, m * 128:(m + 1) * 128], H + "Wbs%d" % wb)
                for m in range(8):
                    wb = m % 2
                    pf.need(pfi[("bs", m)])
                    for (c0, N, tok0) in fblocks:
                        gi = ctr % 2
                        ctr += 1
                        pp, ppk = next_pA()
                        mm_group(pp[:, :N], ppk, [(Wg1[wb][:, k, :], nT[:, k, c0:c0 + N]) for k in range(8)], [nTk, H + "Wg1%d" % wb])
                        P.add("act", lambda e, pp=pp, N=N, gi=gi, m=m: e.activation(out=gsb[gi][:, :N], in_=pp[:, :N], func=AF.Sigmoid, bias=gb[:, m:m + 1], scale=1.0),
                              reads=[ppk, "gb"], writes=[H + "gsb%d" % gi])
                        pp2, ppk2 = next_pA()
                        mm_group(pp2[:, :N], ppk2, [(Wbs[wb][:, kt, :], ynT[:, kt, tok0:tok0 + N]) for kt in range(16)], [H + "ynT", H + "Wbs%d" % wb])
                        P.add("dve", lambda e, pp2=pp2, N=N, gi=gi, m=m, tok0=tok0: e.tensor_tensor(out=MG[:, m, tok0:tok0 + N], in0=pp2[:, :N], in1=gsb[gi][:, :N], op=ALU.mult),
                              reads=[ppk2, H + "gsb%d" % gi], writes=[H + "MG%d" % m])
            chk("S2_%d" % hf)
            P.barrier()
            ys.close()

            with contextlib.ExitStack() as sC:
                sbc = lambda n, s, dt=F32: sb(n + "_%d" % hf, s, dt, stack=sC)
                PGT = sbc("PGT", [128, 8, NPH + 64], BF16)
                Wu = [sbc("Wu%d" % i, [128, 8, 128], BF16) for i in range(2)]
                Wpg = [sbc("Wpg%d" % i, [128, 8, 128], BF16) for i in range(2)]
                Wgp = [sbc("Wgp%d" % i, [128, 8, 128], BF16) for i in range(2)]
                Wbp = [sbc("Wbp%d" % i, [128, 8, 128], BF16) for i in range(2)]
                PW = [sbc("PW%d" % i, [128, 2, 256], BF16) for i in range(2)]
                Wo = sbc("Wo", [128, 8, D], BF16)
                LP = 16 + NPH
                ubP = sbc("ubP", [128, LP])
                ubS = sbc("ubS", [128, 15 + 64])
                stP = [sbc("stP%d" % i, [128, LP]) for i in range(2)]
                stS = [sbc("stS%d" % i, [128, 15 + 64]) for i in range(2)]
                dT = sbc("dT", [128, 2, NPH + 64], BF16)
                spg = [sbc("spg%d" % i, [128, 512]) for i in range(2)]
                tmpc = [sbc("tmpc%d" % i, [128, 512]) for i in range(2)]
                xr = [sbc("xr%d" % i, [128, D]) for i in range(2)]
                yo = [sbc("yo%d" % i, [128, D]) for i in range(2)]
                junkc = sbc("junkc", [128, D], BF16)
                ssq4 = [sbc("ssq4_%d" % i, [128, 1]) for i in range(2)]
                wload(Wo[:, :, :], w_out[:, :], H + "Wo")
                ctr = 0
                pf = Prefetch()
                pfi = {}
                for gi_ in range(4):
                    pfi[("pw", gi_)] = pf.add(PW[gi_ % 2][:, :, :], pool_w[gi_], H + "PW%d" % (gi_ % 2))
                    for j in range(2):
                        ut = gi_ * 2 + j
                        pfi[("u", ut)] = pf.add(Wu[ut % 2][:, :, :], w_in[:, C_U + ut * 128: C_U + (ut + 1) * 128], H + "Wu%d" % (ut % 2))
                    for mo in range(2):
                        ot = gi_ * 2 + mo
                        pfi[("pg", ot)] = pf.add(Wpg[ot % 2][:, :, :], w_in[:, C_PG + ot * 128: C_PG + (ot + 1) * 128], H + "Wpg%d" % (ot % 2))
                for m in range(8):
                    pfi[("gp", m)] = pf.add(Wgp[m % 2][:, :, :], w_in[:, C_GP + m * 128: C_GP + (m + 1) * 128], H + "Wgp%d" % (m % 2))
                    pfi[("bp", m)] = pf.add(Wbp[m % 2][:, :, :], w_bp[:, m * 128:(m + 1) * 128], H + "Wbp%d" % (m % 2))
                for gi_ in range(4):
                    w = POOL_W[gi_]
                    pwb = gi_ % 2
                    pf.need(pfi[("pw", gi_)])
                    for j in range(2):
                        ut = gi_ * 2 + j
                        ub_ = ut % 2
                        pf.need(pfi[("u", ut)])
                        pp, ppk = next_pA()
                        mm_group(pp[:, 0:16], ppk, [(Wu[ub_][:, k, :], nT[:, k, 0:16]) for k in range(8)], [nTk, H + "Wu%d" % ub_])
                        P.add("act", lambda e, pp=pp: e.activation(out=ubP[:, 0:16], in_=pp[:, 0:16], func=AF.Copy), reads=[ppk], writes=[H + "ubP"])
                        for bi in range(2):
                            pp, ppk = next_pA()
                            c0 = 16 + bi * 512
                            mm_group(pp[:, :], ppk, [(Wu[ub_][:, k, :], nT[:, k, c0:c0 + 512]) for k in range(8)], [nTk, H + "Wu%d" % ub_])
                            P.add("act", lambda e, pp=pp, c0=c0: e.activation(out=ubP[:, c0:c0 + 512], in_=pp[:, :], func=AF.Copy), reads=[ppk], writes=[H + "ubP"])
                        seqs = [(ubP, stP, LP, 16, 0, H + "ubP", H + "stP")]
                        if has_s:
                            pp, ppk = next_pA()
                            c0 = 16 + NPH
                            mm_group(pp[:, 0:64], ppk, [(Wu[ub_][:, k, :], nT[:, k, c0:c0 + 64]) for k in range(8)], [nTk, H + "Wu%d" % ub_])
                            P.add("act", lambda e, pp=pp: e.activation(out=ubS[:, 15:79], in_=pp[:, 0:64], func=AF.Copy), reads=[ppk], writes=[H + "ubS"])
                            P.add("pool", lambda e, ut=ut: e.tensor_copy(out=ubS[:, 0:15], in_=splf[:, ut, :]), reads=["splf"], writes=[H + "ubS"])
                            P.add("dve", lambda e, ut=ut: e.tensor_copy(out=plPs[:, ut, :], in_=ubP[:, LP - 15:LP]), reads=[H + "ubP"], writes=["plPs"])
                            P.add("dve", lambda e, ut=ut: e.tensor_copy(out=plSs[:, ut, :], in_=ubS[:, 64:79]), reads=[H + "ubS"], writes=["plSs"])
                            seqs.append((ubS, stS, 79, 15, NPH, H + "ubS", H + "stS"))
                        for (ub, st, L, v0, tk0, ubk, stk) in seqs:
                            cur, curk = ub, ubk
                            sh = 1
                            lo = 0
                            si = 0
                            while sh < w:
                                nlo = lo + sh
                                dstb, dstk = st[si], stk + "%d" % si
                                eng = "pool" if si == 0 else "dve"
                                P.add(eng, lambda e, cur=cur, dstb=dstb, nlo=nlo, sh=sh, L=L: e.tensor_tensor(out=dstb[:, nlo:L], in0=cur[:, nlo:L], in1=cur[:, nlo - sh:L - sh], op=ALU.add),
                                      reads=[curk], writes=[dstk])
                                cur, curk = dstb, dstk
                                lo = nlo
                                sh *= 2
                                si ^= 1
                            nv = L - v0
                            P.add("dve", lambda e, cur=cur, ub=ub, v0=v0, L=L, j=j, tk0=tk0, nv=nv, w=w: e.scalar_tensor_tensor(
                                out=dT[:, j, tk0:tk0 + nv], in0=cur[:, v0:L], scalar=1.0 / w, in1=ub[:, v0:L], op0=ALU.mult, op1=ALU.subtract),
                                reads=[curk, ubk], writes=[H + "dT%d" % j])
                            if hf == 0 and tk0 == 0:
                                P.add("dve", lambda e, cur=cur, gi_=gi_: e.tensor_tensor(out=tmpc[0][:, 0:16], in0=cur[:, 16:32], in1=pinv[:, gi_ * 16:(gi_ + 1) * 16], op=ALU.mult),
                                      reads=[curk, "pinv"], writes=[H + "tmpc0"])
                                P.add("dve", lambda e, ub=ub, j=j: e.tensor_tensor(out=dT[:, j, 0:16], in0=tmpc[0][:, 0:16], in1=ub[:, 16:32], op=ALU.subtract),
                                      reads=[H + "tmpc0", ubk], writes=[H + "dT%d" % j])
                    for mo in range(2):
                        ot = gi_ * 2 + mo
                        wb = ot % 2
                        pf.need(pfi[("pg", ot)])
                        for (c0, N, tok0) in fblocks:
                            gi = ctr % 2
                            ctr += 1
                            pp, ppk = next_pA()
                            mm_group(pp[:, :N], ppk, [(Wpg[wb][:, k, :], nT[:, k, c0:c0 + N]) for k in range(8)], [nTk, H + "Wpg%d" % wb])
                            P.add("act", lambda e, pp=pp, N=N, gi=gi: e.activation(out=spg[gi][:, :N], in_=pp[:, :N], func=AF.Silu), reads=[ppk], writes=[H + "spg%d" % gi])
                            pp2, ppk2 = next_pA()
                            mm_group(pp2[:, :N], ppk2, [(PW[pwb][:, kt, mo * 128:(mo + 1) * 128], dT[:, kt, tok0:tok0 + N]) for kt in range(2)], [H + "dT0", H + "dT1", H + "PW%d" % pwb])
                            P.add("dve", lambda e, pp2=pp2, N=N, gi=gi, ot=ot, tok0=tok0: e.scalar_tensor_tensor(out=PGT[:, ot, tok0:tok0 + N], in0=pp2[:, :N], scalar=psc[:, ot:ot + 1],
                                                                                                               in1=spg[gi][:, :N], op0=ALU.mult, op1=ALU.mult),
                                  reads=[ppk2, H + "spg%d" % gi, "psc"], writes=[H + "PGT"])
                for m in range(8):
                    wb = m % 2
                    pf.need(pfi[("bp", m)])
                    for (c0, N, tok0) in fblocks:
                        gi = ctr % 2
                        ctr += 1
                        pp, ppk = next_pA()
                        mm_group(pp[:, :N], ppk, [(Wgp[wb][:, k, :], nT[:, k, c0:c0 + N]) for k in range(8)], [nTk, H + "Wgp%d" % wb])
                        P.add("act", lambda e, pp=pp, N=N, gi=gi, m=m: e.activation(out=spg[gi][:, :N], in_=pp[:, :N], func=AF.Sigmoid, bias=gb[:, 8 + m:9 + m], scale=1.0),
                              reads=[ppk, "gb"], writes=[H + "spg%d" % gi])
                        pp2, ppk2 = next_pA()
                        mm_group(pp2[:, :N], ppk2, [(Wbp[wb][:, k, :], PGT[:, k, tok0:tok0 + N]) for k in range(8)], [H + "PGT", H + "Wbp%d" % wb])
                        P.add("dve", lambda e, pp2=pp2, N=N, gi=gi: e.tensor_tensor(out=tmpc[gi][:, :N], in0=pp2[:, :N], in1=spg[gi][:, :N], op=ALU.mult),
                              reads=[ppk2, H + "spg%d" % gi], writes=[H + "tmpc%d" % gi])
                        P.add("pool", lambda e, N=N, gi=gi, m=m, tok0=tok0: e.tensor_tensor(out=MG[:, m, tok0:tok0 + N], in0=MG[:, m, tok0:tok0 + N], in1=tmpc[gi][:, :N], op=ALU.add),
                              reads=[H + "tmpc%d" % gi, H + "MG%d" % m], writes=[H + "MG%d" % m])
                chk("S3_%d" % hf)
                mgk = [H + "MG%d" % m for m in range(8)]
                for (c, col0, T, tok0) in chunks:
                    b = c % 2
                    samp = (c == 8)
                    if samp:
                        P.dma(lambda e, b=b: e.dma_start(out=xr[b][:64, :], in_=xin[2064:2128, :]), writes=[H + "xr%d" % b])
                    else:
                        r0 = 16 + hf * NPH + c * 128
                        P.dma(lambda e, b=b, r0=r0: e.dma_start(out=xr[b][:, :], in_=xin[r0:r0 + 128, :]), writes=[H + "xr%d" % b])
                    for half in range(2):
                        pp, ppk = next_pA()
                        mm_group(pp[:T, :], ppk, [(MG[:, k, tok0:tok0 + T], Wo[:, k, half * 512:(half + 1) * 512]) for k in range(8)], mgk + [H + "Wo"])
                        P.add("dve", lambda e, pp=pp, T=T, b=b, half=half: e.tensor_tensor(out=xr[b][:T, half * 512:(half + 1) * 512], in0=pp[:T, :], in1=xr[b][:T, half * 512:(half + 1) * 512], op=ALU.add),
                              reads=[ppk, H + "xr%d" % b], writes=[H + "xr%d" % b])
                    P.add("act", lambda e, T=T, b=b: e.activation(out=junkc[:T, :], in_=xr[b][:T, :], func=AF.Square, accum_out=ssq4[b][:T, :]),
                          reads=[H + "xr%d" % b], writes=[H + "junkc", H + "ssq4%d" % b])
                    rstd_from_ssq(ssq4[b][:T, :], H + "ssq4%d" % b, 1.0 / D)
                    P.add("dve", lambda e, T=T, b=b: e.scalar_tensor_tensor(out=yo[b][:T, :], in0=xr[b][:T, :], scalar=ssq4[b][:T, 0:1], in1=fg[:T, :], op0=ALU.mult, op1=ALU.mult),
                          reads=[H + "xr%d" % b, H + "ssq4%d" % b, "fg"], writes=[H + "yo%d" % b])
                    if samp:
                        P.dma(lambda e, b=b: e.dma_start(out=yS[:, :], in_=yo[b][:64, :]), reads=[H + "yo%d" % b])
                    else:
                        r0 = hf * NPH + c * 128
                        P.dma(lambda e, b=b, r0=r0: e.dma_start(out=yP[r0:r0 + 128, :], in_=yo[b][:, :]), reads=[H + "yo%d" % b])
            P.barrier()
        except _Stop:
            pass
        P.muted = False
        P.dma(lambda e: e.dma_start(out=cvP, in_=cvPs[:]), reads=["cvPs"])
        P.dma(lambda e: e.dma_start(out=cvS, in_=cvSs[:]), reads=["cvSs"])
        P.dma(lambda e: e.dma_start(out=plP, in_=plPs[:]), reads=["plPs"])
        P.dma(lambda e: e.dma_start(out=plS, in_=plSs[:]), reads=["plSs"])
        P.emit(nc)
    P.stats["dbg"] = dbg_list
    return nc, P.stats


_CACHE = {}
STOP = None


def kernel(x_prompt, x_sample, state_ssd, state_conv, state_pool, norm_g, w_in, conv_w, conv_b,
           dt_bias, a_log, d_skip, ssd_norm_g, w_branch_ssd, pool_w, pool_scale, w_branch_pool,
           gate_b, w_out, final_g):
    f = lambda a: np.ascontiguousarray(np.asarray(a, dtype=np.float32))
    x_prompt, x_sample = f(x_prompt), f(x_sample)
    if "nc" not in _CACHE:
        _CACHE["nc"] = build_program(STOP)
    nc, stats = _CACHE["nc"]
    shared = {
        "w_in": f(w_in[0]), "w_bs": f(w_branch_ssd[0]), "w_bp": f(w_branch_pool[0]), "w_out": f(w_out[0]),
        "pool_w": f(pool_w[0]),
        "ng": f(np.asarray(norm_g[0]).reshape(8, 128).T),
        "cw": f(np.asarray(conv_w[0]).reshape(4, 32, 128).transpose(2, 1, 0)),
        "cb": f(np.asarray(conv_b[0]).reshape(32, 128).T),
        "sng": f(np.asarray(ssd_norm_g[0]).reshape(16, 128).T),
        "psc": f(np.asarray(pool_scale[0]).reshape(8, 128).T),
        "gb": f(np.asarray(gate_b[0]).reshape(16, 128).T),
        "dtb": f(np.asarray(dt_bias[0]).reshape(1, 32)), "alog": f(np.asarray(a_log[0]).reshape(1, 32)),
        "dsk": f(np.asarray(d_skip[0]).reshape(1, 32)), "fg": f(np.asarray(final_g).reshape(1, 1024)),
    }
    in_maps = []
    for q in range(8):
        seq, part = q // 4, q % 4
        s0 = part * 2048
        halo = x_prompt[seq, s0 - 16:s0] if part > 0 else np.zeros((16, D), np.float32)
        xin = np.concatenate([halo, x_prompt[seq, s0:s0 + 2048], x_sample[q]], axis=0)
        pinv = np.ones((4, 16), np.float32)
        for gi, w in enumerate(POOL_W):
            for t in range(16):
                pinv[gi, t] = 1.0 / min(w, s0 + t + 1)
        m = dict(shared)
        m["xin"] = f(xin)
        m["sst"] = f(np.asarray(state_ssd[0, q]).reshape(2048, 128).T)
        scv3 = np.asarray(state_conv[0, q]).reshape(3, 32, 128).transpose(2, 1, 0)
        m["scv"] = f(np.concatenate([np.zeros((128, 32, 1), np.float32), scv3], axis=2))
        m["spl"] = f(np.asarray(state_pool[0, q]).reshape(15, 8, 128).transpose(2, 1, 0))
        m["pinv"] = f(pinv.reshape(1, 64))
        npre_rows = NPRE * NPH + 16
        xp = np.zeros((npre_rows, D), np.float32)
        have = min(s0, npre_rows)
        if have > 0:
            xp[npre_rows - have:] = x_prompt[seq, s0 - have:s0]
        pmv = np.zeros((1, 8), np.float32)
        for k in range(NPRE):
            if s0 - NPRE * NPH + k * NPH >= 0:
                pmv[0, k] = 1.0
        m["xpre"] = xp
        m["pm"] = pmv
        in_maps.append(m)
    res = run_bass_kernel_spmd(nc, in_maps, core_ids=list(range(8)))
    R = res.results
    global LAST_RESULTS
    LAST_RESULTS = R
    y_prompt = np.zeros((2, 8192, D), np.float32)
    y_sample = np.zeros((8, 64, D), np.float32)
    ssd_p = np.zeros((1, 2, 32, 64, 128), np.float32)
    ssd_s = np.zeros((1, 8, 32, 64, 128), np.float32)
    conv_p = np.zeros((1, 2, 3, 4096), np.float32)
    conv_s = np.zeros((1, 8, 3, 4096), np.float32)
    pool_p = np.zeros((1, 2, 15, 1024), np.float32)
    pool_s = np.zeros((1, 8, 15, 1024), np.float32)
    for q in range(8):
        seq, part = q // 4, q % 4
        r = R[q]
        y_prompt[seq, part * 2048:(part + 1) * 2048] = r["yP"]
        y_sample[q] = r["yS"]
        ssd_s[0, q] = np.asarray(r["hSo"]).T.reshape(32, 64, 128)
        conv_s[0, q] = np.asarray(r["cvS"]).transpose(2, 1, 0).reshape(3, 4096)
        pool_s[0, q] = np.asarray(r["plS"]).transpose(2, 1, 0).reshape(15, 1024)
        if part == 3:
            ssd_p[0, seq] = np.asarray(r["hPo"]).T.reshape(32, 64, 128)
            conv_p[0, seq] = np.asarray(r["cvP"]).transpose(2, 1, 0).reshape(3, 4096)
            pool_p[0, seq] = np.asarray(r["plP"]).transpose(2, 1, 0).reshape(15, 1024)
    return (y_prompt, y_sample, ssd_p, ssd_s, conv_p, conv_s, pool_p, pool_s)
```

```python
import contextlib
import numpy as np
import concourse.bass as bass
import concourse.mybir as mybir
from concourse.bass_utils import run_bass_kernel_spmd

F32 = mybir.dt.float32
BF16 = mybir.dt.bfloat16
AF = mybir.ActivationFunctionType
ALU = mybir.AluOpType

ENGS = ("pe", "act", "dve", "pool", "sp")
N_DMA_SEMS = 12
EPS = 1e-6


class Op:
    __slots__ = ("eng", "fn", "reads", "writes", "deps", "sig", "sig_idx", "dma_slot", "dma_cnt",
                 "is_dma", "idx", "barrier", "dma_snapshot")

    def __init__(self, eng, fn, reads, writes, is_dma, barrier=False):
        self.eng = eng
        self.fn = fn
        self.reads = tuple(reads)
        self.writes = tuple(writes)
        self.deps = []
        self.sig = False
        self.sig_idx = 0
        self.dma_slot = -1
        self.dma_cnt = 0
        self.is_dma = is_dma
        self.barrier = barrier
        self.dma_snapshot = None


class _Rec:
    def __init__(self):
        self.call = None

    def __getattr__(self, name):
        def f(*a, **k):
            self.call = (name, a, k)
            return self
        return f


def _eager(fn):
    r = _Rec()
    fn(r)
    name, a, k = r.call
    return lambda e: getattr(e, name)(*a, **k)


PSUM_BANK = {"pA0": "pA0", "pA1": "pA1", "pA2": "pA2", "pT": "pT", "pT2": "pT",
             "pL0a": "pL", "pL1a": "pL", "pL2a": "pL", "pL3a": "pL", "pSm": "pLb", "pL0b": "pLb", "pL1b": "pLb", "pL2b": "pLb", "pL3b": "pLb",
             "pY": "pY", "pZo": "pZS", "pS": "pZS", "pCB": "pZS", "pZ": "pCBt"}


def _flat(keys):
    out = []
    for k in keys:
        if isinstance(k, (list, tuple)):
            out.extend(_flat(k))
        else:
            out.append(k)
    return out


class Prog:
    def __init__(self):
        self.ops = []
        self.muted = False

    def add(self, eng, fn, reads=(), writes=()):
        if not self.muted:
            reads, writes = _flat(reads), _flat(writes)
            if eng != "pe":
                extra = ["bk_" + PSUM_BANK[k] for k in reads if k in PSUM_BANK]
            else:
                extra = ["bk_" + PSUM_BANK[k] for k in writes if k in PSUM_BANK]
            if extra:
                writes = list(writes) + sorted(set(extra))
            self.ops.append(Op(eng, _eager(fn), reads, writes, False))

    def dma(self, fn, reads=(), writes=(), eng="sp"):
        if not self.muted:
            self.ops.append(Op(eng, _eager(fn), reads, writes, True))

    def barrier(self):
        if not self.muted:
            self.ops.append(Op("all", None, (), (), False, barrier=True))

    def emit(self, nc):
        ops = self.ops
        last_w = {}
        readers = {}
        for i, op in enumerate(ops):
            op.idx = i
            if op.barrier:
                last_w = {}
                readers = {}
                continue
            deps = set()
            for k in op.reads:
                j = last_w.get(k)
                if j is not None:
                    deps.add((j, "raw"))
            for k in op.writes:
                j = last_w.get(k)
                if j is not None:
                    deps.add((j, "waw"))
                for j in readers.get(k, ()):
                    deps.add((j, "war"))
            for k in op.reads:
                lst = readers.setdefault(k, [])
                if not op.is_dma:
                    lst[:] = [j for j in lst if ops[j].is_dma or ops[j].eng != op.eng]
                lst.append(i)
            for k in op.writes:
                last_w[k] = i
                readers[k] = []
            best = set()
            for j, kind in deps:
                if j == i:
                    continue
                pj = ops[j]
                if pj.eng == op.eng and not pj.is_dma and not op.is_dma:
                    if op.eng == "pe":
                        continue
                    if kind != "raw":
                        continue
                best.add(j)
            op.deps = sorted(best)
            for j in op.deps:
                ops[j].sig = True
        cnt = {e: 0 for e in ENGS}
        dma_cnt = [0] * N_DMA_SEMS
        ndma = 0
        nsw = 0
        nhw = 0
        nbar = 0
        for op in ops:
            if op.barrier:
                op.dma_snapshot = list(dma_cnt)
                nbar += 1
                op.sig_idx = nbar
            elif op.is_dma:
                if op.eng == "pool":
                    op.dma_slot = N_DMA_SEMS // 2 + (nsw % (N_DMA_SEMS // 2))
                    nsw += 1
                else:
                    op.dma_slot = nhw % (N_DMA_SEMS // 2)
                    nhw += 1
                dma_cnt[op.dma_slot] += 1
                op.dma_cnt = dma_cnt[op.dma_slot]
                ndma += 1
            elif op.sig:
                cnt[op.eng] += 1
                op.sig_idx = cnt[op.eng]
        self.stats = dict(cnt, ndma=ndma, nops=len(ops), nbar=nbar)

        with contextlib.ExitStack() as es:
            esem = {e: es.enter_context(nc.semaphore("s_" + e)) for e in ("pe", "act", "dve", "pool")}
            dsem = [es.enter_context(nc.semaphore("s_dma%d" % i)) for i in range(N_DMA_SEMS)]
            bsem = es.enter_context(nc.semaphore("s_bar"))
            block = es.enter_context(nc.Block())

            def run(ename, eng):
                waited = {}

                def wait_dma(slot, v):
                    key = ("d", slot)
                    if v > 0 and waited.get(key, 0) < v:
                        eng.wait_ge(dsem[slot], v)
                        waited[key] = v

                for op in ops:
                    if op.barrier:
                        for s in range(N_DMA_SEMS):
                            wait_dma(s, 16 * op.dma_snapshot[s])
                        eng.drain().then_inc(bsem, 1)
                        eng.wait_ge(bsem, len(ENGS) * op.sig_idx)
                        continue
                    if op.eng != ename:
                        continue
                    if op.is_dma and op.dma_cnt > 1:
                        wait_dma(op.dma_slot, 16 * (op.dma_cnt - 1))
                    for j in op.deps:
                        pj = ops[j]
                        if pj.is_dma:
                            wait_dma(pj.dma_slot, 16 * pj.dma_cnt)
                        else:
                            key = ("e", pj.eng)
                            v = pj.sig_idx
                            if waited.get(key, 0) < v:
                                eng.wait_ge(esem[pj.eng], v)
                                waited[key] = v
                    ins = op.fn(eng)
                    if op.is_dma:
                        ins.then_inc(dsem[op.dma_slot], 16)
                    elif op.sig:
                        ins.then_inc(esem[op.eng], 1)
                if ename == "sp":
                    for s in range(N_DMA_SEMS):
                        wait_dma(s, 16 * dma_cnt[s])

            @block.tensor
            def _(e):
                run("pe", e)

            @block.scalar
            def _(e):
                run("act", e)

            @block.vector
            def _(e):
                run("dve", e)

            @block.gpsimd
            def _(e):
                run("pool", e)

            @block.sync
            def _(e):
                run("sp", e)


D = 1024
DI = 2048
NCOL = 10272
C_XBC = 2048
C_DT = 6144
C_U = 6176
C_PG = 7200
C_GS = 8224
C_GP = 9248
NPH = 1024
NTOKH = 16 + NPH + 64
NPRE = 6
POOL_W = (2, 4, 8, 16)


class _Stop(Exception):
    pass


def build_program(stop=None):
    nc = bass.Bass("TRN2", target_bir_lowering=False)

    def chk(name):
        if stop is not None and name == stop:
            P.muted = True

    di = lambda n, s: nc.dram_tensor(n, s, F32, kind="ExternalInput").ap()
    do = lambda n, s: nc.dram_tensor(n, s, F32, kind="ExternalOutput").ap()
    xin = di("xin", [2128, D])
    xpre = di("xpre", [NPRE * NPH + 16, D])
    pm_d = di("pm", [1, 8])
    w_in = di("w_in", [D, NCOL])
    w_bs = di("w_bs", [DI, D])
    w_bp = di("w_bp", [D, D])
    w_out = di("w_out", [D, D])
    pool_w = di("pool_w", [4, 256, 256])
    sst = di("sst", [128, 2048])
    scv = di("scv", [128, 32, 4])
    spl = di("spl", [128, 8, 15])
    ng_d = di("ng", [128, 8])
    cw_d = di("cw", [128, 32, 4])
    cb_d = di("cb", [128, 32])
    sng_d = di("sng", [128, 16])
    psc_d = di("psc", [128, 8])
    gb_d = di("gb", [128, 16])
    dtb_d = di("dtb", [1, 32])
    alog_d = di("alog", [1, 32])
    dsk_d = di("dsk", [1, 32])
    fg_d = di("fg", [1, D])
    pinv_d = di("pinv", [1, 64])
    yP = do("yP", [2048, D])
    yS = do("yS", [64, D])
    hPo = do("hPo", [128, 2048])
    hSo = do("hSo", [128, 2048])
    cvP = do("cvP", [128, 32, 3])
    cvS = do("cvS", [128, 32, 3])
    plP = do("plP", [128, 8, 15])
    plS = do("plS", [128, 8, 15])

    P = Prog()
    import os as _os2
    DBG = bool(_os2.environ.get("KDEBUG"))
    dbg_list = []

    def dbg_out(name, ap, shape, key):
        if not DBG:
            return
        t = nc.dram_tensor("dbg_" + name, list(shape), F32, kind="ExternalOutput").ap()
        dbg_list.append(name)
        P.dma(lambda e: e.dma_start(out=t, in_=ap), reads=[key], eng="pool")

    with contextlib.ExitStack() as es:
        def sb(name, shape, dt=F32, stack=es):
            return stack.enter_context(nc.sbuf_tensor(name, shape, dt))

        def ps(name, shape, dt=F32):
            return es.enter_context(nc.psum_tensor(name, shape, dt))

        pA = [ps("pA0", [128, 512]), ps("pA1", [128, 512])]
        pT = ps("pT", [128, 1024], BF16)
        pL = ps("pL", [128, 4, 128])
        pLb = ps("pLb", [128, 4, 128])
        pLs = [pL, pLb]
        pSm = pLb[:, :, :].rearrange("p a b -> p (a b)")[:, 0:256]
        pTb = pLb[:, :, :].rearrange("p a b -> p (a b)").bitcast(BF16)
        pLb_keys = ("pL0b", "pL1b", "pL2b", "pL3b", "pSm")
        pY = ps("pY", [128, 512])
        pZS = ps("pZS", [128, 512])
        pCBt = ps("pCBt", [128, 512])
        pa_ctr = [0]
        pa_ext = [False]
        pa_all = [(pA[0], ("pA0",)), (pA[1], ("pA1",)), (pCBt, ("pZ",)),
                  (pL[:, :, :].rearrange("p a b -> p (a b)"), ("pL0a", "pL1a", "pL2a", "pL3a")),
                  (pLb[:, :, :].rearrange("p a b -> p (a b)"), ("pL0b", "pL1b", "pL2b", "pL3b", "pSm")),
                  (pY, ("pY",))]

        def next_pA():
            n = 6 if pa_ext[0] else 2
            i = pa_ctr[0] % n
            pa_ctr[0] += 1
            return pa_all[i]

        tri = sb("tri", [128, 128])
        ones = sb("ones", [128, 128])
        identf = sb("identf", [128, 128])
        ident = sb("ident", [128, 128], BF16)
        trib = sb("trib", [128, 128], BF16)
        onesb = sb("onesb", [128, 128], BF16)
        a_hi = sb("a_hi", [128, 9 * 32], BF16)
        a_lo = sb("a_lo", [128, 9 * 32], BF16)
        ng = sb("ng_s", [128, 8])
        cw = sb("cw_s", [128, 32, 4])
        cb = sb("cb_s", [128, 32])
        sng = sb("sng_s", [128, 16])
        psc = sb("psc_s", [128, 8])
        gb = sb("gb_s", [128, 16])
        dtb = sb("dtb_s", [128, 32])
        Abc = sb("Abc", [128, 32])
        dsk = sb("dsk_s", [128, 32])
        fg = sb("fg_s", [128, D])
        pinv = sb("pinv_s", [128, 64])
        pm = sb("pm_s", [128, 8])
        scvb = sb("scvb", [128, 32, 4], BF16)
        splf = sb("splf", [128, 8, 15])
        Wdt = sb("Wdt", [128, 8, 32], BF16)
        cvPs = sb("cvPs", [128, 32, 3])
        cvSs = sb("cvSs", [128, 32, 3])
        plPs = sb("plPs", [128, 8, 15])
        plSs = sb("plSs", [128, 8, 15])
        hcar = sb("hcar", [128, 2048])
        NCH = 9
        dtn = ["dtx", "mx", "tA", "dt", "a", "nac", "e", "dtdec", "dec"]
        dta = {n: sb("dt_" + n, [128, NCH * 32]) for n in dtn}
        nT = sb("nT", [128, 8, NTOKH], BF16)
        MG = sb("MG", [128, 8, NPH + 64], BF16)

        P.add("pool", lambda e: e.memset(identf[:], 0.0), writes=["identf"])
        P.add("pool", lambda e: e.memset(tri[:], 1.0), writes=["tri"])
        P.add("pool", lambda e: e.memset(ones[:], 1.0), writes=["ones"])
        P.add("pool", lambda e: e.memset(hcar[:], 0.0), writes=["hcar"])
        P.add("pool", lambda e: e.affine_select(out=identf[:], in_=tri[:], pattern=[[-1, 128]], compare_op=ALU.is_equal,
                                                fill=0.0, base=0, channel_multiplier=1), reads=["tri"], writes=["identf"])
        P.add("pool", lambda e: e.affine_select(out=tri[:], in_=tri[:], pattern=[[1, 128]], compare_op=ALU.is_ge,
                                                fill=0.0, base=0, channel_multiplier=-1), reads=["tri", "identf"], writes=["tri"])
        P.add("dve", lambda e: e.tensor_copy(out=ident[:], in_=identf[:]), reads=["identf"], writes=["ident"])
        P.add("dve", lambda e: e.tensor_copy(out=trib[:], in_=tri[:]), reads=["tri"], writes=["trib"])
        P.add("dve", lambda e: e.tensor_copy(out=onesb[:], in_=ones[:]), reads=["ones"], writes=["onesb"])
        for dst, src, key in ((ng, ng_d, "ng"), (cw, cw_d, "cw"), (cb, cb_d, "cb"), (sng, sng_d, "sng"), (psc, psc_d, "psc"),
                              (gb, gb_d, "gb"), (splf, spl, "splf")):
            P.dma(lambda e, dst=dst, src=src: e.dma_start(out=dst[:], in_=src), writes=[key])
        for dst, src, key in ((dtb, dtb_d, "dtb"), (Abc, alog_d, "Abc"), (dsk, dsk_d, "dsk"), (fg, fg_d, "fg"), (pinv, pinv_d, "pinv"), (pm, pm_d, "pm")):
            P.dma(lambda e, dst=dst, src=src: e.dma_start(out=dst[:], in_=src[0:1, :].partition_broadcast(128)), writes=[key])
        P.dma(lambda e: e.dma_start(out=scvb[:], in_=scv), writes=["scvb"], eng="pool")
        P.dma(lambda e: e.dma_start(out=Wdt[:], in_=w_in[:, C_DT:C_DT + 32].rearrange("(k p) c -> p k c", p=128)), writes=["Wdt"], eng="pool")
        P.add("act", lambda e: e.activation(out=Abc[:], in_=Abc[:], func=AF.Exp), reads=["Abc"], writes=["Abc"])
        P.add("dve", lambda e: e.tensor_scalar(out=Abc[:], in0=Abc[:], scalar1=-1.0, scalar2=None, op0=ALU.mult), reads=["Abc"], writes=["Abc"])

        def wload(dst, src_rows_cols, key):
            P.dma(lambda e: e.dma_start(out=dst, in_=src_rows_cols.rearrange("(k p) c -> p k c", p=128)), writes=[key], eng="pool")

        class Prefetch:
            def __init__(self):
                self.th = []
                self.issued = 0

            def add(self, dst, src, key):
                self.th.append((dst, src, key))
                return len(self.th) - 1

            def need(self, i):
                while self.issued <= min(i + 2, len(self.th) - 1):
                    wload(*self.th[self.issued])
                    self.issued += 1

        def mm_group(out_ap, okey, pairs, rkeys):
            n = len(pairs)
            for i, (l, r) in enumerate(pairs):
                P.add("pe", lambda e, l=l, r=r, i=i: e.matmul(out_ap, lhsT=l, rhs=r, start=(i == 0), stop=(i == n - 1)),
                      reads=rkeys, writes=[okey])

        def rstd_from_ssq(ssq_ap, key, inv_n):
            P.add("dve", lambda e: e.tensor_scalar(out=ssq_ap, in0=ssq_ap, scalar1=inv_n, scalar2=EPS, op0=ALU.mult, op1=ALU.add),
                  reads=[key], writes=[key])
            P.add("act", lambda e: e.activation(out=ssq_ap, in_=ssq_ap, func=AF.Sqrt), reads=[key], writes=[key])
            P.add("dve", lambda e: e.reciprocal(out=ssq_ap, in_=ssq_ap), reads=[key], writes=[key])

        ys = contextlib.ExitStack()
        try:
          for blk in range(NPRE + 2):
            pre = blk < NPRE
            hf = -1 if pre else blk - NPRE
            xsrc, xrow0 = (xpre, blk * NPH) if pre else (xin, hf * NPH)
            has_s = (hf == 1)
            ncols = NTOKH if has_s else 16 + NPH
            ntok = NPH + 64 if has_s else NPH
            chunks = [(c, 16 + c * 128, 128, c * 128) for c in range(8)]
            if has_s:
                chunks.append((8, 16 + NPH, 64, NPH))
            fblocks = [(16, 512, 0), (528, 512, 512)]
            if has_s:
                fblocks.append((16 + NPH, 64, NPH))
            H = "pre_" if pre else "h%d_" % hf
            pa_ext[0] = pre

            if not pre:
                ys = contextlib.ExitStack()
                ynT = ys.enter_context(nc.sbuf_tensor("ynT%d" % hf, [128, 16, NPH + 64], BF16))
            if pre and blk > 0:
                new_alloc = False
            else:
                new_alloc = True
                sa = contextlib.ExitStack()
            if new_alloc:
                sba = lambda n, s, dt=F32, sa=sa, blk=blk: sb(n + "_b%d" % blk, s, dt, stack=sa)
                xt = [sba("xt%d" % i, [128, D]) for i in range(2)]
                xb = [sba("xb%d" % i, [128, D], BF16) for i in range(2)]
                ssq0 = [sba("ssq0_%d" % i, [128, 1]) for i in range(2)]
                Wx = [sba("Wx%d" % i, [128, 8, 512], BF16) for i in range(2)]
                Wz = [sba("Wz%d" % i, [128, 8, 256], BF16) for i in range(2)]
                rawP = sba("rawP", [128, 4, 4 + NPH], BF16)
                rawS = sba("rawS", [128, 4, 4 + 64], BF16)
                xsT = sba("xsT", [128, 2, NPH + 64], BF16)
                BT = sba("BT", [128, NPH + 64], BF16)
                CT = sba("CT", [128, NPH + 64], BF16)
                xBt = sba("xBt", [128, NCH, 384], BF16)
                hTa = sba("hTa", [128, NCH, 256], BF16)
                yza = sba("yza", [128, NCH, 256], BF16)
                ssqa = sba("ssqa", [128, NCH])
                dg = sba("dg", [128, 4, 4, 128], BF16)
                szs = sba("szs", [128, 256])
                CBm = [sba("CBm%d" % p_, [128, 128]) for p_ in range(2)]
                Eh = [[sba("E%d_%d" % (p_, i), [128, 128]) for i in range(4)] for p_ in range(2)]
                MT = [[sba("MT%d_%d" % (p_, i), [128, 128], BF16) for i in range(4)] for p_ in range(2)]
                t1 = [sba("t1_%d" % p_, [128, 256]) for p_ in range(2)]
                t3 = sba("t3", [128, 256])
                xdt = [sba("xdt%d" % p_, [128, 256], BF16) for p_ in range(2)]
                xd = [sba("xd%d" % p_, [128, 256], BF16) for p_ in range(2)]
                xdd = [sba("xdd%d" % p_, [128, 256], BF16) for p_ in range(2)]
                tmpH = sba("tmpH", [128, 256])
                hS = sba("hS", [128, 256])
                yn = [sba("yn%d" % p_, [128, 256], BF16) for p_ in range(2)]
                junk = sba("junkA", [128, 256], BF16)

            if True:
                def issue_w(g_):
                    wb_ = g_ % 2
                    wload(Wx[wb_][:, :, 0:256], w_in[:, C_XBC + g_ * 256: C_XBC + (g_ + 1) * 256], H + "Wx%da" % wb_)
                    wload(Wx[wb_][:, :, 256:384], w_in[:, C_XBC + 2048 + g_ * 128: C_XBC + 2048 + (g_ + 1) * 128], H + "Wx%db" % wb_)
                    if not pre:
                        wload(Wx[wb_][:, :, 384:512], w_in[:, C_XBC + 3072 + g_ * 128: C_XBC + 3072 + (g_ + 1) * 128], H + "Wx%dc" % wb_)
                        wload(Wz[wb_][:, :, :], w_in[:, g_ * 256:(g_ + 1) * 256], H + "Wz%d" % wb_)

                issue_w(0)
                for ti, j0 in enumerate(range(0, ncols, 128)):
                    R = min(128, ncols - j0)
                    b = ti % 2
                    xk, bk, sk = H + "xt%d" % b, H + "xb%d" % b, H + "ssq0%d" % b
                    if j0 + R <= 16 + NPH:
                        P.dma(lambda e, b=b, j0=j0, R=R: e.dma_start(out=xt[b][:R, :], in_=xsrc[xrow0 + j0: xrow0 + j0 + R, :]), writes=[xk])
                    else:
                        rp = 16 + NPH - j0
                        P.dma(lambda e, b=b, j0=j0, rp=rp: e.dma_start(out=xt[b][:rp, :], in_=xsrc[xrow0 + j0: xrow0 + j0 + rp, :]), writes=[xk])
                        P.dma(lambda e, b=b, rp=rp: e.dma_start(out=xt[b][rp:rp + 64, :], in_=xin[2064:2128, :]), writes=[xk])
                    P.add("act", lambda e, b=b, R=R: e.activation(out=xb[b][:R, :], in_=xt[b][:R, :], func=AF.Square, accum_out=ssq0[b][:R, :]),
                          reads=[xk], writes=[bk, sk])
                    rstd_from_ssq(ssq0[b][:R, :], sk, 1.0 / D)
                    P.add("dve", lambda e, b=b, R=R: e.tensor_scalar(out=xb[b][:R, :], in0=xt[b][:R, :], scalar1=ssq0[b][:R, 0:1], scalar2=None, op0=ALU.mult),
                          reads=[xk, sk], writes=[bk])
                    for k in range(8):
                        P.add("pe", lambda e, b=b, R=R, k=k: e.transpose(out=pT[:, k * 128:k * 128 + R], in_=xb[b][:R, k * 128:(k + 1) * 128], identity=ident[:R, :R]),
                              reads=[bk, "ident"], writes=["pT"])
                    P.add("dve", lambda e, R=R, j0=j0: e.tensor_tensor(out=nT[:, :, j0:j0 + R], in0=pT[:, :].rearrange("p (k t) -> p k t", k=8)[:, :, :R],
                                                                       in1=ng[:, :].unsqueeze(2).to_broadcast([128, 8, R]), op=ALU.mult),
                          reads=["pT", "ng"], writes=[H + "nT"])
                nTk = H + "nT"
                chk("S0_%d" % hf)

                import os as _os
                if "S0b" in _os.environ.get("KSKIP", ""):
                    P.muted = True
                dk = lambda n: H + "dt_" + n
                A3 = lambda n: dta[n][:, :].rearrange("p (c h) -> p c h", h=32)
                P.add("pool", lambda e: e.memset(dta["dtx"][:], 0.0), writes=[dk("dtx")])
                pD, pDk = next_pA()
                for (c, col0, T, tok0) in chunks:
                    mm_group(pD[:T, c * 32:(c + 1) * 32], pDk, [(nT[:, k, col0:col0 + T], Wdt[:, k, :]) for k in range(8)], [nTk, "Wdt"])
                    P.add("dve", lambda e, c=c, T=T: e.tensor_tensor(out=dta["dtx"][:T, c * 32:(c + 1) * 32], in0=pD[:T, c * 32:(c + 1) * 32], in1=dtb[:T, :], op=ALU.add),
                          reads=[pDk, "dtb"], writes=[dk("dtx")])
                chk("S0b1_%d" % hf)
                P.add("dve", lambda e: e.tensor_scalar(out=dta["mx"][:], in0=dta["dtx"][:], scalar1=0.0, scalar2=None, op0=ALU.max), reads=[dk("dtx")], writes=[dk("mx")])
                P.add("dve", lambda e: e.tensor_scalar(out=dta["tA"][:], in0=dta["dtx"][:], scalar1=0.0, scalar2=None, op0=ALU.min), reads=[dk("dtx")], writes=[dk("tA")])
                P.add("dve", lambda e: e.tensor_tensor(out=dta["tA"][:], in0=dta["tA"][:], in1=dta["mx"][:], op=ALU.subtract), reads=[dk("tA"), dk("mx")], writes=[dk("tA")])
                P.add("act", lambda e: e.activation(out=dta["tA"][:], in_=dta["tA"][:], func=AF.Exp), reads=[dk("tA")], writes=[dk("tA")])
                P.add("act", lambda e: e.activation(out=dta["tA"][:], in_=dta["tA"][:], func=AF.Ln, bias=1.0, scale=1.0), reads=[dk("tA")], writes=[dk("tA")])
                P.add("dve", lambda e: e.tensor_tensor(out=dta["dt"][:], in0=dta["mx"][:], in1=dta["tA"][:], op=ALU.add), reads=[dk("tA"), dk("mx")], writes=[dk("dt")])
                if pre:
                    P.add("dve", lambda e: e.tensor_scalar(out=dta["dt"][:], in0=dta["dt"][:], scalar1=pm[:, blk:blk + 1], scalar2=None, op0=ALU.mult),
                          reads=[dk("dt"), "pm"], writes=[dk("dt")])
                chk("S0b2_%d" % hf)
                P.add("dve", lambda e: e.tensor_tensor(out=A3("a"), in0=A3("dt"), in1=Abc[:, :].unsqueeze(1).to_broadcast([128, NCH, 32]), op=ALU.mult),
                      reads=[dk("dt"), "Abc"], writes=[dk("a")])
                chk("S0b3_%d" % hf)
                pC, pCk = next_pA()
                pE, pEk = next_pA()
                P.add("dve", lambda e: e.tensor_copy(out=a_hi[:], in_=dta["a"][:]), reads=[dk("a")], writes=[H + "a_hi"])
                P.add("dve", lambda e: e.tensor_tensor(out=a_lo[:], in0=dta["a"][:], in1=a_hi[:], op=ALU.subtract), reads=[dk("a"), H + "a_hi"], writes=[H + "a_lo"])
                ak = [H + "a_hi", H + "a_lo"]
                for pi, asrc in enumerate((a_hi, a_lo)):
                    P.add("pe", lambda e, asrc=asrc, pi=pi: e.matmul(pC[:, 0:256], lhsT=trib[:, :], rhs=asrc[:, 0:256], start=(pi == 0), stop=(pi == 1)), reads=["trib"] + ak, writes=[pCk])
                for pi, asrc in enumerate((a_hi, a_lo)):
                    P.add("pe", lambda e, asrc=asrc, pi=pi: e.matmul(pE[:, 0:256], lhsT=onesb[:, :], rhs=asrc[:, 0:256], start=(pi == 0), stop=(pi == 1)), reads=["onesb"] + ak, writes=[pEk])
                ncc = 256
                if has_s:
                    for pi, asrc in enumerate((a_hi, a_lo)):
                        P.add("pe", lambda e, asrc=asrc, pi=pi: e.matmul(pC[:64, 256:288], lhsT=trib[:64, :64], rhs=asrc[:64, 256:288], start=(pi == 0), stop=(pi == 1)), reads=["trib"] + ak, writes=[pCk])
                    for pi, asrc in enumerate((a_hi, a_lo)):
                        P.add("pe", lambda e, asrc=asrc, pi=pi: e.matmul(pE[:, 256:288], lhsT=onesb[:64, :], rhs=asrc[:64, 256:288], start=(pi == 0), stop=(pi == 1)), reads=["onesb"] + ak, writes=[pEk])
                    ncc = 288
                chk("S0b4_%d" % hf)
                P.add("dve", lambda e, ncc=ncc: e.tensor_scalar(out=dta["nac"][:, :ncc], in0=pC[:, :ncc], scalar1=-1.0, scalar2=None, op0=ALU.mult), reads=[pCk], writes=[dk("nac")])
                chk("S0b5_%d" % hf)
                P.add("act", lambda e, ncc=ncc: e.activation(out=dta["e"][:, :ncc], in_=dta["nac"][:, :ncc], func=AF.Exp, scale=-1.0), reads=[dk("nac")], writes=[dk("e")])
                chk("S0b6_%d" % hf)
                P.add("dve", lambda e, ncc=ncc: e.tensor_tensor(out=dta["dtdec"][:, :ncc], in0=pE[:, :ncc], in1=dta["nac"][:, :ncc], op=ALU.add), reads=[pEk, dk("nac")], writes=[dk("dtdec")])
                chk("S0b7_%d" % hf)
                P.add("act", lambda e, ncc=ncc: e.activation(out=dta["dtdec"][:, :ncc], in_=dta["dtdec"][:, :ncc], func=AF.Exp), reads=[dk("dtdec")], writes=[dk("dtdec")])
                P.add("dve", lambda e, ncc=ncc: e.tensor_tensor(out=dta["dtdec"][:, :ncc], in0=dta["dtdec"][:, :ncc], in1=dta["dt"][:, :ncc], op=ALU.mult), reads=[dk("dtdec"), dk("dt")], writes=[dk("dtdec")])
                chk("S0b8_%d" % hf)
                P.add("act", lambda e, ncc=ncc: e.activation(out=dta["dec"][:, :ncc], in_=pE[:, :ncc], func=AF.Exp), reads=[pEk], writes=[dk("dec")])

                if pre:
                    A3t = lambda n: dta[n][:, 0:256].rearrange("p (c h) -> p c h", h=32)
                    P.add("dve", lambda e: e.tensor_copy(out=dta["tA"][:, 0:256], in_=pE[:, 0:256]), reads=[pEk], writes=[dk("tA")])
                    P.add("pool", lambda e: e.memset(dta["mx"][:, 224:256], 0.0), reads=[dk("mx")], writes=[dk("mx")])
                    for c_ in range(6, -1, -1):
                        P.add("dve", lambda e: e.tensor_tensor(out=dta["mx"][:, c_ * 32:(c_ + 1) * 32], in0=dta["mx"][:, (c_ + 1) * 32:(c_ + 2) * 32],
                                                               in1=dta["tA"][:, (c_ + 1) * 32:(c_ + 2) * 32], op=ALU.add),
                              reads=[dk("mx"), dk("tA")], writes=[dk("mx")])
                    P.add("dve", lambda e: e.tensor_tensor(out=dta["e"][:, 0:32], in0=dta["mx"][:, 0:32], in1=dta["tA"][:, 0:32], op=ALU.add),
                          reads=[dk("mx"), dk("tA")], writes=[dk("e")])
                    P.add("act", lambda e: e.activation(out=dta["e"][:, 0:32], in_=dta["e"][:, 0:32], func=AF.Exp), reads=[dk("e")], writes=[dk("e")])
                    P.add("act", lambda e: e.activation(out=dta["mx"][:, 0:256], in_=dta["mx"][:, 0:256], func=AF.Exp), reads=[dk("mx")], writes=[dk("mx")])
                    P.add("dve", lambda e: e.tensor_tensor(out=dta["dtdec"][:, 0:256], in0=dta["dtdec"][:, 0:256], in1=dta["mx"][:, 0:256], op=ALU.mult),
                          reads=[dk("dtdec"), dk("mx")], writes=[dk("dtdec")])
                if "S0b" in _os.environ.get("KSKIP", ""):
                    P.muted = False
                if hf == 1:
                    for nm in ("dt", "a", "nac", "e", "dtdec", "dec"):
                        dbg_out(nm, dta[nm][:, :], [128, NCH * 32], dk(nm))
                chk("S0b_%d" % hf)
                ntile = 3 if pre else 4
                for g in range(8):
                    wb = g % 2
                    Wxk, Wzk = H + "Wx%d" % wb, H + "Wz%d" % wb
                    tiles = [2 * g, 2 * g + 1, 16 + g, 24 + g]
                    hd = slice(g * 4, g * 4 + 4)
                    for j in range(ntile):
                        for s_ in range(4):
                            P.add("dve", lambda e, j=j, s_=s_: e.tensor_scalar(out=dg[:, j, s_, :], in0=ident[:, :], scalar1=cw[:, tiles[j], s_:s_ + 1], scalar2=None, op0=ALU.mult),
                                  reads=["ident", "cw"], writes=[H + "dg%d" % j])

                    def proj(j):
                        tl = tiles[j]
                        wsl = lambda k: Wx[wb][:, k, j * 128:(j + 1) * 128]
                        rk = H + "rawP%d" % j
                        Wxk_j = Wxk + "aabc"[j]
                        rsk = H + "rawS%d" % j
                        pp, ppk = next_pA()
                        mm_group(pp[:, 0:16], ppk, [(wsl(k), nT[:, k, 0:16]) for k in range(8)], [nTk, Wxk_j])
                        P.add("act", lambda e: e.activation(out=rawP[:, j, 0:4], in_=pp[:, 12:16], func=AF.Copy), reads=[ppk], writes=[rk + "h"])
                        for bi in range(2):
                            pp, ppk = next_pA()
                            c0 = 16 + bi * 512
                            mm_group(pp[:, :], ppk, [(wsl(k), nT[:, k, c0:c0 + 512]) for k in range(8)], [nTk, Wxk_j])
                            P.add("act", lambda e: e.activation(out=rawP[:, j, 4 + bi * 512: 4 + (bi + 1) * 512], in_=pp[:, :], func=AF.Copy), reads=[ppk], writes=[rk + "b%d" % bi])
                            if has_s and bi == 1:
                                P.add("dve", lambda e: e.tensor_copy(out=cvPs[:, tl, :], in_=pp[:, 509:512]), reads=[ppk], writes=["cvPs"])
                        if has_s:
                            pp, ppk = next_pA()
                            c0 = 16 + NPH
                            mm_group(pp[:, 0:64], ppk, [(wsl(k), nT[:, k, c0:c0 + 64]) for k in range(8)], [nTk, Wxk_j])
                            P.add("act", lambda e: e.activation(out=rawS[:, j, 4:68], in_=pp[:, 0:64], func=AF.Copy), reads=[ppk], writes=[rsk])
                            P.add("dve", lambda e: e.tensor_copy(out=cvSs[:, tl, :], in_=pp[:, 61:64]), reads=[ppk], writes=["cvSs"])
                            P.add("pool", lambda e: e.tensor_copy(out=rawS[:, j, 0:4], in_=scvb[:, tl, :]), reads=["scvb"], writes=[rsk])

                    def conv(j):
                        tl = tiles[j]
                        rk = H + "rawP%d" % j
                        rsk = H + "rawS%d" % j
                        if j < 2:
                            dst = lambda a, b_: xsT[:, j, a:b_]
                            dkey = H + "xsT%d" % j
                        elif j == 2:
                            dst = lambda a, b_: BT[:, a:b_]
                            dkey = H + "BT"
                        else:
                            dst = lambda a, b_: CT[:, a:b_]
                            dkey = H + "CT"
                        for bi in range(2):
                            pp, ppk = next_pA()
                            rks = [rk + "h", rk + "b0"] if bi == 0 else [rk + "b0", rk + "b1"]
                            mm_group(pp[:, :], ppk, [(dg[:, j, s_, :], rawP[:, j, bi * 512 + s_ + 1: bi * 512 + s_ + 513]) for s_ in range(4)], [H + "dg%d" % j] + rks)
                            P.add("act", lambda e: e.activation(out=dst(bi * 512, (bi + 1) * 512), in_=pp[:, :], func=AF.Silu, bias=cb[:, tl:tl + 1], scale=1.0),
                                  reads=[ppk, "cb"], writes=[dkey + "_%d" % bi])
                        if has_s:
                            pp, ppk = next_pA()
                            mm_group(pp[:, 0:64], ppk, [(dg[:, j, s_, :], rawS[:, j, s_ + 1: s_ + 65]) for s_ in range(4)], [H + "dg%d" % j, rsk])
                            P.add("act", lambda e: e.activation(out=dst(NPH, NPH + 64), in_=pp[:, 0:64], func=AF.Silu, bias=cb[:, tl:tl + 1], scale=1.0),
                                  reads=[ppk, "cb"], writes=[dkey + "_2"])

                    proj(0)
                    for j in range(1, ntile):
                        proj(j)
                        conv(j - 1)
                    conv(ntile - 1)
                    if g + 1 < 8:
                        issue_w(g + 1)
                    chk("S1a_%d_%d" % (hf, g))

                    def ckeys(base, tok0):
                        return [base + "_%d" % (2 if tok0 >= NPH else tok0 // 512)]
                    for (c, col0, T, tok0) in chunks:
                        ptb = pTb if c % 2 else pT
                        pk = list(pLb_keys) if c % 2 else ["pT"]
                        for jj, (src, skey) in enumerate(((xsT[:, 0, tok0:tok0 + T], H + "xsT0"), (xsT[:, 1, tok0:tok0 + T], H + "xsT1"), (BT[:, tok0:tok0 + T], H + "BT"))):
                            P.add("pe", lambda e, src=src, jj=jj, T=T: e.transpose(out=ptb[:T, jj * 128:(jj + 1) * 128], in_=src, identity=ident[:, :]),
                                  reads=ckeys(skey, tok0) + ["ident"], writes=pk)
                        P.add("dve", lambda e, c=c, T=T: e.tensor_copy(out=xBt[:T, c, :], in_=ptb[:T, 0:384]), reads=pk, writes=[H + "xBt%d" % c])
                    hg = hcar[:, g * 256:(g + 1) * 256]
                    hgk = "hcar%d" % g
                    if pre:
                        for (c, col0, T, tok0) in chunks:
                            xk_ = H + "xBt%d" % c
                            xddb = xdd[c % 2]
                            xddk = H + "xdd%d" % (c % 2)
                            P.add("dve", lambda e: e.tensor_tensor(out=xddb[:T, :].rearrange("t (h p) -> t h p", h=4), in0=xBt[:T, c, 0:256].rearrange("t (h p) -> t h p", h=4),
                                                                   in1=A3("dtdec")[:T, c, hd].unsqueeze(2).to_broadcast([T, 4, 64]), op=ALU.mult),
                                  reads=[xk_, dk("dtdec")], writes=[xddk])
                            P.add("pe", lambda e: e.matmul(pZS[:, 256:512], lhsT=xBt[:T, c, 256:384], rhs=xddb[:T, :], start=(c == 0), stop=(c == 7)),
                                  reads=[xk_, xddk], writes=["pS"])
                        P.add("pool", lambda e: e.tensor_tensor(out=tmpH[:, :].rearrange("n (h p) -> n h p", h=4), in0=hg.rearrange("n (h p) -> n h p", h=4),
                                                                in1=dta["e"][:, hd].unsqueeze(2).to_broadcast([128, 4, 64]), op=ALU.mult),
                              reads=[hgk, dk("e")], writes=[H + "tmpH"])
                        P.add("dve", lambda e: e.tensor_tensor(out=hg, in0=pZS[:, 256:512], in1=tmpH[:, :], op=ALU.add), reads=["pS", H + "tmpH"], writes=[hgk])
                        continue
                    def state_ops(c, col0, T, tok0):
                        samp = (c == 8)
                        if samp:
                            P.dma(lambda e: e.dma_start(out=hS[:, :], in_=sst[:, g * 256:(g + 1) * 256]), writes=[H + "hS"])
                            hcur, hk = hS[:, :], H + "hS"
                        else:
                            hcur, hk = hg, hgk
                        xk_ = H + "xBt%d" % c
                        if not pre:
                            P.add("pool", lambda e, c=c, hcur=hcur: e.tensor_copy(out=hTa[:, c, :], in_=hcur), reads=[hk], writes=[H + "hTa%d" % c])
                        xddb = xdd[c % 2]
                        xddk = H + "xdd%d" % (c % 2)
                        P.add("dve", lambda e, c=c, T=T: e.tensor_tensor(out=xddb[:T, :].rearrange("t (h p) -> t h p", h=4), in0=xBt[:T, c, 0:256].rearrange("t (h p) -> t h p", h=4),
                                                                         in1=A3("dtdec")[:T, c, hd].unsqueeze(2).to_broadcast([T, 4, 64]), op=ALU.mult),
                              reads=[xk_, dk("dtdec")], writes=[xddk])
                        P.add("pe", lambda e, c=c, T=T: e.matmul(pSm, lhsT=xBt[:T, c, 256:384], rhs=xddb[:T, :], start=True, stop=True),
                              reads=[xk_, xddk], writes=["pSm"])
                        P.add("pool", lambda e, c=c, hcur=hcur: e.tensor_tensor(out=tmpH[:, :].rearrange("n (h p) -> n h p", h=4), in0=hcur.rearrange("n (h p) -> n h p", h=4),
                                                                                in1=A3("dec")[:, c, hd].unsqueeze(2).to_broadcast([128, 4, 64]), op=ALU.mult),
                              reads=[hk, dk("dec")], writes=[H + "tmpH"])
                        P.add("dve", lambda e, hcur=hcur: e.tensor_tensor(out=hcur, in0=pSm, in1=tmpH[:, :], op=ALU.add), reads=["pSm", H + "tmpH"], writes=[hk])
                        if samp:
                            P.dma(lambda e: e.dma_start(out=hSo[:, g * 256:(g + 1) * 256], in_=hS[:, :]), reads=[hk])
                        elif has_s and c == 7:
                            P.dma(lambda e: e.dma_start(out=hPo[:, g * 256:(g + 1) * 256], in_=hg), reads=[hk])
                    chk("S1b_%d_%d" % (hf, g))
                    if pre:
                        continue
                    pend_sq = [None]

                    def emit_sq(c, T):
                        P.add("act", lambda e: e.activation(out=junk[:T, :], in_=yza[:T, c, :], func=AF.Square, scale=0.5, accum_out=ssqa[:T, c:c + 1]),
                              reads=[H + "yza%d" % c], writes=[H + "junk", H + "ssqa"])

                    for (c, col0, T, tok0) in chunks:
                        state_ops(c, col0, T, tok0)
                        xk_ = H + "xBt%d" % c
                        mm_group(pCBt[:T, 0:256], "pZ", [(nT[:, k, col0:col0 + T], Wz[wb][:, k, :]) for k in range(8)], [nTk, Wzk])
                        P.add("act", lambda e, T=T: e.activation(out=szs[:T, :], in_=pCBt[:T, 0:256], func=AF.Tanh, scale=0.5), reads=["pZ"], writes=[H + "sz"])
                        P.add("dve", lambda e, T=T: e.scalar_tensor_tensor(out=szs[:T, :], in0=szs[:T, :], scalar=1.0, in1=pCBt[:T, 0:256], op0=ALU.add, op1=ALU.mult),
                              reads=["pZ", H + "sz"], writes=[H + "sz"])
                        P.add("pe", lambda e, T=T, tok0=tok0: e.matmul(pZS[:T, 256:256 + T], lhsT=BT[:, tok0:tok0 + T], rhs=CT[:, tok0:tok0 + T], start=True, stop=True),
                              reads=ckeys(H + "BT", tok0) + ckeys(H + "CT", tok0), writes=["pCB"])
                        P.add("pe", lambda e, c=c, T=T, tok0=tok0: e.matmul(pZS[:T, 0:256], lhsT=CT[:, tok0:tok0 + T], rhs=hTa[:, c, :], start=True, stop=True),
                              reads=ckeys(H + "CT", tok0) + [H + "hTa%d" % c], writes=["pZo"])
                        for h in range(4):
                            for pi, asrc in enumerate((a_hi, a_lo)):
                                P.add("pe", lambda e, c=c, T=T, h=h, asrc=asrc, pi=pi: e.matmul(pL[:T, h, :T], lhsT=asrc[:T, c * 32 + g * 4 + h: c * 32 + g * 4 + h + 1].to_broadcast([T, T]),
                                                                                        rhs=trib[:T, :T], start=(pi == 0), stop=(pi == 1)),
                                      reads=[H + "a_hi", H + "a_lo", "trib"], writes=["pL%da" % h])
                        P.add("dve", lambda e, T=T: e.tensor_tensor(out=CBm[0][:T, :T], in0=pZS[:T, 256:256 + T], in1=tri[:T, :T], op=ALU.mult), reads=["pCB", "tri"], writes=[H + "CBm"])
                        P.add("pool", lambda e, c=c, T=T: e.tensor_tensor(out=xdt[0][:T, :].rearrange("t (h p) -> t h p", h=4), in0=xBt[:T, c, 0:256].rearrange("t (h p) -> t h p", h=4),
                                                                          in1=A3("dt")[:T, c, hd].unsqueeze(2).to_broadcast([T, 4, 64]), op=ALU.mult),
                              reads=[xk_, dk("dt")], writes=[H + "xdt"])
                        P.add("pool", lambda e, c=c, T=T: e.tensor_tensor(out=xd[0][:T, :].rearrange("t (h p) -> t h p", h=4), in0=xBt[:T, c, 0:256].rearrange("t (h p) -> t h p", h=4),
                                                                          in1=dsk[:T, hd].unsqueeze(2).to_broadcast([T, 4, 64]), op=ALU.mult),
                              reads=[xk_, "dsk"], writes=[H + "xd"])
                        for h in range(4):
                            P.add("act", lambda e, c=c, T=T, h=h: e.activation(out=Eh[0][h][:T, :T], in_=pL[:T, h, :T], func=AF.Exp,
                                                                               bias=dta["nac"][:T, c * 32 + g * 4 + h: c * 32 + g * 4 + h + 1], scale=1.0),
                                  reads=["pL%da" % h, dk("nac")], writes=[H + "E%d" % h])
                            P.add("dve", lambda e, T=T, h=h: e.scalar_tensor_tensor(out=MT[0][h][:T, :T], in0=Eh[0][h][:T, :T], scalar=1.0, in1=CBm[0][:T, :T], op0=ALU.min, op1=ALU.mult),
                                  reads=[H + "E%d" % h, H + "CBm"], writes=[H + "MT%d" % h])
                        P.add("pe", lambda e, T=T: e.matmul(pY[:T, 0:256], lhsT=ident[:T, :T], rhs=xd[0][:T, :], start=True, stop=False), reads=["ident", H + "xd"], writes=["pY"])
                        for h in range(4):
                            P.add("pe", lambda e, T=T, h=h: e.matmul(pY[:T, h * 64:(h + 1) * 64], lhsT=MT[0][h][:T, :T], rhs=xdt[0][:T, h * 64:(h + 1) * 64], start=False, stop=(h == 3)),
                                  reads=[H + "MT%d" % h, H + "xdt"], writes=["pY"])
                        P.add("dve", lambda e, c=c, T=T: e.tensor_tensor(out=t1[0][:T, :].rearrange("t (h p) -> t h p", h=4), in0=pZS[:T, 0:256].rearrange("t (h p) -> t h p", h=4),
                                                                         in1=A3("e")[:T, c, hd].unsqueeze(2).to_broadcast([T, 4, 64]), op=ALU.mult),
                              reads=["pZo", dk("e")], writes=[H + "t1"])
                        P.add("dve", lambda e, T=T: e.tensor_tensor(out=t3[:T, :], in0=pY[:T, 0:256], in1=t1[0][:T, :], op=ALU.add), reads=["pY", H + "t1"], writes=[H + "t3"])
                        P.add("pool", lambda e, c=c, T=T: e.tensor_tensor(out=yza[:T, c, :], in0=t3[:T, :], in1=szs[:T, :], op=ALU.mult), reads=[H + "t3", H + "sz"], writes=[H + "yza%d" % c])
                        if pend_sq[0] is not None:
                            emit_sq(*pend_sq[0])
                        pend_sq[0] = (c, T)
                    emit_sq(*pend_sq[0])
                    if hf == 1 and g == 0:
                        dbg_out("yza8", yza[:64, 8, :], [64, 256], H + "yza8")
                        dbg_out("yza0", yza[:, 0, :], [128, 256], H + "yza0")
                        dbg_out("ssqa", ssqa[:, :], [128, NCH], H + "ssqa")
                    nchk = len(chunks)
                    if has_s:
                        P.add("pool", lambda e: e.memset(ssqa[64:128, 8:9], 1.0), reads=[H + "ssqa"], writes=[H + "ssqa"])
                    rstd_from_ssq(ssqa[:, 0:nchk], H + "ssqa", 1.0 / 256)
                    for (c, col0, T, tok0) in chunks:
                        ynb = yn[c % 2]
                        ynk = H + "yn%d" % (c % 2)
                        P.add("dve", lambda e, c=c, T=T: e.tensor_scalar(out=ynb[:T, :], in0=yza[:T, c, :], scalar1=ssqa[:T, c:c + 1], scalar2=0.5, op0=ALU.mult, op1=ALU.mult),
                              reads=[H + "yza%d" % c, H + "ssqa"], writes=[ynk])
                        for jj in range(2):
                            P.add("pe", lambda e, T=T, jj=jj: e.transpose(out=pT[:, 512 + jj * 128: 512 + jj * 128 + T], in_=ynb[:T, jj * 128:(jj + 1) * 128], identity=ident[:T, :T]),
                                  reads=[ynk, "ident"], writes=["pT2"])
                        for jj in range(2):
                            P.add("act", lambda e, T=T, jj=jj, tok0=tok0: e.activation(out=ynT[:, 2 * g + jj, tok0:tok0 + T], in_=pT[:, 512 + jj * 128: 512 + jj * 128 + T], func=AF.Copy,
                                                                                      scale=sng[:, 2 * g + jj: 2 * g + jj + 1]),
                                  reads=["pT2", "sng"], writes=[H + "ynT"])
            if pre:
                if blk == NPRE - 1:
                    sa.close()
                    P.barrier()
                continue
            sa.close()
            chk("S1_%d" % hf)
            P.barrier()

            pa_ext[0] = True
            with contextlib.ExitStack() as sB:
                sbb = lambda n, s, dt=F32: sb(n + "_%d" % hf, s, dt, stack=sB)
                Wbs = [sbb("Wbs%d" % i, [128, 16, 128], BF16) for i in range(2)]
                Wg1 = [sbb("Wg1%d" % i, [128, 8, 128], BF16) for i in range(2)]
                gsb = [sbb("gsb%d" % i, [128, 512]) for i in range(2)]
                ctr = 0
                pf = Prefetch()
                pfi = {}
                for m in range(8):
                    wb = m % 2
                    pfi[("g1", m)] = pf.add(Wg1[wb][:, :, :], w_in[:, C_GS + m * 128: C_GS + (m + 1) * 128], H + "Wg1%d" % wb)
                    pfi[("bs", m)] = pf.add(Wbs[wb][:, :, :], w_bs[:, m * 128:(m + 1) * 128], H + "Wbs%d" % wb)
                for m in range(8):
                    wb = m % 2
                    pf.need(pfi[("bs", m)])
                    for (c0, N, tok0) in fblocks:
                        gi = ctr % 2
                        ctr += 1
                        pp, ppk = next_pA()
                        mm_group(pp[:, :N], ppk, [(Wg1[wb][:, k, :], nT[:, k, c0:c0 + N]) for k in range(8)], [nTk, H + "Wg1%d" % wb])
                        P.add("act", lambda e, pp=pp, N=N, gi=gi, m=m: e.activation(out=gsb[gi][:, :N], in_=pp[:, :N], func=AF.Sigmoid, bias=gb[:, m:m + 1], scale=1.0),
                              reads=[ppk, "gb"], writes=[H + "gsb%d" % gi])
                        pp2, ppk2 = next_pA()
                        mm_group(pp2[:, :N], ppk2, [(Wbs[wb][:, kt, :], ynT[:, kt, tok0:tok0 + N]) for kt in range(16)], [H + "ynT", H + "Wbs%d" % wb])
                        P.add("dve", lambda e, pp2=pp2, N=N, gi=gi, m=m, tok0=tok0: e.tensor_tensor(out=MG[:, m, tok0:tok0 + N], in0=pp2[:, :N], in1=gsb[gi][:, :N], op=ALU.mult),
                              reads=[ppk2, H + "gsb%d" % gi], writes=[H + "MG%d" % m])
            chk("S2_%d" % hf)
            P.barrier()
            ys.close()

            with contextlib.ExitStack() as sC:
                sbc = lambda n, s, dt=F32: sb(n + "_%d" % hf, s, dt, stack=sC)
                PGT = sbc("PGT", [128, 8, NPH + 64], BF16)
                Wu = [sbc("Wu%d" % i, [128, 8, 128], BF16) for i in range(2)]
                Wpg = [sbc("Wpg%d" % i, [128, 8, 128], BF16) for i in range(2)]
                Wgp = [sbc("Wgp%d" % i, [128, 8, 128], BF16) for i in range(2)]
                Wbp = [sbc("Wbp%d" % i, [128, 8, 128], BF16) for i in range(2)]
                PW = [sbc("PW%d" % i, [128, 2, 256], BF16) for i in range(2)]
                Wo = sbc("Wo", [128, 8, D], BF16)
                LP = 16 + NPH
                ubP = sbc("ubP", [128, LP])
                ubS = sbc("ubS", [128, 15 + 64])
                stP = [sbc("stP%d" % i, [128, LP]) for i in range(2)]
                stS = [sbc("stS%d" % i, [128, 15 + 64]) for i in range(2)]
                dT = sbc("dT", [128, 2, NPH + 64], BF16)
                spg = [sbc("spg%d" % i, [128, 512]) for i in range(2)]
                tmpc = [sbc("tmpc%d" % i, [128, 512]) for i in range(2)]
                xr = [sbc("xr%d" % i, [128, D]) for i in range(2)]
                yo = [sbc("yo%d" % i, [128, D]) for i in range(2)]
                junkc = sbc("junkc", [128, D], BF16)
                ssq4 = [sbc("ssq4_%d" % i, [128, 1]) for i in range(2)]
                wload(Wo[:, :, :], w_out[:, :], H + "Wo")
                ctr = 0
                pf = Prefetch()
                pfi = {}
                for gi_ in range(4):
                    pfi[("pw", gi_)] = pf.add(PW[gi_ % 2][:, :, :], pool_w[gi_], H + "PW%d" % (gi_ % 2))
                    for j in range(2):
                        ut = gi_ * 2 + j
                        pfi[("u", ut)] = pf.add(Wu[ut % 2][:, :, :], w_in[:, C_U + ut * 128: C_U + (ut + 1) * 128], H + "Wu%d" % (ut % 2))
                    for mo in range(2):
                        ot = gi_ * 2 + mo
                        pfi[("pg", ot)] = pf.add(Wpg[ot % 2][:, :, :], w_in[:, C_PG + ot * 128: C_PG + (ot + 1) * 128], H + "Wpg%d" % (ot % 2))
                for m in range(8):
                    pfi[("gp", m)] = pf.add(Wgp[m % 2][:, :, :], w_in[:, C_GP + m * 128: C_GP + (m + 1) * 128], H + "Wgp%d" % (m % 2))
                    pfi[("bp", m)] = pf.add(Wbp[m % 2][:, :, :], w_bp[:, m * 128:(m + 1) * 128], H + "Wbp%d" % (m % 2))
                for gi_ in range(4):
                    w = POOL_W[gi_]
                    pwb = gi_ % 2
                    pf.need(pfi[("pw", gi_)])
                    for j in range(2):
                        ut = gi_ * 2 + j
                        ub_ = ut % 2
                        pf.need(pfi[("u", ut)])
                        pp, ppk = next_pA()
                        mm_group(pp[:, 0:16], ppk, [(Wu[ub_][:, k, :], nT[:, k, 0:16]) for k in range(8)], [nTk, H + "Wu%d" % ub_])
                        P.add("act", lambda e, pp=pp: e.activation(out=ubP[:, 0:16], in_=pp[:, 0:16], func=AF.Copy), reads=[ppk], writes=[H + "ubP"])
                        for bi in range(2):
                            pp, ppk = next_pA()
                            c0 = 16 + bi * 512
                            mm_group(pp[:, :], ppk, [(Wu[ub_][:, k, :], nT[:, k, c0:c0 + 512]) for k in range(8)], [nTk, H + "Wu%d" % ub_])
                            P.add("act", lambda e, pp=pp, c0=c0: e.activation(out=ubP[:, c0:c0 + 512], in_=pp[:, :], func=AF.Copy), reads=[ppk], writes=[H + "ubP"])
                        seqs = [(ubP, stP, LP, 16, 0, H + "ubP", H + "stP")]
                        if has_s:
                            pp, ppk = next_pA()
                            c0 = 16 + NPH
                            mm_group(pp[:, 0:64], ppk, [(Wu[ub_][:, k, :], nT[:, k, c0:c0 + 64]) for k in range(8)], [nTk, H + "Wu%d" % ub_])
                            P.add("act", lambda e, pp=pp: e.activation(out=ubS[:, 15:79], in_=pp[:, 0:64], func=AF.Copy), reads=[ppk], writes=[H + "ubS"])
                            P.add("pool", lambda e, ut=ut: e.tensor_copy(out=ubS[:, 0:15], in_=splf[:, ut, :]), reads=["splf"], writes=[H + "ubS"])
                            P.add("dve", lambda e, ut=ut: e.tensor_copy(out=plPs[:, ut, :], in_=ubP[:, LP - 15:LP]), reads=[H + "ubP"], writes=["plPs"])
                            P.add("dve", lambda e, ut=ut: e.tensor_copy(out=plSs[:, ut, :], in_=ubS[:, 64:79]), reads=[H + "ubS"], writes=["plSs"])
                            seqs.append((ubS, stS, 79, 15, NPH, H + "ubS", H + "stS"))
                        for (ub, st, L, v0, tk0, ubk, stk) in seqs:
                            cur, curk = ub, ubk
                            sh = 1
                            lo = 0
                            si = 0
                            while sh < w:
                                nlo = lo + sh
                                dstb, dstk = st[si], stk + "%d" % si
                                eng = "pool" if si == 0 else "dve"
                                P.add(eng, lambda e, cur=cur, dstb=dstb, nlo=nlo, sh=sh, L=L: e.tensor_tensor(out=dstb[:, nlo:L], in0=cur[:, nlo:L], in1=cur[:, nlo - sh:L - sh], op=ALU.add),
                                      reads=[curk], writes=[dstk])
                                cur, curk = dstb, dstk
                                lo = nlo
                                sh *= 2
                                si ^= 1
                            nv = L - v0
                            P.add("dve", lambda e, cur=cur, ub=ub, v0=v0, L=L, j=j, tk0=tk0, nv=nv, w=w: e.scalar_tensor_tensor(
                                out=dT[:, j, tk0:tk0 + nv], in0=cur[:, v0:L], scalar=1.0 / w, in1=ub[:, v0:L], op0=ALU.mult, op1=ALU.subtract),
                                reads=[curk, ubk], writes=[H + "dT%d" % j])
                            if hf == 0 and tk0 == 0:
                                P.add("dve", lambda e, cur=cur, gi_=gi_: e.tensor_tensor(out=tmpc[0][:, 0:16], in0=cur[:, 16:32], in1=pinv[:, gi_ * 16:(gi_ + 1) * 16], op=ALU.mult),
                                      reads=[curk, "pinv"], writes=[H + "tmpc0"])
                                P.add("dve", lambda e, ub=ub, j=j: e.tensor_tensor(out=dT[:, j, 0:16], in0=tmpc[0][:, 0:16], in1=ub[:, 16:32], op=ALU.subtract),
                                      reads=[H + "tmpc0", ubk], writes=[H + "dT%d" % j])
                    for mo in range(2):
                        ot = gi_ * 2 + mo
                        wb = ot % 2
                        pf.need(pfi[("pg", ot)])
                        for (c0, N, tok0) in fblocks:
                            gi = ctr % 2
                            ctr += 1
                            pp, ppk = next_pA()
                            mm_group(pp[:, :N], ppk, [(Wpg[wb][:, k, :], nT[:, k, c0:c0 + N]) for k in range(8)], [nTk, H + "Wpg%d" % wb])
                            P.add("act", lambda e, pp=pp, N=N, gi=gi: e.activation(out=spg[gi][:, :N], in_=pp[:, :N], func=AF.Silu), reads=[ppk], writes=[H + "spg%d" % gi])
                            pp2, ppk2 = next_pA()
                            mm_group(pp2[:, :N], ppk2, [(PW[pwb][:, kt, mo * 128:(mo + 1) * 128], dT[:, kt, tok0:tok0 + N]) for kt in range(2)], [H + "dT0", H + "dT1", H + "PW%d" % pwb])
                            P.add("dve", lambda e, pp2=pp2, N=N, gi=gi, ot=ot, tok0=tok0: e.scalar_tensor_tensor(out=PGT[:, ot, tok0:tok0 + N], in0=pp2[:, :N], scalar=psc[:, ot:ot + 1],
                                                                                                               in1=spg[gi][:, :N], op0=ALU.mult, op1=ALU.mult),
                                  reads=[ppk2, H + "spg%d" % gi, "psc"], writes=[H + "PGT"])
                for m in range(8):
                    wb = m % 2
                    pf.need(pfi[("bp", m)])
                    for (c0, N, tok0) in fblocks:
                        gi = ctr % 2
                        ctr += 1
                        pp, ppk = next_pA()
                        mm_group(pp[:, :N], ppk, [(Wgp[wb][:, k, :], nT[:, k, c0:c0 + N]) for k in range(8)], [nTk, H + "Wgp%d" % wb])
                        P.add("act", lambda e, pp=pp, N=N, gi=gi, m=m: e.activation(out=spg[gi][:, :N], in_=pp[:, :N], func=AF.Sigmoid, bias=gb[:, 8 + m:9 + m], scale=1.0),
                              reads=[ppk, "gb"], writes=[H + "spg%d" % gi])
                        pp2, ppk2 = next_pA()
                        mm_group(pp2[:, :N], ppk2, [(Wbp[wb][:, k, :], PGT[:, k, tok0:tok0 + N]) for k in range(8)], [H + "PGT", H + "Wbp%d" % wb])
                        P.add("dve", lambda e, pp2=pp2, N=N, gi=gi: e.tensor_tensor(out=tmpc[gi][:, :N], in0=pp2[:, :N], in1=spg[gi][:, :N], op=ALU.mult),
                              reads=[ppk2, H + "spg%d" % gi], writes=[H + "tmpc%d" % gi])
                        P.add("pool", lambda e, N=N, gi=gi, m=m, tok0=tok0: e.tensor_tensor(out=MG[:, m, tok0:tok0 + N], in0=MG[:, m, tok0:tok0 + N], in1=tmpc[gi][:, :N], op=ALU.add),
                              reads=[H + "tmpc%d" % gi, H + "MG%d" % m], writes=[H + "MG%d" % m])
                chk("S3_%d" % hf)
                mgk = [H + "MG%d" % m for m in range(8)]
                for (c, col0, T, tok0) in chunks:
                    b = c % 2
                    samp = (c == 8)
                    if samp:
                        P.dma(lambda e, b=b: e.dma_start(out=xr[b][:64, :], in_=xin[2064:2128, :]), writes=[H + "xr%d" % b])
                    else:
                        r0 = 16 + hf * NPH + c * 128
                        P.dma(lambda e, b=b, r0=r0: e.dma_start(out=xr[b][:, :], in_=xin[r0:r0 + 128, :]), writes=[H + "xr%d" % b])
                    for half in range(2):
                        pp, ppk = next_pA()
                        mm_group(pp[:T, :], ppk, [(MG[:, k, tok0:tok0 + T], Wo[:, k, half * 512:(half + 1) * 512]) for k in range(8)], mgk + [H + "Wo"])
                        P.add("dve", lambda e, pp=pp, T=T, b=b, half=half: e.tensor_tensor(out=xr[b][:T, half * 512:(half + 1) * 512], in0=pp[:T, :], in1=xr[b][:T, half * 512:(half + 1) * 512], op=ALU.add),
                              reads=[ppk, H + "xr%d" % b], writes=[H + "xr%d" % b])
                    P.add("act", lambda e, T=T, b=b: e.activation(out=junkc[:T, :], in_=xr[b][:T, :], func=AF.Square, accum_out=ssq4[b][:T, :]),
                          reads=[H + "xr%d" % b], writes=[H + "junkc", H + "ssq4%d" % b])
                    rstd_from_ssq(ssq4[b][:T, :], H + "ssq4%d" % b, 1.0 / D)
                    P.add("dve", lambda e, T=T, b=b: e.scalar_tensor_tensor(out=yo[b][:T, :], in0=xr[b][:T, :], scalar=ssq4[b][:T, 0:1], in1=fg[:T, :], op0=ALU.mult, op1=ALU.mult),
                          reads=[H + "xr%d" % b, H + "ssq4%d" % b, "fg"], writes=[H + "yo%d" % b])
                    if samp:
                        P.dma(lambda e, b=b: e.dma_start(out=yS[:, :], in_=yo[b][:64, :]), reads=[H + "yo%d" % b])
                    else:
                        r0 = hf * NPH + c * 128
                        P.dma(lambda e, b=b, r0=r0: e.dma_start(out=yP[r0:r0 + 128, :], in_=yo[b][:, :]), reads=[H + "yo%d" % b])
            P.barrier()
        except _Stop:
            pass
        P.muted = False
        P.dma(lambda e: e.dma_start(out=cvP, in_=cvPs[:]), reads=["cvPs"])
        P.dma(lambda e: e.dma_start(out=cvS, in_=cvSs[:]), reads=["cvSs"])
        P.dma(lambda e: e.dma_start(out=plP, in_=plPs[:]), reads=["plPs"])
        P.dma(lambda e: e.dma_start(out=plS, in_=plSs[:]), reads=["plSs"])
        P.emit(nc)
    P.stats["dbg"] = dbg_list
    return nc, P.stats


_CACHE = {}
STOP = None


def kernel(x_prompt, x_sample, state_ssd, state_conv, state_pool, norm_g, w_in, conv_w, conv_b,
           dt_bias, a_log, d_skip, ssd_norm_g, w_branch_ssd, pool_w, pool_scale, w_branch_pool,
           gate_b, w_out, final_g):
    f = lambda a: np.ascontiguousarray(np.asarray(a, dtype=np.float32))
    x_prompt, x_sample = f(x_prompt), f(x_sample)
    if "nc" not in _CACHE:
        _CACHE["nc"] = build_program(STOP)
    nc, stats = _CACHE["nc"]
    shared = {
        "w_in": f(w_in[0]), "w_bs": f(w_branch_ssd[0]), "w_bp": f(w_branch_pool[0]), "w_out": f(w_out[0]),
        "pool_w": f(pool_w[0]),
        "ng": f(np.asarray(norm_g[0]).reshape(8, 128).T),
        "cw": f(np.asarray(conv_w[0]).reshape(4, 32, 128).transpose(2, 1, 0)),
        "cb": f(np.asarray(conv_b[0]).reshape(32, 128).T),
        "sng": f(np.asarray(ssd_norm_g[0]).reshape(16, 128).T),
        "psc": f(np.asarray(pool_scale[0]).reshape(8, 128).T),
        "gb": f(np.asarray(gate_b[0]).reshape(16, 128).T),
        "dtb": f(np.asarray(dt_bias[0]).reshape(1, 32)), "alog": f(np.asarray(a_log[0]).reshape(1, 32)),
        "dsk": f(np.asarray(d_skip[0]).reshape(1, 32)), "fg": f(np.asarray(final_g).reshape(1, 1024)),
    }
    in_maps = []
    for q in range(8):
        seq, part = q // 4, q % 4
        s0 = part * 2048
        halo = x_prompt[seq, s0 - 16:s0] if part > 0 else np.zeros((16, D), np.float32)
        xin = np.concatenate([halo, x_prompt[seq, s0:s0 + 2048], x_sample[q]], axis=0)
        pinv = np.ones((4, 16), np.float32)
        for gi, w in enumerate(POOL_W):
            for t in range(16):
                pinv[gi, t] = 1.0 / min(w, s0 + t + 1)
        m = dict(shared)
        m["xin"] = f(xin)
        m["sst"] = f(np.asarray(state_ssd[0, q]).reshape(2048, 128).T)
        scv3 = np.asarray(state_conv[0, q]).reshape(3, 32, 128).transpose(2, 1, 0)
        m["scv"] = f(np.concatenate([np.zeros((128, 32, 1), np.float32), scv3], axis=2))
        m["spl"] = f(np.asarray(state_pool[0, q]).reshape(15, 8, 128).transpose(2, 1, 0))
        m["pinv"] = f(pinv.reshape(1, 64))
        npre_rows = NPRE * NPH + 16
        xp = np.zeros((npre_rows, D), np.float32)
        have = min(s0, npre_rows)
        if have > 0:
            xp[npre_rows - have:] = x_prompt[seq, s0 - have:s0]
        pmv = np.zeros((1, 8), np.float32)
        for k in range(NPRE):
            if s0 - NPRE * NPH + k * NPH >= 0:
                pmv[0, k] = 1.0
        m["xpre"] = xp
        m["pm"] = pmv
        in_maps.append(m)
    res = run_bass_kernel_spmd(nc, in_maps, core_ids=list(range(8)))
    R = res.results
    global LAST_RESULTS
    LAST_RESULTS = R
    y_prompt = np.zeros((2, 8192, D), np.float32)
    y_sample = np.zeros((8, 64, D), np.float32)
    ssd_p = np.zeros((1, 2, 32, 64, 128), np.float32)
    ssd_s = np.zeros((1, 8, 32, 64, 128), np.float32)
    conv_p = np.zeros((1, 2, 3, 4096), np.float32)
    conv_s = np.zeros((1, 8, 3, 4096), np.float32)
    pool_p = np.zeros((1, 2, 15, 1024), np.float32)
    pool_s = np.zeros((1, 8, 15, 1024), np.float32)
    for q in range(8):
        seq, part = q // 4, q % 4
        r = R[q]
        y_prompt[seq, part * 2048:(part + 1) * 2048] = r["yP"]
        y_sample[q] = r["yS"]
        ssd_s[0, q] = np.asarray(r["hSo"]).T.reshape(32, 64, 128)
        conv_s[0, q] = np.asarray(r["cvS"]).transpose(2, 1, 0).reshape(3, 4096)
        pool_s[0, q] = np.asarray(r["plS"]).transpose(2, 1, 0).reshape(15, 1024)
        if part == 3:
            ssd_p[0, seq] = np.asarray(r["hPo"]).T.reshape(32, 64, 128)
            conv_p[0, seq] = np.asarray(r["cvP"]).transpose(2, 1, 0).reshape(3, 4096)
            pool_p[0, seq] = np.asarray(r["plP"]).transpose(2, 1, 0).reshape(15, 1024)
    return (y_prompt, y_sample, ssd_p, ssd_s, conv_p, conv_s, pool_p, pool_s)
```

```python
import contextlib
import numpy as np
import concourse.bass as bass
import concourse.mybir as mybir
from concourse.bass_utils import run_bass_kernel_spmd

F32 = mybir.dt.float32
BF16 = mybir.dt.bfloat16
AF = mybir.ActivationFunctionType
ALU = mybir.AluOpType

ENGS = ("pe", "act", "dve", "pool", "sp")
N_DMA_SEMS = 12
EPS = 1e-6


class Op:
    __slots__ = ("eng", "fn", "reads", "writes", "deps", "sig", "sig_idx", "dma_slot", "dma_cnt",
                 "is_dma", "idx", "barrier", "dma_snapshot")

    def __init__(self, eng, fn, reads, writes, is_dma, barrier=False):
        self.eng = eng
        self.fn = fn
        self.reads = tuple(reads)
        self.writes = tuple(writes)
        self.deps = []
        self.sig = False
        self.sig_idx = 0
        self.dma_slot = -1
        self.dma_cnt = 0
        self.is_dma = is_dma
        self.barrier = barrier
        self.dma_snapshot = None


class _Rec:
    def __init__(self):
        self.call = None

    def __getattr__(self, name):
        def f(*a, **k):
            self.call = (name, a, k)
            return self
        return f


def _eager(fn):
    r = _Rec()
    fn(r)
    name, a, k = r.call
    return lambda e: getattr(e, name)(*a, **k)


PSUM_BANK = {"pA0": "pA0", "pA1": "pA1", "pA2": "pA2", "pT": "pT", "pT2": "pT",
             "pL0a": "pL", "pL1a": "pL", "pL2a": "pL", "pL3a": "pL", "pSm": "pLb", "pL0b": "pLb", "pL1b": "pLb", "pL2b": "pLb", "pL3b": "pLb",
             "pY": "pY", "pZo": "pZS", "pS": "pZS", "pCB": "pZS", "pZ": "pCBt"}


def _flat(keys):
    out = []
    for k in keys:
        if isinstance(k, (list, tuple)):
            out.extend(_flat(k))
        else:
            out.append(k)
    return out


class Prog:
    def __init__(self):
        self.ops = []
        self.muted = False

    def add(self, eng, fn, reads=(), writes=()):
        if not self.muted:
            reads, writes = _flat(reads), _flat(writes)
            if eng != "pe":
                extra = ["bk_" + PSUM_BANK[k] for k in reads if k in PSUM_BANK]
            else:
                extra = ["bk_" + PSUM_BANK[k] for k in writes if k in PSUM_BANK]
            if extra:
                writes = list(writes) + sorted(set(extra))
            self.ops.append(Op(eng, _eager(fn), reads, writes, False))

    def dma(self, fn, reads=(), writes=(), eng="sp"):
        if not self.muted:
            self.ops.append(Op(eng, _eager(fn), reads, writes, True))

    def barrier(self):
        if not self.muted:
            self.ops.append(Op("all", None, (), (), False, barrier=True))

    def emit(self, nc):
        ops = self.ops
        last_w = {}
        readers = {}
        for i, op in enumerate(ops):
            op.idx = i
            if op.barrier:
                last_w = {}
                readers = {}
                continue
            deps = set()
            for k in op.reads:
                j = last_w.get(k)
                if j is not None:
                    deps.add((j, "raw"))
            for k in op.writes:
                j = last_w.get(k)
                if j is not None:
                    deps.add((j, "waw"))
                for j in readers.get(k, ()):
                    deps.add((j, "war"))
            for k in op.reads:
                lst = readers.setdefault(k, [])
                if not op.is_dma:
                    lst[:] = [j for j in lst if ops[j].is_dma or ops[j].eng != op.eng]
                lst.append(i)
            for k in op.writes:
                last_w[k] = i
                readers[k] = []
            best = set()
            for j, kind in deps:
                if j == i:
                    continue
                pj = ops[j]
                if pj.eng == op.eng and not pj.is_dma and not op.is_dma:
                    if op.eng == "pe":
                        continue
                    if kind != "raw":
                        continue
                best.add(j)
            op.deps = sorted(best)
            for j in op.deps:
                ops[j].sig = True
        cnt = {e: 0 for e in ENGS}
        dma_cnt = [0] * N_DMA_SEMS
        ndma = 0
        nsw = 0
        nhw = 0
        nbar = 0
        for op in ops:
            if op.barrier:
                op.dma_snapshot = list(dma_cnt)
                nbar += 1
                op.sig_idx = nbar
            elif op.is_dma:
                if op.eng == "pool":
                    op.dma_slot = N_DMA_SEMS // 2 + (nsw % (N_DMA_SEMS // 2))
                    nsw += 1
                else:
                    op.dma_slot = nhw % (N_DMA_SEMS // 2)
                    nhw += 1
                dma_cnt[op.dma_slot] += 1
                op.dma_cnt = dma_cnt[op.dma_slot]
                ndma += 1
            elif op.sig:
                cnt[op.eng] += 1
                op.sig_idx = cnt[op.eng]
        self.stats = dict(cnt, ndma=ndma, nops=len(ops), nbar=nbar)

        with contextlib.ExitStack() as es:
            esem = {e: es.enter_context(nc.semaphore("s_" + e)) for e in ("pe", "act", "dve", "pool")}
            dsem = [es.enter_context(nc.semaphore("s_dma%d" % i)) for i in range(N_DMA_SEMS)]
            bsem = es.enter_context(nc.semaphore("s_bar"))
            block = es.enter_context(nc.Block())

            def run(ename, eng):
                waited = {}

                def wait_dma(slot, v):
                    key = ("d", slot)
                    if v > 0 and waited.get(key, 0) < v:
                        eng.wait_ge(dsem[slot], v)
                        waited[key] = v

                for op in ops:
                    if op.barrier:
                        for s in range(N_DMA_SEMS):
                            wait_dma(s, 16 * op.dma_snapshot[s])
                        eng.drain().then_inc(bsem, 1)
                        eng.wait_ge(bsem, len(ENGS) * op.sig_idx)
                        continue
                    if op.eng != ename:
                        continue
                    if op.is_dma and op.dma_cnt > 1:
                        wait_dma(op.dma_slot, 16 * (op.dma_cnt - 1))
                    for j in op.deps:
                        pj = ops[j]
                        if pj.is_dma:
                            wait_dma(pj.dma_slot, 16 * pj.dma_cnt)
                        else:
                            key = ("e", pj.eng)
                            v = pj.sig_idx
                            if waited.get(key, 0) < v:
                                eng.wait_ge(esem[pj.eng], v)
                                waited[key] = v
                    ins = op.fn(eng)
                    if op.is_dma:
                        ins.then_inc(dsem[op.dma_slot], 16)
                    elif op.sig:
                        ins.then_inc(esem[op.eng], 1)
                if ename == "sp":
                    for s in range(N_DMA_SEMS):
                        wait_dma(s, 16 * dma_cnt[s])

            @block.tensor
            def _(e):
                run("pe", e)

            @block.scalar
            def _(e):
                run("act", e)

            @block.vector
            def _(e):
                run("dve", e)

            @block.gpsimd
            def _(e):
                run("pool", e)

            @block.sync
            def _(e):
                run("sp", e)


D = 1024
DI = 2048
NCOL = 10272
C_XBC = 2048
C_DT = 6144
C_U = 6176
C_PG = 7200
C_GS = 8224
C_GP = 9248
NPH = 1024
NTOKH = 16 + NPH + 64
NPRE = 6
POOL_W = (2, 4, 8, 16)


class _Stop(Exception):
    pass


def build_program(stop=None):
    nc = bass.Bass("TRN2", target_bir_lowering=False)

    def chk(name):
        if stop is not None and name == stop:
            P.muted = True

    di = lambda n, s: nc.dram_tensor(n, s, F32, kind="ExternalInput").ap()
    do = lambda n, s: nc.dram_tensor(n, s, F32, kind="ExternalOutput").ap()
    xin = di("xin", [2128, D])
    xpre = di("xpre", [NPRE * NPH + 16, D])
    pm_d = di("pm", [1, 8])
    w_in = di("w_in", [D, NCOL])
    w_bs = di("w_bs", [DI, D])
    w_bp = di("w_bp", [D, D])
    w_out = di("w_out", [D, D])
    pool_w = di("pool_w", [4, 256, 256])
    sst = di("sst", [128, 2048])
    scv = di("scv", [128, 32, 4])
    spl = di("spl", [128, 8, 15])
    ng_d = di("ng", [128, 8])
    cw_d = di("cw", [128, 32, 4])
    cb_d = di("cb", [128, 32])
    sng_d = di("sng", [128, 16])
    psc_d = di("psc", [128, 8])
    gb_d = di("gb", [128, 16])
    dtb_d = di("dtb", [1, 32])
    alog_d = di("alog", [1, 32])
    dsk_d = di("dsk", [1, 32])
    fg_d = di("fg", [1, D])
    pinv_d = di("pinv", [1, 64])
    yP = do("yP", [2048, D])
    yS = do("yS", [64, D])
    hPo = do("hPo", [128, 2048])
    hSo = do("hSo", [128, 2048])
    cvP = do("cvP", [128, 32, 3])
    cvS = do("cvS", [128, 32, 3])
    plP = do("plP", [128, 8, 15])
    plS = do("plS", [128, 8, 15])

    P = Prog()
    import os as _os2
    DBG = bool(_os2.environ.get("KDEBUG"))
    dbg_list = []

    def dbg_out(name, ap, shape, key):
        if not DBG:
            return
        t = nc.dram_tensor("dbg_" + name, list(shape), F32, kind="ExternalOutput").ap()
        dbg_list.append(name)
        P.dma(lambda e: e.dma_start(out=t, in_=ap), reads=[key], eng="pool")

    with contextlib.ExitStack() as es:
        def sb(name, shape, dt=F32, stack=es):
            return stack.enter_context(nc.sbuf_tensor(name, shape, dt))

        def ps(name, shape, dt=F32):
            return es.enter_context(nc.psum_tensor(name, shape, dt))

        pA = [ps("pA0", [128, 512]), ps("pA1", [128, 512])]
        pT = ps("pT", [128, 1024], BF16)
        pL = ps("pL", [128, 4, 128])
        pLb = ps("pLb", [128, 4, 128])
        pLs = [pL, pLb]
        pSm = pLb[:, :, :].rearrange("p a b -> p (a b)")[:, 0:256]
        pTb = pLb[:, :, :].rearrange("p a b -> p (a b)").bitcast(BF16)
        pLb_keys = ("pL0b", "pL1b", "pL2b", "pL3b", "pSm")
        pY = ps("pY", [128, 512])
        pZS = ps("pZS", [128, 512])
        pCBt = ps("pCBt", [128, 512])
        pa_ctr = [0]
        pa_ext = [False]
        pa_all = [(pA[0], ("pA0",)), (pA[1], ("pA1",)), (pCBt, ("pZ",)),
                  (pL[:, :, :].rearrange("p a b -> p (a b)"), ("pL0a", "pL1a", "pL2a", "pL3a")),
                  (pLb[:, :, :].rearrange("p a b -> p (a b)"), ("pL0b", "pL1b", "pL2b", "pL3b", "pSm")),
                  (pY, ("pY",))]

        def next_pA():
            n = 6 if pa_ext[0] else 2
            i = pa_ctr[0] % n
            pa_ctr[0] += 1
            return pa_all[i]

        tri = sb("tri", [128, 128])
        ones = sb("ones", [128, 128])
        identf = sb("identf", [128, 128])
        ident = sb("ident", [128, 128], BF16)
        trib = sb("trib", [128, 128], BF16)
        onesb = sb("onesb", [128, 128], BF16)
        a_hi = sb("a_hi", [128, 9 * 32], BF16)
        a_lo = sb("a_lo", [128, 9 * 32], BF16)
        ng = sb("ng_s", [128, 8])
        cw = sb("cw_s", [128, 32, 4])
        cb = sb("cb_s", [128, 32])
        sng = sb("sng_s", [128, 16])
        psc = sb("psc_s", [128, 8])
        gb = sb("gb_s", [128, 16])
        dtb = sb("dtb_s", [128, 32])
        Abc = sb("Abc", [128, 32])
        dsk = sb("dsk_s", [128, 32])
        fg = sb("fg_s", [128, D])
        pinv = sb("pinv_s", [128, 64])
        pm = sb("pm_s", [128, 8])
        scvb = sb("scvb", [128, 32, 4], BF16)
        splf = sb("splf", [128, 8, 15])
        Wdt = sb("Wdt", [128, 8, 32], BF16)
        cvPs = sb("cvPs", [128, 32, 3])
        cvSs = sb("cvSs", [128, 32, 3])
        plPs = sb("plPs", [128, 8, 15])
        plSs = sb("plSs", [128, 8, 15])
        hcar = sb("hcar", [128, 2048])
        NCH = 9
        dtn = ["dtx", "mx", "tA", "dt", "a", "nac", "e", "dtdec", "dec"]
        dta = {n: sb("dt_" + n, [128, NCH * 32]) for n in dtn}
        nT = sb("nT", [128, 8, NTOKH], BF16)
        MG = sb("MG", [128, 8, NPH + 64], BF16)

        P.add("pool", lambda e: e.memset(identf[:], 0.0), writes=["identf"])
        P.add("pool", lambda e: e.memset(tri[:], 1.0), writes=["tri"])
        P.add("pool", lambda e: e.memset(ones[:], 1.0), writes=["ones"])
        P.add("pool", lambda e: e.memset(hcar[:], 0.0), writes=["hcar"])
        P.add("pool", lambda e: e.affine_select(out=identf[:], in_=tri[:], pattern=[[-1, 128]], compare_op=ALU.is_equal,
                                                fill=0.0, base=0, channel_multiplier=1), reads=["tri"], writes=["identf"])
        P.add("pool", lambda e: e.affine_select(out=tri[:], in_=tri[:], pattern=[[1, 128]], compare_op=ALU.is_ge,
                                                fill=0.0, base=0, channel_multiplier=-1), reads=["tri", "identf"], writes=["tri"])
        P.add("dve", lambda e: e.tensor_copy(out=ident[:], in_=identf[:]), reads=["identf"], writes=["ident"])
        P.add("dve", lambda e: e.tensor_copy(out=trib[:], in_=tri[:]), reads=["tri"], writes=["trib"])
        P.add("dve", lambda e: e.tensor_copy(out=onesb[:], in_=ones[:]), reads=["ones"], writes=["onesb"])
        for dst, src, key in ((ng, ng_d, "ng"), (cw, cw_d, "cw"), (cb, cb_d, "cb"), (sng, sng_d, "sng"), (psc, psc_d, "psc"),
                              (gb, gb_d, "gb"), (splf, spl, "splf")):
            P.dma(lambda e, dst=dst, src=src: e.dma_start(out=dst[:], in_=src), writes=[key])
        for dst, src, key in ((dtb, dtb_d, "dtb"), (Abc, alog_d, "Abc"), (dsk, dsk_d, "dsk"), (fg, fg_d, "fg"), (pinv, pinv_d, "pinv"), (pm, pm_d, "pm")):
            P.dma(lambda e, dst=dst, src=src: e.dma_start(out=dst[:], in_=src[0:1, :].partition_broadcast(128)), writes=[key])
        P.dma(lambda e: e.dma_start(out=scvb[:], in_=scv), writes=["scvb"], eng="pool")
        P.dma(lambda e: e.dma_start(out=Wdt[:], in_=w_in[:, C_DT:C_DT + 32].rearrange("(k p) c -> p k c", p=128)), writes=["Wdt"], eng="pool")
        P.add("act", lambda e: e.activation(out=Abc[:], in_=Abc[:], func=AF.Exp), reads=["Abc"], writes=["Abc"])
        P.add("dve", lambda e: e.tensor_scalar(out=Abc[:], in0=Abc[:], scalar1=-1.0, scalar2=None, op0=ALU.mult), reads=["Abc"], writes=["Abc"])

        def wload(dst, src_rows_cols, key):
            P.dma(lambda e: e.dma_start(out=dst, in_=src_rows_cols.rearrange("(k p) c -> p k c", p=128)), writes=[key], eng="pool")

        class Prefetch:
            def __init__(self):
                self.th = []
                self.issued = 0

            def add(self, dst, src, key):
                self.th.append((dst, src, key))
                return len(self.th) - 1

            def need(self, i):
                while self.issued <= min(i + 2, len(self.th) - 1):
                    wload(*self.th[self.issued])
                    self.issued += 1

        def mm_group(out_ap, okey, pairs, rkeys):
            n = len(pairs)
            for i, (l, r) in enumerate(pairs):
                P.add("pe", lambda e, l=l, r=r, i=i: e.matmul(out_ap, lhsT=l, rhs=r, start=(i == 0), stop=(i == n - 1)),
                      reads=rkeys, writes=[okey])

        def rstd_from_ssq(ssq_ap, key, inv_n):
            P.add("dve", lambda e: e.tensor_scalar(out=ssq_ap, in0=ssq_ap, scalar1=inv_n, scalar2=EPS, op0=ALU.mult, op1=ALU.add),
                  reads=[key], writes=[key])
            P.add("act", lambda e: e.activation(out=ssq_ap, in_=ssq_ap, func=AF.Sqrt), reads=[key], writes=[key])
            P.add("dve", lambda e: e.reciprocal(out=ssq_ap, in_=ssq_ap), reads=[key], writes=[key])

        ys = contextlib.ExitStack()
        try:
          for blk in range(NPRE + 2):
            pre = blk < NPRE
            hf = -1 if pre else blk - NPRE
            xsrc, xrow0 = (xpre, blk * NPH) if pre else (xin, hf * NPH)
            has_s = (hf == 1)
            ncols = NTOKH if has_s else 16 + NPH
            ntok = NPH + 64 if has_s else NPH
            chunks = [(c, 16 + c * 128, 128, c * 128) for c in range(8)]
            if has_s:
                chunks.append((8, 16 + NPH, 64, NPH))
            fblocks = [(16, 512, 0), (528, 512, 512)]
            if has_s:
                fblocks.append((16 + NPH, 64, NPH))
            H = "pre_" if pre else "h%d_" % hf
            pa_ext[0] = pre

            if not pre:
                ys = contextlib.ExitStack()
                ynT = ys.enter_context(nc.sbuf_tensor("ynT%d" % hf, [128, 16, NPH + 64], BF16))
            if pre and blk > 0:
                new_alloc = False
            else:
                new_alloc = True
                sa = contextlib.ExitStack()
            if new_alloc:
                sba = lambda n, s, dt=F32, sa=sa, blk=blk: sb(n + "_b%d" % blk, s, dt, stack=sa)
                xt = [sba("xt%d" % i, [128, D]) for i in range(2)]
                xb = [sba("xb%d" % i, [128, D], BF16) for i in range(2)]
                ssq0 = [sba("ssq0_%d" % i, [128, 1]) for i in range(2)]
                Wx = [sba("Wx%d" % i, [128, 8, 512], BF16) for i in range(2)]
                Wz = [sba("Wz%d" % i, [128, 8, 256], BF16) for i in range(2)]
                rawP = sba("rawP", [128, 4, 4 + NPH], BF16)
                rawS = sba("rawS", [128, 4, 4 + 64], BF16)
                xsT = sba("xsT", [128, 2, NPH + 64], BF16)
                BT = sba("BT", [128, NPH + 64], BF16)
                CT = sba("CT", [128, NPH + 64], BF16)
                xBt = sba("xBt", [128, NCH, 384], BF16)
                hTa = sba("hTa", [128, NCH, 256], BF16)
                yza = sba("yza", [128, NCH, 256], BF16)
                ssqa = sba("ssqa", [128, NCH])
                dg = sba("dg", [128, 4, 4, 128], BF16)
                szs = sba("szs", [128, 256])
                CBm = [sba("CBm%d" % p_, [128, 128]) for p_ in range(2)]
                Eh = [[sba("E%d_%d" % (p_, i), [128, 128]) for i in range(4)] for p_ in range(2)]
                MT = [[sba("MT%d_%d" % (p_, i), [128, 128], BF16) for i in range(4)] for p_ in range(2)]
                t1 = [sba("t1_%d" % p_, [128, 256]) for p_ in range(2)]
                t3 = sba("t3", [128, 256])
                xdt = [sba("xdt%d" % p_, [128, 256], BF16) for p_ in range(2)]
                xd = [sba("xd%d" % p_, [128, 256], BF16) for p_ in range(2)]
                xdd = [sba("xdd%d" % p_, [128, 256], BF16) for p_ in range(2)]
                tmpH = sba("tmpH", [128, 256])
                hS = sba("hS", [128, 256])
                yn = [sba("yn%d" % p_, [128, 256], BF16) for p_ in range(2)]
                junk = sba("junkA", [128, 256], BF16)

            if True:
                def issue_w(g_):
                    wb_ = g_ % 2
                    wload(Wx[wb_][:, :, 0:256], w_in[:, C_XBC + g_ * 256: C_XBC + (g_ + 1) * 256], H + "Wx%da" % wb_)
                    wload(Wx[wb_][:, :, 256:384], w_in[:, C_XBC + 2048 + g_ * 128: C_XBC + 2048 + (g_ + 1) * 128], H + "Wx%db" % wb_)
                    if not pre:
                        wload(Wx[wb_][:, :, 384:512], w_in[:, C_XBC + 3072 + g_ * 128: C_XBC + 3072 + (g_ + 1) * 128], H + "Wx%dc" % wb_)
                        wload(Wz[wb_][:, :, :], w_in[:, g_ * 256:(g_ + 1) * 256], H + "Wz%d" % wb_)

                issue_w(0)
                for ti, j0 in enumerate(range(0, ncols, 128)):
                    R = min(128, ncols - j0)
                    b = ti % 2
                    xk, bk, sk = H + "xt%d" % b, H + "xb%d" % b, H + "ssq0%d" % b
                    if j0 + R <= 16 + NPH:
                        P.dma(lambda e, b=b, j0=j0, R=R: e.dma_start(out=xt[b][:R, :], in_=xsrc[xrow0 + j0: xrow0 + j0 + R, :]), writes=[xk])
                    else:
                        rp = 16 + NPH - j0
                        P.dma(lambda e, b=b, j0=j0, rp=rp: e.dma_start(out=xt[b][:rp, :], in_=xsrc[xrow0 + j0: xrow0 + j0 + rp, :]), writes=[xk])
                        P.dma(lambda e, b=b, rp=rp: e.dma_start(out=xt[b][rp:rp + 64, :], in_=xin[2064:2128, :]), writes=[xk])
                    P.add("act", lambda e, b=b, R=R: e.activation(out=xb[b][:R, :], in_=xt[b][:R, :], func=AF.Square, accum_out=ssq0[b][:R, :]),
                          reads=[xk], writes=[bk, sk])
                    rstd_from_ssq(ssq0[b][:R, :], sk, 1.0 / D)
                    P.add("dve", lambda e, b=b, R=R: e.tensor_scalar(out=xb[b][:R, :], in0=xt[b][:R, :], scalar1=ssq0[b][:R, 0:1], scalar2=None, op0=ALU.mult),
                          reads=[xk, sk], writes=[bk])
                    for k in range(8):
                        P.add("pe", lambda e, b=b, R=R, k=k: e.transpose(out=pT[:, k * 128:k * 128 + R], in_=xb[b][:R, k * 128:(k + 1) * 128], identity=ident[:R, :R]),
                              reads=[bk, "ident"], writes=["pT"])
                    P.add("dve", lambda e, R=R, j0=j0: e.tensor_tensor(out=nT[:, :, j0:j0 + R], in0=pT[:, :].rearrange("p (k t) -> p k t", k=8)[:, :, :R],
                                                                       in1=ng[:, :].unsqueeze(2).to_broadcast([128, 8, R]), op=ALU.mult),
                          reads=["pT", "ng"], writes=[H + "nT"])
                nTk = H + "nT"
                chk("S0_%d" % hf)

                import os as _os
                if "S0b" in _os.environ.get("KSKIP", ""):
                    P.muted = True
                dk = lambda n: H + "dt_" + n
                A3 = lambda n: dta[n][:, :].rearrange("p (c h) -> p c h", h=32)
                P.add("pool", lambda e: e.memset(dta["dtx"][:], 0.0), writes=[dk("dtx")])
                pD, pDk = next_pA()
                for (c, col0, T, tok0) in chunks:
                    mm_group(pD[:T, c * 32:(c + 1) * 32], pDk, [(nT[:, k, col0:col0 + T], Wdt[:, k, :]) for k in range(8)], [nTk, "Wdt"])
                    P.add("dve", lambda e, c=c, T=T: e.tensor_tensor(out=dta["dtx"][:T, c * 32:(c + 1) * 32], in0=pD[:T, c * 32:(c + 1) * 32], in1=dtb[:T, :], op=ALU.add),
                          reads=[pDk, "dtb"], writes=[dk("dtx")])
                chk("S0b1_%d" % hf)
                P.add("dve", lambda e: e.tensor_scalar(out=dta["mx"][:], in0=dta["dtx"][:], scalar1=0.0, scalar2=None, op0=ALU.max), reads=[dk("dtx")], writes=[dk("mx")])
                P.add("dve", lambda e: e.tensor_scalar(out=dta["tA"][:], in0=dta["dtx"][:], scalar1=0.0, scalar2=None, op0=ALU.min), reads=[dk("dtx")], writes=[dk("tA")])
                P.add("dve", lambda e: e.tensor_tensor(out=dta["tA"][:], in0=dta["tA"][:], in1=dta["mx"][:], op=ALU.subtract), reads=[dk("tA"), dk("mx")], writes=[dk("tA")])
                P.add("act", lambda e: e.activation(out=dta["tA"][:], in_=dta["tA"][:], func=AF.Exp), reads=[dk("tA")], writes=[dk("tA")])
                P.add("act", lambda e: e.activation(out=dta["tA"][:], in_=dta["tA"][:], func=AF.Ln, bias=1.0, scale=1.0), reads=[dk("tA")], writes=[dk("tA")])
                P.add("dve", lambda e: e.tensor_tensor(out=dta["dt"][:], in0=dta["mx"][:], in1=dta["tA"][:], op=ALU.add), reads=[dk("tA"), dk("mx")], writes=[dk("dt")])
                if pre:
                    P.add("dve", lambda e: e.tensor_scalar(out=dta["dt"][:], in0=dta["dt"][:], scalar1=pm[:, blk:blk + 1], scalar2=None, op0=ALU.mult),
                          reads=[dk("dt"), "pm"], writes=[dk("dt")])
                chk("S0b2_%d" % hf)
                P.add("dve", lambda e: e.tensor_tensor(out=A3("a"), in0=A3("dt"), in1=Abc[:, :].unsqueeze(1).to_broadcast([128, NCH, 32]), op=ALU.mult),
                      reads=[dk("dt"), "Abc"], writes=[dk("a")])
                chk("S0b3_%d" % hf)
                pC, pCk = next_pA()
                pE, pEk = next_pA()
                P.add("dve", lambda e: e.tensor_copy(out=a_hi[:], in_=dta["a"][:]), reads=[dk("a")], writes=[H + "a_hi"])
                P.add("dve", lambda e: e.tensor_tensor(out=a_lo[:], in0=dta["a"][:], in1=a_hi[:], op=ALU.subtract), reads=[dk("a"), H + "a_hi"], writes=[H + "a_lo"])
                ak = [H + "a_hi", H + "a_lo"]
                for pi, asrc in enumerate((a_hi, a_lo)):
                    P.add("pe", lambda e, asrc=asrc, pi=pi: e.matmul(pC[:, 0:256], lhsT=trib[:, :], rhs=asrc[:, 0:256], start=(pi == 0), stop=(pi == 1)), reads=["trib"] + ak, writes=[pCk])
                for pi, asrc in enumerate((a_hi, a_lo)):
                    P.add("pe", lambda e, asrc=asrc, pi=pi: e.matmul(pE[:, 0:256], lhsT=onesb[:, :], rhs=asrc[:, 0:256], start=(pi == 0), stop=(pi == 1)), reads=["onesb"] + ak, writes=[pEk])
                ncc = 256
                if has_s:
                    for pi, asrc in enumerate((a_hi, a_lo)):
                        P.add("pe", lambda e, asrc=asrc, pi=pi: e.matmul(pC[:64, 256:288], lhsT=trib[:64, :64], rhs=asrc[:64, 256:288], start=(pi == 0), stop=(pi == 1)), reads=["trib"] + ak, writes=[pCk])
                    for pi, asrc in enumerate((a_hi, a_lo)):
                        P.add("pe", lambda e, asrc=asrc, pi=pi: e.matmul(pE[:, 256:288], lhsT=onesb[:64, :], rhs=asrc[:64, 256:288], start=(pi == 0), stop=(pi == 1)), reads=["onesb"] + ak, writes=[pEk])
                    ncc = 288
                chk("S0b4_%d" % hf)
                P.add("dve", lambda e, ncc=ncc: e.tensor_scalar(out=dta["nac"][:, :ncc], in0=pC[:, :ncc], scalar1=-1.0, scalar2=None, op0=ALU.mult), reads=[pCk], writes=[dk("nac")])
                chk("S0b5_%d" % hf)
                P.add("act", lambda e, ncc=ncc: e.activation(out=dta["e"][:, :ncc], in_=dta["nac"][:, :ncc], func=AF.Exp, scale=-1.0), reads=[dk("nac")], writes=[dk("e")])
                chk("S0b6_%d" % hf)
                P.add("dve", lambda e, ncc=ncc: e.tensor_tensor(out=dta["dtdec"][:, :ncc], in0=pE[:, :ncc], in1=dta["nac"][:, :ncc], op=ALU.add), reads=[pEk, dk("nac")], writes=[dk("dtdec")])
                chk("S0b7_%d" % hf)
                P.add("act", lambda e, ncc=ncc: e.activation(out=dta["dtdec"][:, :ncc], in_=dta["dtdec"][:, :ncc], func=AF.Exp), reads=[dk("dtdec")], writes=[dk("dtdec")])
                P.add("dve", lambda e, ncc=ncc: e.tensor_tensor(out=dta["dtdec"][:, :ncc], in0=dta["dtdec"][:, :ncc], in1=dta["dt"][:, :ncc], op=ALU.mult), reads=[dk("dtdec"), dk("dt")], writes=[dk("dtdec")])
                chk("S0b8_%d" % hf)
                P.add("act", lambda e, ncc=ncc: e.activation(out=dta["dec"][:, :ncc], in_=pE[:, :ncc], func=AF.Exp), reads=[pEk], writes=[dk("dec")])

                if pre:
                    A3t = lambda n: dta[n][:, 0:256].rearrange("p (c h) -> p c h", h=32)
                    P.add("dve", lambda e: e.tensor_copy(out=dta["tA"][:, 0:256], in_=pE[:, 0:256]), reads=[pEk], writes=[dk("tA")])
                    P.add("pool", lambda e: e.memset(dta["mx"][:, 224:256], 0.0), reads=[dk("mx")], writes=[dk("mx")])
                    for c_ in range(6, -1, -1):
                        P.add("dve", lambda e: e.tensor_tensor(out=dta["mx"][:, c_ * 32:(c_ + 1) * 32], in0=dta["mx"][:, (c_ + 1) * 32:(c_ + 2) * 32],
                                                               in1=dta["tA"][:, (c_ + 1) * 32:(c_ + 2) * 32], op=ALU.add),
                              reads=[dk("mx"), dk("tA")], writes=[dk("mx")])
                    P.add("dve", lambda e: e.tensor_tensor(out=dta["e"][:, 0:32], in0=dta["mx"][:, 0:32], in1=dta["tA"][:, 0:32], op=ALU.add),
                          reads=[dk("mx"), dk("tA")], writes=[dk("e")])
                    P.add("act", lambda e: e.activation(out=dta["e"][:, 0:32], in_=dta["e"][:, 0:32], func=AF.Exp), reads=[dk("e")], writes=[dk("e")])
                    P.add("act", lambda e: e.activation(out=dta["mx"][:, 0:256], in_=dta["mx"][:, 0:256], func=AF.Exp), reads=[dk("mx")], writes=[dk("mx")])
                    P.add("dve", lambda e: e.tensor_tensor(out=dta["dtdec"][:, 0:256], in0=dta["dtdec"][:, 0:256], in1=dta["mx"][:, 0:256], op=ALU.mult),
                          reads=[dk("dtdec"), dk("mx")], writes=[dk("dtdec")])
                if "S0b" in _os.environ.get("KSKIP", ""):
                    P.muted = False
                if hf == 1:
                    for nm in ("dt", "a", "nac", "e", "dtdec", "dec"):
                        dbg_out(nm, dta[nm][:, :], [128, NCH * 32], dk(nm))
                chk("S0b_%d" % hf)
                ntile = 3 if pre else 4
                for g in range(8):
                    wb = g % 2
                    Wxk, Wzk = H + "Wx%d" % wb, H + "Wz%d" % wb
                    tiles = [2 * g, 2 * g + 1, 16 + g, 24 + g]
                    hd = slice(g * 4, g * 4 + 4)
                    for j in range(ntile):
                        for s_ in range(4):
                            P.add("dve", lambda e, j=j, s_=s_: e.tensor_scalar(out=dg[:, j, s_, :], in0=ident[:, :], scalar1=cw[:, tiles[j], s_:s_ + 1], scalar2=None, op0=ALU.mult),
                                  reads=["ident", "cw"], writes=[H + "dg%d" % j])

                    def proj(j):
                        tl = tiles[j]
                        wsl = lambda k: Wx[wb][:, k, j * 128:(j + 1) * 128]
                        rk = H + "rawP%d" % j
                        Wxk_j = Wxk + "aabc"[j]
                        rsk = H + "rawS%d" % j
                        pp, ppk = next_pA()
                        mm_group(pp[:, 0:16], ppk, [(wsl(k), nT[:, k, 0:16]) for k in range(8)], [nTk, Wxk_j])
                        P.add("act", lambda e: e.activation(out=rawP[:, j, 0:4], in_=pp[:, 12:16], func=AF.Copy), reads=[ppk], writes=[rk + "h"])
                        for bi in range(2):
                            pp, ppk = next_pA()
                            c0 = 16 + bi * 512
                            mm_group(pp[:, :], ppk, [(wsl(k), nT[:, k, c0:c0 + 512]) for k in range(8)], [nTk, Wxk_j])
                            P.add("act", lambda e: e.activation(out=rawP[:, j, 4 + bi * 512: 4 + (bi + 1) * 512], in_=pp[:, :], func=AF.Copy), reads=[ppk], writes=[rk + "b%d" % bi])
                            if has_s and bi == 1:
                                P.add("dve", lambda e: e.tensor_copy(out=cvPs[:, tl, :], in_=pp[:, 509:512]), reads=[ppk], writes=["cvPs"])
                        if has_s:
                            pp, ppk = next_pA()
                            c0 = 16 + NPH
                            mm_group(pp[:, 0:64], ppk, [(wsl(k), nT[:, k, c0:c0 + 64]) for k in range(8)], [nTk, Wxk_j])
                            P.add("act", lambda e: e.activation(out=rawS[:, j, 4:68], in_=pp[:, 0:64], func=AF.Copy), reads=[ppk], writes=[rsk])
                            P.add("dve", lambda e: e.tensor_copy(out=cvSs[:, tl, :], in_=pp[:, 61:64]), reads=[ppk], writes=["cvSs"])
                            P.add("pool", lambda e: e.tensor_copy(out=rawS[:, j, 0:4], in_=scvb[:, tl, :]), reads=["scvb"], writes=[rsk])

                    def conv(j):
                        tl = tiles[j]
                        rk = H + "rawP%d" % j
                        rsk = H + "rawS%d" % j
                        if j < 2:
                            dst = lambda a, b_: xsT[:, j, a:b_]
                            dkey = H + "xsT%d" % j
                        elif j == 2:
                            dst = lambda a, b_: BT[:, a:b_]
                            dkey = H + "BT"
                        else:
                            dst = lambda a, b_: CT[:, a:b_]
                            dkey = H + "CT"
                        for bi in range(2):
                            pp, ppk = next_pA()
                            rks = [rk + "h", rk + "b0"] if bi == 0 else [rk + "b0", rk + "b1"]
                            mm_group(pp[:, :], ppk, [(dg[:, j, s_, :], rawP[:, j, bi * 512 + s_ + 1: bi * 512 + s_ + 513]) for s_ in range(4)], [H + "dg%d" % j] + rks)
                            P.add("act", lambda e: e.activation(out=dst(bi * 512, (bi + 1) * 512), in_=pp[:, :], func=AF.Silu, bias=cb[:, tl:tl + 1], scale=1.0),
                                  reads=[ppk, "cb"], writes=[dkey + "_%d" % bi])
                        if has_s:
                            pp, ppk = next_pA()
                            mm_group(pp[:, 0:64], ppk, [(dg[:, j, s_, :], rawS[:, j, s_ + 1: s_ + 65]) for s_ in range(4)], [H + "dg%d" % j, rsk])
                            P.add("act", lambda e: e.activation(out=dst(NPH, NPH + 64), in_=pp[:, 0:64], func=AF.Silu, bias=cb[:, tl:tl + 1], scale=1.0),
                                  reads=[ppk, "cb"], writes=[dkey + "_2"])

                    proj(0)
                    for j in range(1, ntile):
                        proj(j)
                        conv(j - 1)
                    conv(ntile - 1)
                    if g + 1 < 8:
                        issue_w(g + 1)
                    chk("S1a_%d_%d" % (hf, g))

                    def ckeys(base, tok0):
                        return [base + "_%d" % (2 if tok0 >= NPH else tok0 // 512)]
                    for (c, col0, T, tok0) in chunks:
                        ptb = pTb if c % 2 else pT
                        pk = list(pLb_keys) if c % 2 else ["pT"]
                        for jj, (src, skey) in enumerate(((xsT[:, 0, tok0:tok0 + T], H + "xsT0"), (xsT[:, 1, tok0:tok0 + T], H + "xsT1"), (BT[:, tok0:tok0 + T], H + "BT"))):
                            P.add("pe", lambda e, src=src, jj=jj, T=T: e.transpose(out=ptb[:T, jj * 128:(jj + 1) * 128], in_=src, identity=ident[:, :]),
                                  reads=ckeys(skey, tok0) + ["ident"], writes=pk)
                        P.add("dve", lambda e, c=c, T=T: e.tensor_copy(out=xBt[:T, c, :], in_=ptb[:T, 0:384]), reads=pk, writes=[H + "xBt%d" % c])
                    hg = hcar[:, g * 256:(g + 1) * 256]
                    hgk = "hcar%d" % g
                    if pre:
                        for (c, col0, T, tok0) in chunks:
                            xk_ = H + "xBt%d" % c
                            xddb = xdd[c % 2]
                            xddk = H + "xdd%d" % (c % 2)
                            P.add("dve", lambda e: e.tensor_tensor(out=xddb[:T, :].rearrange("t (h p) -> t h p", h=4), in0=xBt[:T, c, 0:256].rearrange("t (h p) -> t h p", h=4),
                                                                   in1=A3("dtdec")[:T, c, hd].unsqueeze(2).to_broadcast([T, 4, 64]), op=ALU.mult),
                                  reads=[xk_, dk("dtdec")], writes=[xddk])
                            P.add("pe", lambda e: e.matmul(pZS[:, 256:512], lhsT=xBt[:T, c, 256:384], rhs=xddb[:T, :], start=(c == 0), stop=(c == 7)),
                                  reads=[xk_, xddk], writes=["pS"])
                        P.add("pool", lambda e: e.tensor_tensor(out=tmpH[:, :].rearrange("n (h p) -> n h p", h=4), in0=hg.rearrange("n (h p) -> n h p", h=4),
                                                                in1=dta["e"][:, hd].unsqueeze(2).to_broadcast([128, 4, 64]), op=ALU.mult),
                              reads=[hgk, dk("e")], writes=[H + "tmpH"])
                        P.add("dve", lambda e: e.tensor_tensor(out=hg, in0=pZS[:, 256:512], in1=tmpH[:, :], op=ALU.add), reads=["pS", H + "tmpH"], writes=[hgk])
                        continue
                    def state_ops(c, col0, T, tok0):
                        samp = (c == 8)
                        if samp:
                            P.dma(lambda e: e.dma_start(out=hS[:, :], in_=sst[:, g * 256:(g + 1) * 256]), writes=[H + "hS"])
                            hcur, hk = hS[:, :], H + "hS"
                        else:
                            hcur, hk = hg, hgk
                        xk_ = H + "xBt%d" % c
                        if not pre:
                            P.add("pool", lambda e, c=c, hcur=hcur: e.tensor_copy(out=hTa[:, c, :], in_=hcur), reads=[hk], writes=[H + "hTa%d" % c])
                        xddb = xdd[c % 2]
                        xddk = H + "xdd%d" % (c % 2)
                        P.add("dve", lambda e, c=c, T=T: e.tensor_tensor(out=xddb[:T, :].rearrange("t (h p) -> t h p", h=4), in0=xBt[:T, c, 0:256].rearrange("t (h p) -> t h p", h=4),
                                                                         in1=A3("dtdec")[:T, c, hd].unsqueeze(2).to_broadcast([T, 4, 64]), op=ALU.mult),
                              reads=[xk_, dk("dtdec")], writes=[xddk])
                        P.add("pe", lambda e, c=c, T=T: e.matmul(pSm, lhsT=xBt[:T, c, 256:384], rhs=xddb[:T, :], start=True, stop=True),
                              reads=[xk_, xddk], writes=["pSm"])
                        P.add("pool", lambda e, c=c, hcur=hcur: e.tensor_tensor(out=tmpH[:, :].rearrange("n (h p) -> n h p", h=4), in0=hcur.rearrange("n (h p) -> n h p", h=4),
                                                                                in1=A3("dec")[:, c, hd].unsqueeze(2).to_broadcast([128, 4, 64]), op=ALU.mult),
                              reads=[hk, dk("dec")], writes=[H + "tmpH"])
                        P.add("dve", lambda e, hcur=hcur: e.tensor_tensor(out=hcur, in0=pSm, in1=tmpH[:, :], op=ALU.add), reads=["pSm", H + "tmpH"], writes=[hk])
                        if samp:
                            P.dma(lambda e: e.dma_start(out=hSo[:, g * 256:(g + 1) * 256], in_=hS[:, :]), reads=[hk])
                        elif has_s and c == 7:
                            P.dma(lambda e: e.dma_start(out=hPo[:, g * 256:(g + 1) * 256], in_=hg), reads=[hk])
                    chk("S1b_%d_%d" % (hf, g))
                    if pre:
                        continue
                    pend_sq = [None]

                    def emit_sq(c, T):
                        P.add("act", lambda e: e.activation(out=junk[:T, :], in_=yza[:T, c, :], func=AF.Square, scale=0.5, accum_out=ssqa[:T, c:c + 1]),
                              reads=[H + "yza%d" % c], writes=[H + "junk", H + "ssqa"])

                    for (c, col0, T, tok0) in chunks:
                        state_ops(c, col0, T, tok0)
                        xk_ = H + "xBt%d" % c
                        mm_group(pCBt[:T, 0:256], "pZ", [(nT[:, k, col0:col0 + T], Wz[wb][:, k, :]) for k in range(8)], [nTk, Wzk])
                        P.add("act", lambda e, T=T: e.activation(out=szs[:T, :], in_=pCBt[:T, 0:256], func=AF.Tanh, scale=0.5), reads=["pZ"], writes=[H + "sz"])
                        P.add("dve", lambda e, T=T: e.scalar_tensor_tensor(out=szs[:T, :], in0=szs[:T, :], scalar=1.0, in1=pCBt[:T, 0:256], op0=ALU.add, op1=ALU.mult),
                              reads=["pZ", H + "sz"], writes=[H + "sz"])
                        P.add("pe", lambda e, T=T, tok0=tok0: e.matmul(pZS[:T, 256:256 + T], lhsT=BT[:, tok0:tok0 + T], rhs=CT[:, tok0:tok0 + T], start=True, stop=True),
                              reads=ckeys(H + "BT", tok0) + ckeys(H + "CT", tok0), writes=["pCB"])
                        P.add("pe", lambda e, c=c, T=T, tok0=tok0: e.matmul(pZS[:T, 0:256], lhsT=CT[:, tok0:tok0 + T], rhs=hTa[:, c, :], start=True, stop=True),
                              reads=ckeys(H + "CT", tok0) + [H + "hTa%d" % c], writes=["pZo"])
                        for h in range(4):
                            for pi, asrc in enumerate((a_hi, a_lo)):
                                P.add("pe", lambda e, c=c, T=T, h=h, asrc=asrc, pi=pi: e.matmul(pL[:T, h, :T], lhsT=asrc[:T, c * 32 + g * 4 + h: c * 32 + g * 4 + h + 1].to_broadcast([T, T]),
                                                                                        rhs=trib[:T, :T], start=(pi == 0), stop=(pi == 1)),
                                      reads=[H + "a_hi", H + "a_lo", "trib"], writes=["pL%da" % h])
                        P.add("dve", lambda e, T=T: e.tensor_tensor(out=CBm[0][:T, :T], in0=pZS[:T, 256:256 + T], in1=tri[:T, :T], op=ALU.mult), reads=["pCB", "tri"], writes=[H + "CBm"])
                        P.add("pool", lambda e, c=c, T=T: e.tensor_tensor(out=xdt[0][:T, :].rearrange("t (h p) -> t h p", h=4), in0=xBt[:T, c, 0:256].rearrange("t (h p) -> t h p", h=4),
                                                                          in1=A3("dt")[:T, c, hd].unsqueeze(2).to_broadcast([T, 4, 64]), op=ALU.mult),
                              reads=[xk_, dk("dt")], writes=[H + "xdt"])
                        P.add("pool", lambda e, c=c, T=T: e.tensor_tensor(out=xd[0][:T, :].rearrange("t (h p) -> t h p", h=4), in0=xBt[:T, c, 0:256].rearrange("t (h p) -> t h p", h=4),
                                                                          in1=dsk[:T, hd].unsqueeze(2).to_broadcast([T, 4, 64]), op=ALU.mult),
                              reads=[xk_, "dsk"], writes=[H + "xd"])
                        for h in range(4):
                            P.add("act", lambda e, c=c, T=T, h=h: e.activation(out=Eh[0][h][:T, :T], in_=pL[:T, h, :T], func=AF.Exp,
                                                                               bias=dta["nac"][:T, c * 32 + g * 4 + h: c * 32 + g * 4 + h + 1], scale=1.0),
                                  reads=["pL%da" % h, dk("nac")], writes=[H + "E%d" % h])
                            P.add("dve", lambda e, T=T, h=h: e.scalar_tensor_tensor(out=MT[0][h][:T, :T], in0=Eh[0][h][:T, :T], scalar=1.0, in1=CBm[0][:T, :T], op0=ALU.min, op1=ALU.mult),
                                  reads=[H + "E%d" % h, H + "CBm"], writes=[H + "MT%d" % h])
                        P.add("pe", lambda e, T=T: e.matmul(pY[:T, 0:256], lhsT=ident[:T, :T], rhs=xd[0][:T, :], start=True, stop=False), reads=["ident", H + "xd"], writes=["pY"])
                        for h in range(4):
                            P.add("pe", lambda e, T=T, h=h: e.matmul(pY[:T, h * 64:(h + 1) * 64], lhsT=MT[0][h][:T, :T], rhs=xdt[0][:T, h * 64:(h + 1) * 64], start=False, stop=(h == 3)),
                                  reads=[H + "MT%d" % h, H + "xdt"], writes=["pY"])
                        P.add("dve", lambda e, c=c, T=T: e.tensor_tensor(out=t1[0][:T, :].rearrange("t (h p) -> t h p", h=4), in0=pZS[:T, 0:256].rearrange("t (h p) -> t h p", h=4),
                                                                         in1=A3("e")[:T, c, hd].unsqueeze(2).to_broadcast([T, 4, 64]), op=ALU.mult),
                              reads=["pZo", dk("e")], writes=[H + "t1"])
                        P.add("dve", lambda e, T=T: e.tensor_tensor(out=t3[:T, :], in0=pY[:T, 0:256], in1=t1[0][:T, :], op=ALU.add), reads=["pY", H + "t1"], writes=[H + "t3"])
                        P.add("pool", lambda e, c=c, T=T: e.tensor_tensor(out=yza[:T, c, :], in0=t3[:T, :], in1=szs[:T, :], op=ALU.mult), reads=[H + "t3", H + "sz"], writes=[H + "yza%d" % c])
                        if pend_sq[0] is not None:
                            emit_sq(*pend_sq[0])
                        pend_sq[0] = (c, T)
                    emit_sq(*pend_sq[0])
                    if hf == 1 and g == 0:
                        dbg_out("yza8", yza[:64, 8, :], [64, 256], H + "yza8")
                        dbg_out("yza0", yza[:, 0, :], [128, 256], H + "yza0")
                        dbg_out("ssqa", ssqa[:, :], [128, NCH], H + "ssqa")
                    nchk = len(chunks)
                    if has_s:
                        P.add("pool", lambda e: e.memset(ssqa[64:128, 8:9], 1.0), reads=[H + "ssqa"], writes=[H + "ssqa"])
                    rstd_from_ssq(ssqa[:, 0:nchk], H + "ssqa", 1.0 / 256)
                    for (c, col0, T, tok0) in chunks:
                        ynb = yn[c % 2]
                        ynk = H + "yn%d" % (c % 2)
                        P.add("dve", lambda e, c=c, T=T: e.tensor_scalar(out=ynb[:T, :], in0=yza[:T, c, :], scalar1=ssqa[:T, c:c + 1], scalar2=0.5, op0=ALU.mult, op1=ALU.mult),
                              reads=[H + "yza%d" % c, H + "ssqa"], writes=[ynk])
                        ytb = pTb if c % 2 else pT
                        yb0 = 0 if c % 2 else 512
                        yks = list(pLb_keys) if c % 2 else ["pT2"]
                        for jj in range(2):
                            P.add("pe", lambda e, T=T, jj=jj: e.transpose(out=ytb[:, yb0 + jj * 128: yb0 + jj * 128 + T], in_=ynb[:T, jj * 128:(jj + 1) * 128], identity=ident[:T, :T]),
                                  reads=[ynk, "ident"], writes=yks)
                        for jj in range(2):
                            P.add("act", lambda e, T=T, jj=jj, tok0=tok0: e.activation(out=ynT[:, 2 * g + jj, tok0:tok0 + T], in_=ytb[:, yb0 + jj * 128: yb0 + jj * 128 + T], func=AF.Copy,
                                                                                      scale=sng[:, 2 * g + jj: 2 * g + jj + 1]),
                                  reads=yks + ["sng"], writes=[H + "ynT"])
            if pre:
                if blk == NPRE - 1:
                    sa.close()
                    P.barrier()
                continue
            sa.close()
            chk("S1_%d" % hf)
            P.barrier()

            pa_ext[0] = True
            with contextlib.ExitStack() as sB:
                sbb = lambda n, s, dt=F32: sb(n + "_%d" % hf, s, dt, stack=sB)
                Wbs = [sbb("Wbs%d" % i, [128, 16, 128], BF16) for i in range(2)]
                Wg1 = [sbb("Wg1%d" % i, [128, 8, 128], BF16) for i in range(2)]
                gsb = [sbb("gsb%d" % i, [128, 512]) for i in range(2)]
                ctr = 0
                pf = Prefetch()
                pfi = {}
                for m in range(8):
                    wb = m % 2
                    pfi[("g1", m)] = pf.add(Wg1[wb][:, :, :], w_in[:, C_GS + m * 128: C_GS + (m + 1) * 128], H + "Wg1%d" % wb)
                    pfi[("bs", m)] = pf.add(Wbs[wb][:, :, :], w_bs[:, m * 128:(m + 1) * 128], H + "Wbs%d" % wb)
                for m in range(8):
                    wb = m % 2
                    pf.need(pfi[("bs", m)])
                    for (c0, N, tok0) in fblocks:
                        gi = ctr % 2
                        ctr += 1
                        pp, ppk = next_pA()
                        mm_group(pp[:, :N], ppk, [(Wg1[wb][:, k, :], nT[:, k, c0:c0 + N]) for k in range(8)], [nTk, H + "Wg1%d" % wb])
                        P.add("act", lambda e, pp=pp, N=N, gi=gi, m=m: e.activation(out=gsb[gi][:, :N], in_=pp[:, :N], func=AF.Sigmoid, bias=gb[:, m:m + 1], scale=1.0),
                              reads=[ppk, "gb"], writes=[H + "gsb%d" % gi])
                        pp2, ppk2 = next_pA()
                        mm_group(pp2[:, :N], ppk2, [(Wbs[wb][:, kt, :], ynT[:, kt, tok0:tok0 + N]) for kt in range(16)], [H + "ynT", H + "Wbs%d" % wb])
                        P.add("dve", lambda e, pp2=pp2, N=N, gi=gi, m=m, tok0=tok0: e.tensor_tensor(out=MG[:, m, tok0:tok0 + N], in0=pp2[:, :N], in1=gsb[gi][:, :N], op=ALU.mult),
                              reads=[ppk2, H + "gsb%d" % gi], writes=[H + "MG%d" % m])
            chk("S2_%d" % hf)
            P.barrier()
            ys.close()

            with contextlib.ExitStack() as sC:
                sbc = lambda n, s, dt=F32: sb(n + "_%d" % hf, s, dt, stack=sC)
                PGT = sbc("PGT", [128, 8, NPH + 64], BF16)
                Wu = [sbc("Wu%d" % i, [128, 8, 128], BF16) for i in range(2)]
                Wpg = [sbc("Wpg%d" % i, [128, 8, 128], BF16) for i in range(2)]
                Wgp = [sbc("Wgp%d" % i, [128, 8, 128], BF16) for i in range(2)]
                Wbp = [sbc("Wbp%d" % i, [128, 8, 128], BF16) for i in range(2)]
                PW = [sbc("PW%d" % i, [128, 2, 256], BF16) for i in range(2)]
                Wo = sbc("Wo", [128, 8, D], BF16)
                LP = 16 + NPH
                ubP = sbc("ubP", [128, LP])
                ubS = sbc("ubS", [128, 15 + 64])
                stP = [sbc("stP%d" % i, [128, LP]) for i in range(2)]
                stS = [sbc("stS%d" % i, [128, 15 + 64]) for i in range(2)]
                dT = sbc("dT", [128, 2, NPH + 64], BF16)
                spg = [sbc("spg%d" % i, [128, 512]) for i in range(2)]
                tmpc = [sbc("tmpc%d" % i, [128, 512]) for i in range(2)]
                xr = [sbc("xr%d" % i, [128, D]) for i in range(2)]
                yo = [sbc("yo%d" % i, [128, D]) for i in range(2)]
                junkc = sbc("junkc", [128, D], BF16)
                ssq4 = [sbc("ssq4_%d" % i, [128, 1]) for i in range(2)]
                wload(Wo[:, :, :], w_out[:, :], H + "Wo")
                ctr = 0
                pf = Prefetch()
                pfi = {}
                for gi_ in range(4):
                    pfi[("pw", gi_)] = pf.add(PW[gi_ % 2][:, :, :], pool_w[gi_], H + "PW%d" % (gi_ % 2))
                    for j in range(2):
                        ut = gi_ * 2 + j
                        pfi[("u", ut)] = pf.add(Wu[ut % 2][:, :, :], w_in[:, C_U + ut * 128: C_U + (ut + 1) * 128], H + "Wu%d" % (ut % 2))
                    for mo in range(2):
                        ot = gi_ * 2 + mo
                        pfi[("pg", ot)] = pf.add(Wpg[ot % 2][:, :, :], w_in[:, C_PG + ot * 128: C_PG + (ot + 1) * 128], H + "Wpg%d" % (ot % 2))
                for m in range(8):
                    pfi[("gp", m)] = pf.add(Wgp[m % 2][:, :, :], w_in[:, C_GP + m * 128: C_GP + (m + 1) * 128], H + "Wgp%d" % (m % 2))
                    pfi[("bp", m)] = pf.add(Wbp[m % 2][:, :, :], w_bp[:, m * 128:(m + 1) * 128], H + "Wbp%d" % (m % 2))
                for gi_ in range(4):
                    w = POOL_W[gi_]
                    pwb = gi_ % 2
                    pf.need(pfi[("pw", gi_)])
                    for j in range(2):
                        ut = gi_ * 2 + j
                        ub_ = ut % 2
                        pf.need(pfi[("u", ut)])
                        pp, ppk = next_pA()
                        mm_group(pp[:, 0:16], ppk, [(Wu[ub_][:, k, :], nT[:, k, 0:16]) for k in range(8)], [nTk, H + "Wu%d" % ub_])
                        P.add("act", lambda e, pp=pp: e.activation(out=ubP[:, 0:16], in_=pp[:, 0:16], func=AF.Copy), reads=[ppk], writes=[H + "ubP"])
                        for bi in range(2):
                            pp, ppk = next_pA()
                            c0 = 16 + bi * 512
                            mm_group(pp[:, :], ppk, [(Wu[ub_][:, k, :], nT[:, k, c0:c0 + 512]) for k in range(8)], [nTk, H + "Wu%d" % ub_])
                            P.add("act", lambda e, pp=pp, c0=c0: e.activation(out=ubP[:, c0:c0 + 512], in_=pp[:, :], func=AF.Copy), reads=[ppk], writes=[H + "ubP"])
                        seqs = [(ubP, stP, LP, 16, 0, H + "ubP", H + "stP")]
                        if has_s:
                            pp, ppk = next_pA()
                            c0 = 16 + NPH
                            mm_group(pp[:, 0:64], ppk, [(Wu[ub_][:, k, :], nT[:, k, c0:c0 + 64]) for k in range(8)], [nTk, H + "Wu%d" % ub_])
                            P.add("act", lambda e, pp=pp: e.activation(out=ubS[:, 15:79], in_=pp[:, 0:64], func=AF.Copy), reads=[ppk], writes=[H + "ubS"])
                            P.add("pool", lambda e, ut=ut: e.tensor_copy(out=ubS[:, 0:15], in_=splf[:, ut, :]), reads=["splf"], writes=[H + "ubS"])
                            P.add("dve", lambda e, ut=ut: e.tensor_copy(out=plPs[:, ut, :], in_=ubP[:, LP - 15:LP]), reads=[H + "ubP"], writes=["plPs"])
                            P.add("dve", lambda e, ut=ut: e.tensor_copy(out=plSs[:, ut, :], in_=ubS[:, 64:79]), reads=[H + "ubS"], writes=["plSs"])
                            seqs.append((ubS, stS, 79, 15, NPH, H + "ubS", H + "stS"))
                        for (ub, st, L, v0, tk0, ubk, stk) in seqs:
                            cur, curk = ub, ubk
                            sh = 1
                            lo = 0
                            si = 0
                            while sh < w:
                                nlo = lo + sh
                                dstb, dstk = st[si], stk + "%d" % si
                                eng = "pool" if si == 0 else "dve"
                                P.add(eng, lambda e, cur=cur, dstb=dstb, nlo=nlo, sh=sh, L=L: e.tensor_tensor(out=dstb[:, nlo:L], in0=cur[:, nlo:L], in1=cur[:, nlo - sh:L - sh], op=ALU.add),
                                      reads=[curk], writes=[dstk])
                                cur, curk = dstb, dstk
                                lo = nlo
                                sh *= 2
                                si ^= 1
                            nv = L - v0
                            P.add("dve", lambda e, cur=cur, ub=ub, v0=v0, L=L, j=j, tk0=tk0, nv=nv, w=w: e.scalar_tensor_tensor(
                                out=dT[:, j, tk0:tk0 + nv], in0=cur[:, v0:L], scalar=1.0 / w, in1=ub[:, v0:L], op0=ALU.mult, op1=ALU.subtract),
                                reads=[curk, ubk], writes=[H + "dT%d" % j])
                            if hf == 0 and tk0 == 0:
                                P.add("dve", lambda e, cur=cur, gi_=gi_: e.tensor_tensor(out=tmpc[0][:, 0:16], in0=cur[:, 16:32], in1=pinv[:, gi_ * 16:(gi_ + 1) * 16], op=ALU.mult),
                                      reads=[curk, "pinv"], writes=[H + "tmpc0"])
                                P.add("dve", lambda e, ub=ub, j=j: e.tensor_tensor(out=dT[:, j, 0:16], in0=tmpc[0][:, 0:16], in1=ub[:, 16:32], op=ALU.subtract),
                                      reads=[H + "tmpc0", ubk], writes=[H + "dT%d" % j])
                    for mo in range(2):
                        ot = gi_ * 2 + mo
                        wb = ot % 2
                        pf.need(pfi[("pg", ot)])
                        for (c0, N, tok0) in fblocks:
                            gi = ctr % 2
                            ctr += 1
                            pp, ppk = next_pA()
                            mm_group(pp[:, :N], ppk, [(Wpg[wb][:, k, :], nT[:, k, c0:c0 + N]) for k in range(8)], [nTk, H + "Wpg%d" % wb])
                            P.add("act", lambda e, pp=pp, N=N, gi=gi: e.activation(out=spg[gi][:, :N], in_=pp[:, :N], func=AF.Silu), reads=[ppk], writes=[H + "spg%d" % gi])
                            pp2, ppk2 = next_pA()
                            mm_group(pp2[:, :N], ppk2, [(PW[pwb][:, kt, mo * 128:(mo + 1) * 128], dT[:, kt, tok0:tok0 + N]) for kt in range(2)], [H + "dT0", H + "dT1", H + "PW%d" % pwb])
                            P.add("dve", lambda e, pp2=pp2, N=N, gi=gi, ot=ot, tok0=tok0: e.scalar_tensor_tensor(out=PGT[:, ot, tok0:tok0 + N], in0=pp2[:, :N], scalar=psc[:, ot:ot + 1],
                                                                                                               in1=spg[gi][:, :N], op0=ALU.mult, op1=ALU.mult),
                                  reads=[ppk2, H + "spg%d" % gi, "psc"], writes=[H + "PGT"])
                for m in range(8):
                    wb = m % 2
                    pf.need(pfi[("bp", m)])
                    for (c0, N, tok0) in fblocks:
                        gi = ctr % 2
                        ctr += 1
                        pp, ppk = next_pA()
                        mm_group(pp[:, :N], ppk, [(Wgp[wb][:, k, :], nT[:, k, c0:c0 + N]) for k in range(8)], [nTk, H + "Wgp%d" % wb])
                        P.add("act", lambda e, pp=pp, N=N, gi=gi, m=m: e.activation(out=spg[gi][:, :N], in_=pp[:, :N], func=AF.Sigmoid, bias=gb[:, 8 + m:9 + m], scale=1.0),
                              reads=[ppk, "gb"], writes=[H + "spg%d" % gi])
                        pp2, ppk2 = next_pA()
                        mm_group(pp2[:, :N], ppk2, [(Wbp[wb][:, k, :], PGT[:, k, tok0:tok0 + N]) for k in range(8)], [H + "PGT", H + "Wbp%d" % wb])
                        P.add("dve", lambda e, pp2=pp2, N=N, gi=gi: e.tensor_tensor(out=tmpc[gi][:, :N], in0=pp2[:, :N], in1=spg[gi][:, :N], op=ALU.mult),
                              reads=[ppk2, H + "spg%d" % gi], writes=[H + "tmpc%d" % gi])
                        P.add("pool", lambda e, N=N, gi=gi, m=m, tok0=tok0: e.tensor_tensor(out=MG[:, m, tok0:tok0 + N], in0=MG[:, m, tok0:tok0 + N], in1=tmpc[gi][:, :N], op=ALU.add),
                              reads=[H + "tmpc%d" % gi, H + "MG%d" % m], writes=[H + "MG%d" % m])
                chk("S3_%d" % hf)
                mgk = [H + "MG%d" % m for m in range(8)]
                for (c, col0, T, tok0) in chunks:
                    b = c % 2
                    samp = (c == 8)
                    if samp:
                        P.dma(lambda e, b=b: e.dma_start(out=xr[b][:64, :], in_=xin[2064:2128, :]), writes=[H + "xr%d" % b])
                    else:
                        r0 = 16 + hf * NPH + c * 128
                        P.dma(lambda e, b=b, r0=r0: e.dma_start(out=xr[b][:, :], in_=xin[r0:r0 + 128, :]), writes=[H + "xr%d" % b])
                    for half in range(2):
                        pp, ppk = next_pA()
                        mm_group(pp[:T, :], ppk, [(MG[:, k, tok0:tok0 + T], Wo[:, k, half * 512:(half + 1) * 512]) for k in range(8)], mgk + [H + "Wo"])
                        P.add("dve", lambda e, pp=pp, T=T, b=b, half=half: e.tensor_tensor(out=xr[b][:T, half * 512:(half + 1) * 512], in0=pp[:T, :], in1=xr[b][:T, half * 512:(half + 1) * 512], op=ALU.add),
                              reads=[ppk, H + "xr%d" % b], writes=[H + "xr%d" % b])
                    P.add("act", lambda e, T=T, b=b: e.activation(out=junkc[:T, :], in_=xr[b][:T, :], func=AF.Square, accum_out=ssq4[b][:T, :]),
                          reads=[H + "xr%d" % b], writes=[H + "junkc", H + "ssq4%d" % b])
                    rstd_from_ssq(ssq4[b][:T, :], H + "ssq4%d" % b, 1.0 / D)
                    P.add("dve", lambda e, T=T, b=b: e.scalar_tensor_tensor(out=yo[b][:T, :], in0=xr[b][:T, :], scalar=ssq4[b][:T, 0:1], in1=fg[:T, :], op0=ALU.mult, op1=ALU.mult),
                          reads=[H + "xr%d" % b, H + "ssq4%d" % b, "fg"], writes=[H + "yo%d" % b])
                    if samp:
                        P.dma(lambda e, b=b: e.dma_start(out=yS[:, :], in_=yo[b][:64, :]), reads=[H + "yo%d" % b])
                    else:
                        r0 = hf * NPH + c * 128
                        P.dma(lambda e, b=b, r0=r0: e.dma_start(out=yP[r0:r0 + 128, :], in_=yo[b][:, :]), reads=[H + "yo%d" % b])
            P.barrier()
        except _Stop:
            pass
        P.muted = False
        P.dma(lambda e: e.dma_start(out=cvP, in_=cvPs[:]), reads=["cvPs"])
        P.dma(lambda e: e.dma_start(out=cvS, in_=cvSs[:]), reads=["cvSs"])
        P.dma(lambda e: e.dma_start(out=plP, in_=plPs[:]), reads=["plPs"])
        P.dma(lambda e: e.dma_start(out=plS, in_=plSs[:]), reads=["plSs"])
        P.emit(nc)
    P.stats["dbg"] = dbg_list
    return nc, P.stats


_CACHE = {}
STOP = None


def kernel(x_prompt, x_sample, state_ssd, state_conv, state_pool, norm_g, w_in, conv_w, conv_b,
           dt_bias, a_log, d_skip, ssd_norm_g, w_branch_ssd, pool_w, pool_scale, w_branch_pool,
           gate_b, w_out, final_g):
    f = lambda a: np.ascontiguousarray(np.asarray(a, dtype=np.float32))
    x_prompt, x_sample = f(x_prompt), f(x_sample)
    if "nc" not in _CACHE:
        _CACHE["nc"] = build_program(STOP)
    nc, stats = _CACHE["nc"]
    shared = {
        "w_in": f(w_in[0]), "w_bs": f(w_branch_ssd[0]), "w_bp": f(w_branch_pool[0]), "w_out": f(w_out[0]),
        "pool_w": f(pool_w[0]),
        "ng": f(np.asarray(norm_g[0]).reshape(8, 128).T),
        "cw": f(np.asarray(conv_w[0]).reshape(4, 32, 128).transpose(2, 1, 0)),
        "cb": f(np.asarray(conv_b[0]).reshape(32, 128).T),
        "sng": f(np.asarray(ssd_norm_g[0]).reshape(16, 128).T),
        "psc": f(np.asarray(pool_scale[0]).reshape(8, 128).T),
        "gb": f(np.asarray(gate_b[0]).reshape(16, 128).T),
        "dtb": f(np.asarray(dt_bias[0]).reshape(1, 32)), "alog": f(np.asarray(a_log[0]).reshape(1, 32)),
        "dsk": f(np.asarray(d_skip[0]).reshape(1, 32)), "fg": f(np.asarray(final_g).reshape(1, 1024)),
    }
    in_maps = []
    for q in range(8):
        seq, part = q // 4, q % 4
        s0 = part * 2048
        halo = x_prompt[seq, s0 - 16:s0] if part > 0 else np.zeros((16, D), np.float32)
        xin = np.concatenate([halo, x_prompt[seq, s0:s0 + 2048], x_sample[q]], axis=0)
        pinv = np.ones((4, 16), np.float32)
        for gi, w in enumerate(POOL_W):
            for t in range(16):
                pinv[gi, t] = 1.0 / min(w, s0 + t + 1)
        m = dict(shared)
        m["xin"] = f(xin)
        m["sst"] = f(np.asarray(state_ssd[0, q]).reshape(2048, 128).T)
        scv3 = np.asarray(state_conv[0, q]).reshape(3, 32, 128).transpose(2, 1, 0)
        m["scv"] = f(np.concatenate([np.zeros((128, 32, 1), np.float32), scv3], axis=2))
        m["spl"] = f(np.asarray(state_pool[0, q]).reshape(15, 8, 128).transpose(2, 1, 0))
        m["pinv"] = f(pinv.reshape(1, 64))
        npre_rows = NPRE * NPH + 16
        xp = np.zeros((npre_rows, D), np.float32)
        have = min(s0, npre_rows)
        if have > 0:
            xp[npre_rows - have:] = x_prompt[seq, s0 - have:s0]
        pmv = np.zeros((1, 8), np.float32)
        for k in range(NPRE):
            if s0 - NPRE * NPH + k * NPH >= 0:
                pmv[0, k] = 1.0
        m["xpre"] = xp
        m["pm"] = pmv
        in_maps.append(m)
    res = run_bass_kernel_spmd(nc, in_maps, core_ids=list(range(8)))
    R = res.results
    global LAST_RESULTS
    LAST_RESULTS = R
    y_prompt = np.zeros((2, 8192, D), np.float32)
    y_sample = np.zeros((8, 64, D), np.float32)
    ssd_p = np.zeros((1, 2, 32, 64, 128), np.float32)
    ssd_s = np.zeros((1, 8, 32, 64, 128), np.float32)
    conv_p = np.zeros((1, 2, 3, 4096), np.float32)
    conv_s = np.zeros((1, 8, 3, 4096), np.float32)
    pool_p = np.zeros((1, 2, 15, 1024), np.float32)
    pool_s = np.zeros((1, 8, 15, 1024), np.float32)
    for q in range(8):
        seq, part = q // 4, q % 4
        r = R[q]
        y_prompt[seq, part * 2048:(part + 1) * 2048] = r["yP"]
        y_sample[q] = r["yS"]
        ssd_s[0, q] = np.asarray(r["hSo"]).T.reshape(32, 64, 128)
        conv_s[0, q] = np.asarray(r["cvS"]).transpose(2, 1, 0).reshape(3, 4096)
        pool_s[0, q] = np.asarray(r["plS"]).transpose(2, 1, 0).reshape(15, 1024)
        if part == 3:
            ssd_p[0, seq] = np.asarray(r["hPo"]).T.reshape(32, 64, 128)
            conv_p[0, seq] = np.asarray(r["cvP"]).transpose(2, 1, 0).reshape(3, 4096)
            pool_p[0, seq] = np.asarray(r["plP"]).transpose(2, 1, 0).reshape(15, 1024)
    return (y_prompt, y_sample, ssd_p, ssd_s, conv_p, conv_s, pool_p, pool_s)
```

```python
import contextlib
import numpy as np
import concourse.bass as bass
import concourse.mybir as mybir
from concourse.bass_utils import run_bass_kernel_spmd

F32 = mybir.dt.float32
BF16 = mybir.dt.bfloat16
AF = mybir.ActivationFunctionType
ALU = mybir.AluOpType

ENGS = ("pe", "act", "dve", "pool", "sp")
N_DMA_SEMS = 12
EPS = 1e-6


class Op:
    __slots__ = ("eng", "fn", "reads", "writes", "deps", "sig", "sig_idx", "dma_slot", "dma_cnt",
                 "is_dma", "idx", "barrier", "dma_snapshot")

    def __init__(self, eng, fn, reads, writes, is_dma, barrier=False):
        self.eng = eng
        self.fn = fn
        self.reads = tuple(reads)
        self.writes = tuple(writes)
        self.deps = []
        self.sig = False
        self.sig_idx = 0
        self.dma_slot = -1
        self.dma_cnt = 0
        self.is_dma = is_dma
        self.barrier = barrier
        self.dma_snapshot = None


class _Rec:
    def __init__(self):
        self.call = None

    def __getattr__(self, name):
        def f(*a, **k):
            self.call = (name, a, k)
            return self
        return f


def _eager(fn):
    r = _Rec()
    fn(r)
    name, a, k = r.call
    return lambda e: getattr(e, name)(*a, **k)


PSUM_BANK = {"pA0": "pA0", "pA1": "pA1", "pA2": "pA2", "pT": "pT", "pT2": "pT",
             "pL0a": "pL", "pL1a": "pL", "pL2a": "pL", "pL3a": "pL", "pSm": "pLb", "pL0b": "pLb", "pL1b": "pLb", "pL2b": "pLb", "pL3b": "pLb",
             "pY": "pY", "pZo": "pZS", "pS": "pZS", "pCB": "pZS", "pZ": "pCBt"}


def _flat(keys):
    out = []
    for k in keys:
        if isinstance(k, (list, tuple)):
            out.extend(_flat(k))
        else:
            out.append(k)
    return out


class Prog:
    def __init__(self):
        self.ops = []
        self.muted = False

    def add(self, eng, fn, reads=(), writes=()):
        if not self.muted:
            reads, writes = _flat(reads), _flat(writes)
            if eng != "pe":
                extra = ["bk_" + PSUM_BANK[k] for k in reads if k in PSUM_BANK]
            else:
                extra = ["bk_" + PSUM_BANK[k] for k in writes if k in PSUM_BANK]
            if extra:
                writes = list(writes) + sorted(set(extra))
            self.ops.append(Op(eng, _eager(fn), reads, writes, False))

    def dma(self, fn, reads=(), writes=(), eng="sp"):
        if not self.muted:
            self.ops.append(Op(eng, _eager(fn), reads, writes, True))

    def barrier(self):
        if not self.muted:
            self.ops.append(Op("all", None, (), (), False, barrier=True))

    def emit(self, nc):
        ops = self.ops
        last_w = {}
        readers = {}
        for i, op in enumerate(ops):
            op.idx = i
            if op.barrier:
                last_w = {}
                readers = {}
                continue
            deps = set()
            for k in op.reads:
                j = last_w.get(k)
                if j is not None:
                    deps.add((j, "raw"))
            for k in op.writes:
                j = last_w.get(k)
                if j is not None:
                    deps.add((j, "waw"))
                for j in readers.get(k, ()):
                    deps.add((j, "war"))
            for k in op.reads:
                lst = readers.setdefault(k, [])
                if not op.is_dma:
                    lst[:] = [j for j in lst if ops[j].is_dma or ops[j].eng != op.eng]
                lst.append(i)
            for k in op.writes:
                last_w[k] = i
                readers[k] = []
            best = set()
            for j, kind in deps:
                if j == i:
                    continue
                pj = ops[j]
                if pj.eng == op.eng and not pj.is_dma and not op.is_dma:
                    if op.eng == "pe":
                        continue
                    if kind != "raw":
                        continue
                best.add(j)
            op.deps = sorted(best)
            for j in op.deps:
                ops[j].sig = True
        cnt = {e: 0 for e in ENGS}
        dma_cnt = [0] * N_DMA_SEMS
        ndma = 0
        nsw = 0
        nhw = 0
        nbar = 0
        for op in ops:
            if op.barrier:
                op.dma_snapshot = list(dma_cnt)
                nbar += 1
                op.sig_idx = nbar
            elif op.is_dma:
                if op.eng == "pool":
                    op.dma_slot = N_DMA_SEMS // 2 + (nsw % (N_DMA_SEMS // 2))
                    nsw += 1
                else:
                    op.dma_slot = nhw % (N_DMA_SEMS // 2)
                    nhw += 1
                dma_cnt[op.dma_slot] += 1
                op.dma_cnt = dma_cnt[op.dma_slot]
                ndma += 1
            elif op.sig:
                cnt[op.eng] += 1
                op.sig_idx = cnt[op.eng]
        self.stats = dict(cnt, ndma=ndma, nops=len(ops), nbar=nbar)

        with contextlib.ExitStack() as es:
            esem = {e: es.enter_context(nc.semaphore("s_" + e)) for e in ("pe", "act", "dve", "pool")}
            dsem = [es.enter_context(nc.semaphore("s_dma%d" % i)) for i in range(N_DMA_SEMS)]
            bsem = es.enter_context(nc.semaphore("s_bar"))
            block = es.enter_context(nc.Block())

            def run(ename, eng):
                waited = {}

                def wait_dma(slot, v):
                    key = ("d", slot)
                    if v > 0 and waited.get(key, 0) < v:
                        eng.wait_ge(dsem[slot], v)
                        waited[key] = v

                for op in ops:
                    if op.barrier:
                        for s in range(N_DMA_SEMS):
                            wait_dma(s, 16 * op.dma_snapshot[s])
                        eng.drain().then_inc(bsem, 1)
                        eng.wait_ge(bsem, len(ENGS) * op.sig_idx)
                        continue
                    if op.eng != ename:
                        continue
                    if op.is_dma and op.dma_cnt > 1:
                        wait_dma(op.dma_slot, 16 * (op.dma_cnt - 1))
                    for j in op.deps:
                        pj = ops[j]
                        if pj.is_dma:
                            wait_dma(pj.dma_slot, 16 * pj.dma_cnt)
                        else:
                            key = ("e", pj.eng)
                            v = pj.sig_idx
                            if waited.get(key, 0) < v:
                                eng.wait_ge(esem[pj.eng], v)
                                waited[key] = v
                    ins = op.fn(eng)
                    if op.is_dma:
                        ins.then_inc(dsem[op.dma_slot], 16)
                    elif op.sig:
                        ins.then_inc(esem[op.eng], 1)
                if ename == "sp":
                    for s in range(N_DMA_SEMS):
                        wait_dma(s, 16 * dma_cnt[s])

            @block.tensor
            def _(e):
                run("pe", e)

            @block.scalar
            def _(e):
                run("act", e)

            @block.vector
            def _(e):
                run("dve", e)

            @block.gpsimd
            def _(e):
                run("pool", e)

            @block.sync
            def _(e):
                run("sp", e)


D = 1024
DI = 2048
NCOL = 10272
C_XBC = 2048
C_DT = 6144
C_U = 6176
C_PG = 7200
C_GS = 8224
C_GP = 9248
NPH = 1024
NTOKH = 16 + NPH + 64
NPRE = 6
POOL_W = (2, 4, 8, 16)


class _Stop(Exception):
    pass


def build_program(stop=None):
    nc = bass.Bass("TRN2", target_bir_lowering=False)

    def chk(name):
        if stop is not None and name == stop:
            P.muted = True

    di = lambda n, s: nc.dram_tensor(n, s, F32, kind="ExternalInput").ap()
    do = lambda n, s: nc.dram_tensor(n, s, F32, kind="ExternalOutput").ap()
    xin = di("xin", [2128, D])
    xpre = di("xpre", [NPRE * NPH + 16, D])
    pm_d = di("pm", [1, 8])
    w_in = di("w_in", [D, NCOL])
    w_bs = di("w_bs", [DI, D])
    w_bp = di("w_bp", [D, D])
    w_out = di("w_out", [D, D])
    pool_w = di("pool_w", [4, 256, 256])
    sst = di("sst", [128, 2048])
    scv = di("scv", [128, 32, 4])
    spl = di("spl", [128, 8, 15])
    ng_d = di("ng", [128, 8])
    cw_d = di("cw", [128, 32, 4])
    cb_d = di("cb", [128, 32])
    sng_d = di("sng", [128, 16])
    psc_d = di("psc", [128, 8])
    gb_d = di("gb", [128, 16])
    dtb_d = di("dtb", [1, 32])
    alog_d = di("alog", [1, 32])
    dsk_d = di("dsk", [1, 32])
    fg_d = di("fg", [1, D])
    pinv_d = di("pinv", [1, 64])
    yP = do("yP", [2048, D])
    yS = do("yS", [64, D])
    hPo = do("hPo", [128, 2048])
    hSo = do("hSo", [128, 2048])
    cvP = do("cvP", [128, 32, 3])
    cvS = do("cvS", [128, 32, 3])
    plP = do("plP", [128, 8, 15])
    plS = do("plS", [128, 8, 15])

    P = Prog()
    import os as _os2
    DBG = bool(_os2.environ.get("KDEBUG"))
    dbg_list = []

    def dbg_out(name, ap, shape, key):
        if not DBG:
            return
        t = nc.dram_tensor("dbg_" + name, list(shape), F32, kind="ExternalOutput").ap()
        dbg_list.append(name)
        P.dma(lambda e: e.dma_start(out=t, in_=ap), reads=[key], eng="pool")

    with contextlib.ExitStack() as es:
        def sb(name, shape, dt=F32, stack=es):
            return stack.enter_context(nc.sbuf_tensor(name, shape, dt))

        def ps(name, shape, dt=F32):
            return es.enter_context(nc.psum_tensor(name, shape, dt))

        pA = [ps("pA0", [128, 512]), ps("pA1", [128, 512])]
        pT = ps("pT", [128, 1024], BF16)
        pL = ps("pL", [128, 4, 128])
        pLb = ps("pLb", [128, 4, 128])
        pLs = [pL, pLb]
        pSm = pLb[:, :, :].rearrange("p a b -> p (a b)")[:, 0:256]
        pTb = pLb[:, :, :].rearrange("p a b -> p (a b)").bitcast(BF16)
        pLb_keys = ("pL0b", "pL1b", "pL2b", "pL3b", "pSm")
        pY = ps("pY", [128, 512])
        pZS = ps("pZS", [128, 512])
        pCBt = ps("pCBt", [128, 512])
        pa_ctr = [0]
        pa_ext = [False]
        pa_all = [(pA[0], ("pA0",)), (pA[1], ("pA1",)), (pCBt, ("pZ",)),
                  (pL[:, :, :].rearrange("p a b -> p (a b)"), ("pL0a", "pL1a", "pL2a", "pL3a")),
                  (pLb[:, :, :].rearrange("p a b -> p (a b)"), ("pL0b", "pL1b", "pL2b", "pL3b", "pSm")),
                  (pY, ("pY",))]

        def next_pA():
            n = 6 if pa_ext[0] else 2
            i = pa_ctr[0] % n
            pa_ctr[0] += 1
            return pa_all[i]

        tri = sb("tri", [128, 128])
        ones = sb("ones", [128, 128])
        identf = sb("identf", [128, 128])
        ident = sb("ident", [128, 128], BF16)
        trib = sb("trib", [128, 128], BF16)
        onesb = sb("onesb", [128, 128], BF16)
        a_hi = sb("a_hi", [128, 9 * 32], BF16)
        a_lo = sb("a_lo", [128, 9 * 32], BF16)
        ng = sb("ng_s", [128, 8])
        cw = sb("cw_s", [128, 32, 4])
        cb = sb("cb_s", [128, 32])
        sng = sb("sng_s", [128, 16])
        psc = sb("psc_s", [128, 8])
        gb = sb("gb_s", [128, 16])
        dtb = sb("dtb_s", [128, 32])
        Abc = sb("Abc", [128, 32])
        dsk = sb("dsk_s", [128, 32])
        fg = sb("fg_s", [128, D])
        pinv = sb("pinv_s", [128, 64])
        pm = sb("pm_s", [128, 8])
        scvb = sb("scvb", [128, 32, 4], BF16)
        splf = sb("splf", [128, 8, 15])
        Wdt = sb("Wdt", [128, 8, 32], BF16)
        cvPs = sb("cvPs", [128, 32, 3])
        cvSs = sb("cvSs", [128, 32, 3])
        plPs = sb("plPs", [128, 8, 15])
        plSs = sb("plSs", [128, 8, 15])
        hcar = sb("hcar", [128, 2048])
        NCH = 9
        dtn = ["dtx", "mx", "tA", "dt", "a", "nac", "e", "dtdec", "dec"]
        dta = {n: sb("dt_" + n, [128, NCH * 32]) for n in dtn}
        nT = sb("nT", [128, 8, NTOKH], BF16)
        MG = sb("MG", [128, 8, NPH + 64], BF16)

        P.add("pool", lambda e: e.memset(identf[:], 0.0), writes=["identf"])
        P.add("pool", lambda e: e.memset(tri[:], 1.0), writes=["tri"])
        P.add("pool", lambda e: e.memset(ones[:], 1.0), writes=["ones"])
        P.add("pool", lambda e: e.memset(hcar[:], 0.0), writes=["hcar"])
        P.add("pool", lambda e: e.affine_select(out=identf[:], in_=tri[:], pattern=[[-1, 128]], compare_op=ALU.is_equal,
                                                fill=0.0, base=0, channel_multiplier=1), reads=["tri"], writes=["identf"])
        P.add("pool", lambda e: e.affine_select(out=tri[:], in_=tri[:], pattern=[[1, 128]], compare_op=ALU.is_ge,
                                                fill=0.0, base=0, channel_multiplier=-1), reads=["tri", "identf"], writes=["tri"])
        P.add("dve", lambda e: e.tensor_copy(out=ident[:], in_=identf[:]), reads=["identf"], writes=["ident"])
        P.add("dve", lambda e: e.tensor_copy(out=trib[:], in_=tri[:]), reads=["tri"], writes=["trib"])
        P.add("dve", lambda e: e.tensor_copy(out=onesb[:], in_=ones[:]), reads=["ones"], writes=["onesb"])
        for dst, src, key in ((ng, ng_d, "ng"), (cw, cw_d, "cw"), (cb, cb_d, "cb"), (sng, sng_d, "sng"), (psc, psc_d, "psc"),
                              (gb, gb_d, "gb"), (splf, spl, "splf")):
            P.dma(lambda e, dst=dst, src=src: e.dma_start(out=dst[:], in_=src), writes=[key])
        for dst, src, key in ((dtb, dtb_d, "dtb"), (Abc, alog_d, "Abc"), (dsk, dsk_d, "dsk"), (fg, fg_d, "fg"), (pinv, pinv_d, "pinv"), (pm, pm_d, "pm")):
            P.dma(lambda e, dst=dst, src=src: e.dma_start(out=dst[:], in_=src[0:1, :].partition_broadcast(128)), writes=[key])
        P.dma(lambda e: e.dma_start(out=scvb[:], in_=scv), writes=["scvb"], eng="pool")
        P.dma(lambda e: e.dma_start(out=Wdt[:], in_=w_in[:, C_DT:C_DT + 32].rearrange("(k p) c -> p k c", p=128)), writes=["Wdt"], eng="pool")
        P.add("act", lambda e: e.activation(out=Abc[:], in_=Abc[:], func=AF.Exp), reads=["Abc"], writes=["Abc"])
        P.add("dve", lambda e: e.tensor_scalar(out=Abc[:], in0=Abc[:], scalar1=-1.0, scalar2=None, op0=ALU.mult), reads=["Abc"], writes=["Abc"])

        def wload(dst, src_rows_cols, key):
            P.dma(lambda e: e.dma_start(out=dst, in_=src_rows_cols.rearrange("(k p) c -> p k c", p=128)), writes=[key], eng="pool")

        class Prefetch:
            def __init__(self):
                self.th = []
                self.issued = 0

            def add(self, dst, src, key):
                self.th.append((dst, src, key))
                return len(self.th) - 1

            def need(self, i):
                while self.issued <= min(i + 2, len(self.th) - 1):
                    wload(*self.th[self.issued])
                    self.issued += 1

        def mm_group(out_ap, okey, pairs, rkeys):
            n = len(pairs)
            for i, (l, r) in enumerate(pairs):
                P.add("pe", lambda e, l=l, r=r, i=i: e.matmul(out_ap, lhsT=l, rhs=r, start=(i == 0), stop=(i == n - 1)),
                      reads=rkeys, writes=[okey])

        def rstd_from_ssq(ssq_ap, key, inv_n):
            P.add("dve", lambda e: e.tensor_scalar(out=ssq_ap, in0=ssq_ap, scalar1=inv_n, scalar2=EPS, op0=ALU.mult, op1=ALU.add),
                  reads=[key], writes=[key])
            P.add("act", lambda e: e.activation(out=ssq_ap, in_=ssq_ap, func=AF.Sqrt), reads=[key], writes=[key])
            P.add("dve", lambda e: e.reciprocal(out=ssq_ap, in_=ssq_ap), reads=[key], writes=[key])

        ys = contextlib.ExitStack()
        try:
          for blk in range(NPRE + 2):
            pre = blk < NPRE
            hf = -1 if pre else blk - NPRE
            xsrc, xrow0 = (xpre, blk * NPH) if pre else (xin, hf * NPH)
            has_s = (hf == 1)
            ncols = NTOKH if has_s else 16 + NPH
            ntok = NPH + 64 if has_s else NPH
            chunks = [(c, 16 + c * 128, 128, c * 128) for c in range(8)]
            if has_s:
                chunks.append((8, 16 + NPH, 64, NPH))
            fblocks = [(16, 512, 0), (528, 512, 512)]
            if has_s:
                fblocks.append((16 + NPH, 64, NPH))
            H = "pre_" if pre else "h%d_" % hf
            pa_ext[0] = pre

            if not pre:
                ys = contextlib.ExitStack()
                ynT = ys.enter_context(nc.sbuf_tensor("ynT%d" % hf, [128, 16, NPH + 64], BF16))
            if pre and blk > 0:
                new_alloc = False
            else:
                new_alloc = True
                sa = contextlib.ExitStack()
            if new_alloc:
                sba = lambda n, s, dt=F32, sa=sa, blk=blk: sb(n + "_b%d" % blk, s, dt, stack=sa)
                xt = [sba("xt%d" % i, [128, D]) for i in range(2)]
                xb = [sba("xb%d" % i, [128, D], BF16) for i in range(2)]
                ssq0 = [sba("ssq0_%d" % i, [128, 1]) for i in range(2)]
                Wx = [sba("Wx%d" % i, [128, 8, 512], BF16) for i in range(2)]
                Wz = [sba("Wz%d" % i, [128, 8, 256], BF16) for i in range(2)]
                rawP = sba("rawP", [128, 4, 4 + NPH], BF16)
                rawS = sba("rawS", [128, 4, 4 + 64], BF16)
                xsT = sba("xsT", [128, 2, NPH + 64], BF16)
                BT = sba("BT", [128, NPH + 64], BF16)
                CT = sba("CT", [128, NPH + 64], BF16)
                xBt = sba("xBt", [128, NCH, 384], BF16)
                hTa = sba("hTa", [128, NCH, 256], BF16)
                yza = sba("yza", [128, NCH, 256], BF16)
                ssqa = sba("ssqa", [128, NCH])
                dg = sba("dg", [128, 4, 4, 128], BF16)
                szs = sba("szs", [128, 256])
                CBm = [sba("CBm%d" % p_, [128, 128]) for p_ in range(2)]
                Eh = [[sba("E%d_%d" % (p_, i), [128, 128]) for i in range(4)] for p_ in range(2)]
                MT = [[sba("MT%d_%d" % (p_, i), [128, 128], BF16) for i in range(4)] for p_ in range(2)]
                t1 = [sba("t1_%d" % p_, [128, 256]) for p_ in range(2)]
                t3 = sba("t3", [128, 256])
                xdt = [sba("xdt%d" % p_, [128, 256], BF16) for p_ in range(2)]
                xd = [sba("xd%d" % p_, [128, 256], BF16) for p_ in range(2)]
                xdd = [sba("xdd%d" % p_, [128, 256], BF16) for p_ in range(2)]
                tmpH = sba("tmpH", [128, 256])
                hS = sba("hS", [128, 256])
                yn = [sba("yn%d" % p_, [128, 256], BF16) for p_ in range(2)]
                junk = sba("junkA", [128, 256], BF16)

            if True:
                def issue_w(g_):
                    wb_ = g_ % 2
                    wload(Wx[wb_][:, :, 0:256], w_in[:, C_XBC + g_ * 256: C_XBC + (g_ + 1) * 256], H + "Wx%da" % wb_)
                    wload(Wx[wb_][:, :, 256:384], w_in[:, C_XBC + 2048 + g_ * 128: C_XBC + 2048 + (g_ + 1) * 128], H + "Wx%db" % wb_)
                    if not pre:
                        wload(Wx[wb_][:, :, 384:512], w_in[:, C_XBC + 3072 + g_ * 128: C_XBC + 3072 + (g_ + 1) * 128], H + "Wx%dc" % wb_)
                        wload(Wz[wb_][:, :, :], w_in[:, g_ * 256:(g_ + 1) * 256], H + "Wz%d" % wb_)

                issue_w(0)
                for ti, j0 in enumerate(range(0, ncols, 128)):
                    R = min(128, ncols - j0)
                    b = ti % 2
                    xk, bk, sk = H + "xt%d" % b, H + "xb%d" % b, H + "ssq0%d" % b
                    if j0 + R <= 16 + NPH:
                        P.dma(lambda e, b=b, j0=j0, R=R: e.dma_start(out=xt[b][:R, :], in_=xsrc[xrow0 + j0: xrow0 + j0 + R, :]), writes=[xk])
                    else:
                        rp = 16 + NPH - j0
                        P.dma(lambda e, b=b, j0=j0, rp=rp: e.dma_start(out=xt[b][:rp, :], in_=xsrc[xrow0 + j0: xrow0 + j0 + rp, :]), writes=[xk])
                        P.dma(lambda e, b=b, rp=rp: e.dma_start(out=xt[b][rp:rp + 64, :], in_=xin[2064:2128, :]), writes=[xk])
                    P.add("act", lambda e, b=b, R=R: e.activation(out=xb[b][:R, :], in_=xt[b][:R, :], func=AF.Square, accum_out=ssq0[b][:R, :]),
                          reads=[xk], writes=[bk, sk])
                    rstd_from_ssq(ssq0[b][:R, :], sk, 1.0 / D)
                    P.add("dve", lambda e, b=b, R=R: e.tensor_scalar(out=xb[b][:R, :], in0=xt[b][:R, :], scalar1=ssq0[b][:R, 0:1], scalar2=None, op0=ALU.mult),
                          reads=[xk, sk], writes=[bk])
                    xtb = pTb if ti % 2 else pT
                    xks = list(pLb_keys) if ti % 2 else ["pT", "pT2"]
                    for k in range(8):
                        P.add("pe", lambda e, b=b, R=R, k=k: e.transpose(out=xtb[:, k * 128:k * 128 + R], in_=xb[b][:R, k * 128:(k + 1) * 128], identity=ident[:R, :R]),
                              reads=[bk, "ident"], writes=xks)
                    P.add("dve", lambda e, R=R, j0=j0: e.tensor_tensor(out=nT[:, :, j0:j0 + R], in0=xtb[:, :].rearrange("p (k t) -> p k t", k=8)[:, :, :R],
                                                                       in1=ng[:, :].unsqueeze(2).to_broadcast([128, 8, R]), op=ALU.mult),
                          reads=xks + ["ng"], writes=[H + "nT"])
                nTk = H + "nT"
                chk("S0_%d" % hf)

                import os as _os
                if "S0b" in _os.environ.get("KSKIP", ""):
                    P.muted = True
                dk = lambda n: H + "dt_" + n
                A3 = lambda n: dta[n][:, :].rearrange("p (c h) -> p c h", h=32)
                P.add("pool", lambda e: e.memset(dta["dtx"][:], 0.0), writes=[dk("dtx")])
                pD, pDk = next_pA()
                for (c, col0, T, tok0) in chunks:
                    mm_group(pD[:T, c * 32:(c + 1) * 32], pDk, [(nT[:, k, col0:col0 + T], Wdt[:, k, :]) for k in range(8)], [nTk, "Wdt"])
                    P.add("dve", lambda e, c=c, T=T: e.tensor_tensor(out=dta["dtx"][:T, c * 32:(c + 1) * 32], in0=pD[:T, c * 32:(c + 1) * 32], in1=dtb[:T, :], op=ALU.add),
                          reads=[pDk, "dtb"], writes=[dk("dtx")])
                chk("S0b1_%d" % hf)
                P.add("dve", lambda e: e.tensor_scalar(out=dta["mx"][:], in0=dta["dtx"][:], scalar1=0.0, scalar2=None, op0=ALU.max), reads=[dk("dtx")], writes=[dk("mx")])
                P.add("dve", lambda e: e.tensor_scalar(out=dta["tA"][:], in0=dta["dtx"][:], scalar1=0.0, scalar2=None, op0=ALU.min), reads=[dk("dtx")], writes=[dk("tA")])
                P.add("dve", lambda e: e.tensor_tensor(out=dta["tA"][:], in0=dta["tA"][:], in1=dta["mx"][:], op=ALU.subtract), reads=[dk("tA"), dk("mx")], writes=[dk("tA")])
                P.add("act", lambda e: e.activation(out=dta["tA"][:], in_=dta["tA"][:], func=AF.Exp), reads=[dk("tA")], writes=[dk("tA")])
                P.add("act", lambda e: e.activation(out=dta["tA"][:], in_=dta["tA"][:], func=AF.Ln, bias=1.0, scale=1.0), reads=[dk("tA")], writes=[dk("tA")])
                P.add("dve", lambda e: e.tensor_tensor(out=dta["dt"][:], in0=dta["mx"][:], in1=dta["tA"][:], op=ALU.add), reads=[dk("tA"), dk("mx")], writes=[dk("dt")])
                if pre:
                    P.add("dve", lambda e: e.tensor_scalar(out=dta["dt"][:], in0=dta["dt"][:], scalar1=pm[:, blk:blk + 1], scalar2=None, op0=ALU.mult),
                          reads=[dk("dt"), "pm"], writes=[dk("dt")])
                chk("S0b2_%d" % hf)
                P.add("dve", lambda e: e.tensor_tensor(out=A3("a"), in0=A3("dt"), in1=Abc[:, :].unsqueeze(1).to_broadcast([128, NCH, 32]), op=ALU.mult),
                      reads=[dk("dt"), "Abc"], writes=[dk("a")])
                chk("S0b3_%d" % hf)
                pC, pCk = next_pA()
                pE, pEk = next_pA()
                P.add("dve", lambda e: e.tensor_copy(out=a_hi[:], in_=dta["a"][:]), reads=[dk("a")], writes=[H + "a_hi"])
                P.add("dve", lambda e: e.tensor_tensor(out=a_lo[:], in0=dta["a"][:], in1=a_hi[:], op=ALU.subtract), reads=[dk("a"), H + "a_hi"], writes=[H + "a_lo"])
                ak = [H + "a_hi", H + "a_lo"]
                for pi, asrc in enumerate((a_hi, a_lo)):
                    P.add("pe", lambda e, asrc=asrc, pi=pi: e.matmul(pC[:, 0:256], lhsT=trib[:, :], rhs=asrc[:, 0:256], start=(pi == 0), stop=(pi == 1)), reads=["trib"] + ak, writes=[pCk])
                for pi, asrc in enumerate((a_hi, a_lo)):
                    P.add("pe", lambda e, asrc=asrc, pi=pi: e.matmul(pE[:, 0:256], lhsT=onesb[:, :], rhs=asrc[:, 0:256], start=(pi == 0), stop=(pi == 1)), reads=["onesb"] + ak, writes=[pEk])
                ncc = 256
                if has_s:
                    for pi, asrc in enumerate((a_hi, a_lo)):
                        P.add("pe", lambda e, asrc=asrc, pi=pi: e.matmul(pC[:64, 256:288], lhsT=trib[:64, :64], rhs=asrc[:64, 256:288], start=(pi == 0), stop=(pi == 1)), reads=["trib"] + ak, writes=[pCk])
                    for pi, asrc in enumerate((a_hi, a_lo)):
                        P.add("pe", lambda e, asrc=asrc, pi=pi: e.matmul(pE[:, 256:288], lhsT=onesb[:64, :], rhs=asrc[:64, 256:288], start=(pi == 0), stop=(pi == 1)), reads=["onesb"] + ak, writes=[pEk])
                    ncc = 288
                chk("S0b4_%d" % hf)
                P.add("dve", lambda e, ncc=ncc: e.tensor_scalar(out=dta["nac"][:, :ncc], in0=pC[:, :ncc], scalar1=-1.0, scalar2=None, op0=ALU.mult), reads=[pCk], writes=[dk("nac")])
                chk("S0b5_%d" % hf)
                P.add("act", lambda e, ncc=ncc: e.activation(out=dta["e"][:, :ncc], in_=dta["nac"][:, :ncc], func=AF.Exp, scale=-1.0), reads=[dk("nac")], writes=[dk("e")])
                chk("S0b6_%d" % hf)
                P.add("dve", lambda e, ncc=ncc: e.tensor_tensor(out=dta["dtdec"][:, :ncc], in0=pE[:, :ncc], in1=dta["nac"][:, :ncc], op=ALU.add), reads=[pEk, dk("nac")], writes=[dk("dtdec")])
                chk("S0b7_%d" % hf)
                P.add("act", lambda e, ncc=ncc: e.activation(out=dta["dtdec"][:, :ncc], in_=dta["dtdec"][:, :ncc], func=AF.Exp), reads=[dk("dtdec")], writes=[dk("dtdec")])
                P.add("dve", lambda e, ncc=ncc: e.tensor_tensor(out=dta["dtdec"][:, :ncc], in0=dta["dtdec"][:, :ncc], in1=dta["dt"][:, :ncc], op=ALU.mult), reads=[dk("dtdec"), dk("dt")], writes=[dk("dtdec")])
                chk("S0b8_%d" % hf)
                P.add("act", lambda e, ncc=ncc: e.activation(out=dta["dec"][:, :ncc], in_=pE[:, :ncc], func=AF.Exp), reads=[pEk], writes=[dk("dec")])

                if pre:
                    A3t = lambda n: dta[n][:, 0:256].rearrange("p (c h) -> p c h", h=32)
                    P.add("dve", lambda e: e.tensor_copy(out=dta["tA"][:, 0:256], in_=pE[:, 0:256]), reads=[pEk], writes=[dk("tA")])
                    P.add("pool", lambda e: e.memset(dta["mx"][:, 224:256], 0.0), reads=[dk("mx")], writes=[dk("mx")])
                    for c_ in range(6, -1, -1):
                        P.add("dve", lambda e: e.tensor_tensor(out=dta["mx"][:, c_ * 32:(c_ + 1) * 32], in0=dta["mx"][:, (c_ + 1) * 32:(c_ + 2) * 32],
                                                               in1=dta["tA"][:, (c_ + 1) * 32:(c_ + 2) * 32], op=ALU.add),
                              reads=[dk("mx"), dk("tA")], writes=[dk("mx")])
                    P.add("dve", lambda e: e.tensor_tensor(out=dta["e"][:, 0:32], in0=dta["mx"][:, 0:32], in1=dta["tA"][:, 0:32], op=ALU.add),
                          reads=[dk("mx"), dk("tA")], writes=[dk("e")])
                    P.add("act", lambda e: e.activation(out=dta["e"][:, 0:32], in_=dta["e"][:, 0:32], func=AF.Exp), reads=[dk("e")], writes=[dk("e")])
                    P.add("act", lambda e: e.activation(out=dta["mx"][:, 0:256], in_=dta["mx"][:, 0:256], func=AF.Exp), reads=[dk("mx")], writes=[dk("mx")])
                    P.add("dve", lambda e: e.tensor_tensor(out=dta["dtdec"][:, 0:256], in0=dta["dtdec"][:, 0:256], in1=dta["mx"][:, 0:256], op=ALU.mult),
                          reads=[dk("dtdec"), dk("mx")], writes=[dk("dtdec")])
                if "S0b" in _os.environ.get("KSKIP", ""):
                    P.muted = False
                if hf == 1:
                    for nm in ("dt", "a", "nac", "e", "dtdec", "dec"):
                        dbg_out(nm, dta[nm][:, :], [128, NCH * 32], dk(nm))
                chk("S0b_%d" % hf)
                ntile = 3 if pre else 4
                for g in range(8):
                    wb = g % 2
                    Wxk, Wzk = H + "Wx%d" % wb, H + "Wz%d" % wb
                    tiles = [2 * g, 2 * g + 1, 16 + g, 24 + g]
                    hd = slice(g * 4, g * 4 + 4)
                    for j in range(ntile):
                        for s_ in range(4):
                            P.add("dve", lambda e, j=j, s_=s_: e.tensor_scalar(out=dg[:, j, s_, :], in0=ident[:, :], scalar1=cw[:, tiles[j], s_:s_ + 1], scalar2=None, op0=ALU.mult),
                                  reads=["ident", "cw"], writes=[H + "dg%d" % j])

                    def proj(j):
                        tl = tiles[j]
                        wsl = lambda k: Wx[wb][:, k, j * 128:(j + 1) * 128]
                        rk = H + "rawP%d" % j
                        Wxk_j = Wxk + "aabc"[j]
                        rsk = H + "rawS%d" % j
                        pp, ppk = next_pA()
                        mm_group(pp[:, 0:16], ppk, [(wsl(k), nT[:, k, 0:16]) for k in range(8)], [nTk, Wxk_j])
                        P.add("act", lambda e: e.activation(out=rawP[:, j, 0:4], in_=pp[:, 12:16], func=AF.Copy), reads=[ppk], writes=[rk + "h"])
                        for bi in range(2):
                            pp, ppk = next_pA()
                            c0 = 16 + bi * 512
                            mm_group(pp[:, :], ppk, [(wsl(k), nT[:, k, c0:c0 + 512]) for k in range(8)], [nTk, Wxk_j])
                            P.add("act", lambda e: e.activation(out=rawP[:, j, 4 + bi * 512: 4 + (bi + 1) * 512], in_=pp[:, :], func=AF.Copy), reads=[ppk], writes=[rk + "b%d" % bi])
                            if has_s and bi == 1:
                                P.add("dve", lambda e: e.tensor_copy(out=cvPs[:, tl, :], in_=pp[:, 509:512]), reads=[ppk], writes=["cvPs"])
                        if has_s:
                            pp, ppk = next_pA()
                            c0 = 16 + NPH
                            mm_group(pp[:, 0:64], ppk, [(wsl(k), nT[:, k, c0:c0 + 64]) for k in range(8)], [nTk, Wxk_j])
                            P.add("act", lambda e: e.activation(out=rawS[:, j, 4:68], in_=pp[:, 0:64], func=AF.Copy), reads=[ppk], writes=[rsk])
                            P.add("dve", lambda e: e.tensor_copy(out=cvSs[:, tl, :], in_=pp[:, 61:64]), reads=[ppk], writes=["cvSs"])
                            P.add("pool", lambda e: e.tensor_copy(out=rawS[:, j, 0:4], in_=scvb[:, tl, :]), reads=["scvb"], writes=[rsk])

                    def conv(j):
                        tl = tiles[j]
                        rk = H + "rawP%d" % j
                        rsk = H + "rawS%d" % j
                        if j < 2:
                            dst = lambda a, b_: xsT[:, j, a:b_]
                            dkey = H + "xsT%d" % j
                        elif j == 2:
                            dst = lambda a, b_: BT[:, a:b_]
                            dkey = H + "BT"
                        else:
                            dst = lambda a, b_: CT[:, a:b_]
                            dkey = H + "CT"
                        for bi in range(2):
                            pp, ppk = next_pA()
                            rks = [rk + "h", rk + "b0"] if bi == 0 else [rk + "b0", rk + "b1"]
                            mm_group(pp[:, :], ppk, [(dg[:, j, s_, :], rawP[:, j, bi * 512 + s_ + 1: bi * 512 + s_ + 513]) for s_ in range(4)], [H + "dg%d" % j] + rks)
                            P.add("act", lambda e: e.activation(out=dst(bi * 512, (bi + 1) * 512), in_=pp[:, :], func=AF.Silu, bias=cb[:, tl:tl + 1], scale=1.0),
                                  reads=[ppk, "cb"], writes=[dkey + "_%d" % bi])
                        if has_s:
                            pp, ppk = next_pA()
                            mm_group(pp[:, 0:64], ppk, [(dg[:, j, s_, :], rawS[:, j, s_ + 1: s_ + 65]) for s_ in range(4)], [H + "dg%d" % j, rsk])
                            P.add("act", lambda e: e.activation(out=dst(NPH, NPH + 64), in_=pp[:, 0:64], func=AF.Silu, bias=cb[:, tl:tl + 1], scale=1.0),
                                  reads=[ppk, "cb"], writes=[dkey + "_2"])

                    proj(0)
                    for j in range(1, ntile):
                        proj(j)
                        conv(j - 1)
                    conv(ntile - 1)
                    if g + 1 < 8:
                        issue_w(g + 1)
                    chk("S1a_%d_%d" % (hf, g))

                    def ckeys(base, tok0):
                        return [base + "_%d" % (2 if tok0 >= NPH else tok0 // 512)]
                    for (c, col0, T, tok0) in chunks:
                        ptb = pTb if c % 2 else pT
                        pk = list(pLb_keys) if c % 2 else ["pT"]
                        for jj, (src, skey) in enumerate(((xsT[:, 0, tok0:tok0 + T], H + "xsT0"), (xsT[:, 1, tok0:tok0 + T], H + "xsT1"), (BT[:, tok0:tok0 + T], H + "BT"))):
                            P.add("pe", lambda e, src=src, jj=jj, T=T: e.transpose(out=ptb[:T, jj * 128:(jj + 1) * 128], in_=src, identity=ident[:, :]),
                                  reads=ckeys(skey, tok0) + ["ident"], writes=pk)
                        P.add("dve", lambda e, c=c, T=T: e.tensor_copy(out=xBt[:T, c, :], in_=ptb[:T, 0:384]), reads=pk, writes=[H + "xBt%d" % c])
                    hg = hcar[:, g * 256:(g + 1) * 256]
                    hgk = "hcar%d" % g
                    if pre:
                        for (c, col0, T, tok0) in chunks:
                            xk_ = H + "xBt%d" % c
                            xddb = xdd[c % 2]
                            xddk = H + "xdd%d" % (c % 2)
                            P.add("dve", lambda e: e.tensor_tensor(out=xddb[:T, :].rearrange("t (h p) -> t h p", h=4), in0=xBt[:T, c, 0:256].rearrange("t (h p) -> t h p", h=4),
                                                                   in1=A3("dtdec")[:T, c, hd].unsqueeze(2).to_broadcast([T, 4, 64]), op=ALU.mult),
                                  reads=[xk_, dk("dtdec")], writes=[xddk])
                            P.add("pe", lambda e: e.matmul(pZS[:, 256:512], lhsT=xBt[:T, c, 256:384], rhs=xddb[:T, :], start=(c == 0), stop=(c == 7)),
                                  reads=[xk_, xddk], writes=["pS"])
                        P.add("pool", lambda e: e.tensor_tensor(out=tmpH[:, :].rearrange("n (h p) -> n h p", h=4), in0=hg.rearrange("n (h p) -> n h p", h=4),
                                                                in1=dta["e"][:, hd].unsqueeze(2).to_broadcast([128, 4, 64]), op=ALU.mult),
                              reads=[hgk, dk("e")], writes=[H + "tmpH"])
                        P.add("dve", lambda e: e.tensor_tensor(out=hg, in0=pZS[:, 256:512], in1=tmpH[:, :], op=ALU.add), reads=["pS", H + "tmpH"], writes=[hgk])
                        continue
                    def state_ops(c, col0, T, tok0):
                        samp = (c == 8)
                        if samp:
                            P.dma(lambda e: e.dma_start(out=hS[:, :], in_=sst[:, g * 256:(g + 1) * 256]), writes=[H + "hS"])
                            hcur, hk = hS[:, :], H + "hS"
                        else:
                            hcur, hk = hg, hgk
                        xk_ = H + "xBt%d" % c
                        if not pre:
                            P.add("pool", lambda e, c=c, hcur=hcur: e.tensor_copy(out=hTa[:, c, :], in_=hcur), reads=[hk], writes=[H + "hTa%d" % c])
                        xddb = xdd[c % 2]
                        xddk = H + "xdd%d" % (c % 2)
                        P.add("dve", lambda e, c=c, T=T: e.tensor_tensor(out=xddb[:T, :].rearrange("t (h p) -> t h p", h=4), in0=xBt[:T, c, 0:256].rearrange("t (h p) -> t h p", h=4),
                                                                         in1=A3("dtdec")[:T, c, hd].unsqueeze(2).to_broadcast([T, 4, 64]), op=ALU.mult),
                              reads=[xk_, dk("dtdec")], writes=[xddk])
                        P.add("pe", lambda e, c=c, T=T: e.matmul(pSm, lhsT=xBt[:T, c, 256:384], rhs=xddb[:T, :], start=True, stop=True),
                              reads=[xk_, xddk], writes=["pSm"])
                        P.add("pool", lambda e, c=c, hcur=hcur: e.tensor_tensor(out=tmpH[:, :].rearrange("n (h p) -> n h p", h=4), in0=hcur.rearrange("n (h p) -> n h p", h=4),
                                                                                in1=A3("dec")[:, c, hd].unsqueeze(2).to_broadcast([128, 4, 64]), op=ALU.mult),
                              reads=[hk, dk("dec")], writes=[H + "tmpH"])
                        P.add("dve", lambda e, hcur=hcur: e.tensor_tensor(out=hcur, in0=pSm, in1=tmpH[:, :], op=ALU.add), reads=["pSm", H + "tmpH"], writes=[hk])
                        if samp:
                            P.dma(lambda e: e.dma_start(out=hSo[:, g * 256:(g + 1) * 256], in_=hS[:, :]), reads=[hk])
                        elif has_s and c == 7:
                            P.dma(lambda e: e.dma_start(out=hPo[:, g * 256:(g + 1) * 256], in_=hg), reads=[hk])
                    chk("S1b_%d_%d" % (hf, g))
                    if pre:
                        continue
                    pend_sq = [None]

                    def emit_sq(c, T):
                        P.add("act", lambda e: e.activation(out=junk[:T, :], in_=yza[:T, c, :], func=AF.Square, scale=0.5, accum_out=ssqa[:T, c:c + 1]),
                              reads=[H + "yza%d" % c], writes=[H + "junk", H + "ssqa"])

                    for (c, col0, T, tok0) in chunks:
                        state_ops(c, col0, T, tok0)
                        xk_ = H + "xBt%d" % c
                        mm_group(pCBt[:T, 0:256], "pZ", [(nT[:, k, col0:col0 + T], Wz[wb][:, k, :]) for k in range(8)], [nTk, Wzk])
                        P.add("act", lambda e, T=T: e.activation(out=szs[:T, :], in_=pCBt[:T, 0:256], func=AF.Tanh, scale=0.5), reads=["pZ"], writes=[H + "sz"])
                        P.add("dve", lambda e, T=T: e.scalar_tensor_tensor(out=szs[:T, :], in0=szs[:T, :], scalar=1.0, in1=pCBt[:T, 0:256], op0=ALU.add, op1=ALU.mult),
                              reads=["pZ", H + "sz"], writes=[H + "sz"])
                        P.add("pe", lambda e, T=T, tok0=tok0: e.matmul(pZS[:T, 256:256 + T], lhsT=BT[:, tok0:tok0 + T], rhs=CT[:, tok0:tok0 + T], start=True, stop=True),
                              reads=ckeys(H + "BT", tok0) + ckeys(H + "CT", tok0), writes=["pCB"])
                        P.add("pe", lambda e, c=c, T=T, tok0=tok0: e.matmul(pZS[:T, 0:256], lhsT=CT[:, tok0:tok0 + T], rhs=hTa[:, c, :], start=True, stop=True),
                              reads=ckeys(H + "CT", tok0) + [H + "hTa%d" % c], writes=["pZo"])
                        for h in range(4):
                            for pi, asrc in enumerate((a_hi, a_lo)):
                                P.add("pe", lambda e, c=c, T=T, h=h, asrc=asrc, pi=pi: e.matmul(pL[:T, h, :T], lhsT=asrc[:T, c * 32 + g * 4 + h: c * 32 + g * 4 + h + 1].to_broadcast([T, T]),
                                                                                        rhs=trib[:T, :T], start=(pi == 0), stop=(pi == 1)),
                                      reads=[H + "a_hi", H + "a_lo", "trib"], writes=["pL%da" % h])
                        P.add("dve", lambda e, T=T: e.tensor_tensor(out=CBm[0][:T, :T], in0=pZS[:T, 256:256 + T], in1=tri[:T, :T], op=ALU.mult), reads=["pCB", "tri"], writes=[H + "CBm"])
                        P.add("pool", lambda e, c=c, T=T: e.tensor_tensor(out=xdt[0][:T, :].rearrange("t (h p) -> t h p", h=4), in0=xBt[:T, c, 0:256].rearrange("t (h p) -> t h p", h=4),
                                                                          in1=A3("dt")[:T, c, hd].unsqueeze(2).to_broadcast([T, 4, 64]), op=ALU.mult),
                              reads=[xk_, dk("dt")], writes=[H + "xdt"])
                        P.add("pool", lambda e, c=c, T=T: e.tensor_tensor(out=xd[0][:T, :].rearrange("t (h p) -> t h p", h=4), in0=xBt[:T, c, 0:256].rearrange("t (h p) -> t h p", h=4),
                                                                          in1=dsk[:T, hd].unsqueeze(2).to_broadcast([T, 4, 64]), op=ALU.mult),
                              reads=[xk_, "dsk"], writes=[H + "xd"])
                        for h in range(4):
                            P.add("act", lambda e, c=c, T=T, h=h: e.activation(out=Eh[0][h][:T, :T], in_=pL[:T, h, :T], func=AF.Exp,
                                                                               bias=dta["nac"][:T, c * 32 + g * 4 + h: c * 32 + g * 4 + h + 1], scale=1.0),
                                  reads=["pL%da" % h, dk("nac")], writes=[H + "E%d" % h])
                            P.add("dve", lambda e, T=T, h=h: e.scalar_tensor_tensor(out=MT[0][h][:T, :T], in0=Eh[0][h][:T, :T], scalar=1.0, in1=CBm[0][:T, :T], op0=ALU.min, op1=ALU.mult),
                                  reads=[H + "E%d" % h, H + "CBm"], writes=[H + "MT%d" % h])
                        P.add("pe", lambda e, T=T: e.matmul(pY[:T, 0:256], lhsT=ident[:T, :T], rhs=xd[0][:T, :], start=True, stop=False), reads=["ident", H + "xd"], writes=["pY"])
                        for h in range(4):
                            P.add("pe", lambda e, T=T, h=h: e.matmul(pY[:T, h * 64:(h + 1) * 64], lhsT=MT[0][h][:T, :T], rhs=xdt[0][:T, h * 64:(h + 1) * 64], start=False, stop=(h == 3)),
                                  reads=[H + "MT%d" % h, H + "xdt"], writes=["pY"])
                        P.add("dve", lambda e, c=c, T=T: e.tensor_tensor(out=t1[0][:T, :].rearrange("t (h p) -> t h p", h=4), in0=pZS[:T, 0:256].rearrange("t (h p) -> t h p", h=4),
                                                                         in1=A3("e")[:T, c, hd].unsqueeze(2).to_broadcast([T, 4, 64]), op=ALU.mult),
                              reads=["pZo", dk("e")], writes=[H + "t1"])
                        P.add("dve", lambda e, T=T: e.tensor_tensor(out=t3[:T, :], in0=pY[:T, 0:256], in1=t1[0][:T, :], op=ALU.add), reads=["pY", H + "t1"], writes=[H + "t3"])
                        P.add("pool", lambda e, c=c, T=T: e.tensor_tensor(out=yza[:T, c, :], in0=t3[:T, :], in1=szs[:T, :], op=ALU.mult), reads=[H + "t3", H + "sz"], writes=[H + "yza%d" % c])
                        if pend_sq[0] is not None:
                            emit_sq(*pend_sq[0])
                        pend_sq[0] = (c, T)
                    emit_sq(*pend_sq[0])
                    if hf == 1 and g == 0:
                        dbg_out("yza8", yza[:64, 8, :], [64, 256], H + "yza8")
                        dbg_out("yza0", yza[:, 0, :], [128, 256], H + "yza0")
                        dbg_out("ssqa", ssqa[:, :], [128, NCH], H + "ssqa")
                    nchk = len(chunks)
                    if has_s:
                        P.add("pool", lambda e: e.memset(ssqa[64:128, 8:9], 1.0), reads=[H + "ssqa"], writes=[H + "ssqa"])
                    rstd_from_ssq(ssqa[:, 0:nchk], H + "ssqa", 1.0 / 256)
                    for (c, col0, T, tok0) in chunks:
                        ynb = yn[c % 2]
                        ynk = H + "yn%d" % (c % 2)
                        P.add("dve", lambda e, c=c, T=T: e.tensor_scalar(out=ynb[:T, :], in0=yza[:T, c, :], scalar1=ssqa[:T, c:c + 1], scalar2=0.5, op0=ALU.mult, op1=ALU.mult),
                              reads=[H + "yza%d" % c, H + "ssqa"], writes=[ynk])
                        ytb = pTb if c % 2 else pT
                        yb0 = 0 if c % 2 else 512
                        yks = list(pLb_keys) if c % 2 else ["pT2"]
                        for jj in range(2):
                            P.add("pe", lambda e, T=T, jj=jj: e.transpose(out=ytb[:, yb0 + jj * 128: yb0 + jj * 128 + T], in_=ynb[:T, jj * 128:(jj + 1) * 128], identity=ident[:T, :T]),
                                  reads=[ynk, "ident"], writes=yks)
                        for jj in range(2):
                            P.add("act", lambda e, T=T, jj=jj, tok0=tok0: e.activation(out=ynT[:, 2 * g + jj, tok0:tok0 + T], in_=ytb[:, yb0 + jj * 128: yb0 + jj * 128 + T], func=AF.Copy,
                                                                                      scale=sng[:, 2 * g + jj: 2 * g + jj + 1]),
                                  reads=yks + ["sng"], writes=[H + "ynT"])
            if pre:
                if blk == NPRE - 1:
                    sa.close()
                    P.barrier()
                continue
            sa.close()
            chk("S1_%d" % hf)
            P.barrier()

            pa_ext[0] = True
            with contextlib.ExitStack() as sB:
                sbb = lambda n, s, dt=F32: sb(n + "_%d" % hf, s, dt, stack=sB)
                Wbs = [sbb("Wbs%d" % i, [128, 16, 128], BF16) for i in range(2)]
                Wg1 = [sbb("Wg1%d" % i, [128, 8, 128], BF16) for i in range(2)]
                gsb = [sbb("gsb%d" % i, [128, 512]) for i in range(2)]
                ctr = 0
                pf = Prefetch()
                pfi = {}
                for m in range(8):
                    wb = m % 2
                    pfi[("g1", m)] = pf.add(Wg1[wb][:, :, :], w_in[:, C_GS + m * 128: C_GS + (m + 1) * 128], H + "Wg1%d" % wb)
                    pfi[("bs", m)] = pf.add(Wbs[wb][:, :, :], w_bs[:, m * 128:(m + 1) * 128], H + "Wbs%d" % wb)
                for m in range(8):
                    wb = m % 2
                    pf.need(pfi[("bs", m)])
                    for (c0, N, tok0) in fblocks:
                        gi = ctr % 2
                        ctr += 1
                        pp, ppk = next_pA()
                        mm_group(pp[:, :N], ppk, [(Wg1[wb][:, k, :], nT[:, k, c0:c0 + N]) for k in range(8)], [nTk, H + "Wg1%d" % wb])
                        P.add("act", lambda e, pp=pp, N=N, gi=gi, m=m: e.activation(out=gsb[gi][:, :N], in_=pp[:, :N], func=AF.Sigmoid, bias=gb[:, m:m + 1], scale=1.0),
                              reads=[ppk, "gb"], writes=[H + "gsb%d" % gi])
                        pp2, ppk2 = next_pA()
                        mm_group(pp2[:, :N], ppk2, [(Wbs[wb][:, kt, :], ynT[:, kt, tok0:tok0 + N]) for kt in range(16)], [H + "ynT", H + "Wbs%d" % wb])
                        P.add("dve", lambda e, pp2=pp2, N=N, gi=gi, m=m, tok0=tok0: e.tensor_tensor(out=MG[:, m, tok0:tok0 + N], in0=pp2[:, :N], in1=gsb[gi][:, :N], op=ALU.mult),
                              reads=[ppk2, H + "gsb%d" % gi], writes=[H + "MG%d" % m])
            chk("S2_%d" % hf)
            P.barrier()
            ys.close()

            with contextlib.ExitStack() as sC:
                sbc = lambda n, s, dt=F32: sb(n + "_%d" % hf, s, dt, stack=sC)
                PGT = sbc("PGT", [128, 8, NPH + 64], BF16)
                Wu = [sbc("Wu%d" % i, [128, 8, 128], BF16) for i in range(2)]
                Wpg = [sbc("Wpg%d" % i, [128, 8, 128], BF16) for i in range(2)]
                Wgp = [sbc("Wgp%d" % i, [128, 8, 128], BF16) for i in range(2)]
                Wbp = [sbc("Wbp%d" % i, [128, 8, 128], BF16) for i in range(2)]
                PW = [sbc("PW%d" % i, [128, 2, 256], BF16) for i in range(2)]
                Wo = sbc("Wo", [128, 8, D], BF16)
                LP = 16 + NPH
                ubP = sbc("ubP", [128, LP])
                ubS = sbc("ubS", [128, 15 + 64])
                stP = [sbc("stP%d" % i, [128, LP]) for i in range(2)]
                stS = [sbc("stS%d" % i, [128, 15 + 64]) for i in range(2)]
                dT = sbc("dT", [128, 2, NPH + 64], BF16)
                spg = [sbc("spg%d" % i, [128, 512]) for i in range(2)]
                tmpc = [sbc("tmpc%d" % i, [128, 512]) for i in range(2)]
                xr = [sbc("xr%d" % i, [128, D]) for i in range(2)]
                yo = [sbc("yo%d" % i, [128, D]) for i in range(2)]
                junkc = sbc("junkc", [128, D], BF16)
                ssq4 = [sbc("ssq4_%d" % i, [128, 1]) for i in range(2)]
                wload(Wo[:, :, :], w_out[:, :], H + "Wo")
                ctr = 0
                pf = Prefetch()
                pfi = {}
                for gi_ in range(4):
                    pfi[("pw", gi_)] = pf.add(PW[gi_ % 2][:, :, :], pool_w[gi_], H + "PW%d" % (gi_ % 2))
                    for j in range(2):
                        ut = gi_ * 2 + j
                        pfi[("u", ut)] = pf.add(Wu[ut % 2][:, :, :], w_in[:, C_U + ut * 128: C_U + (ut + 1) * 128], H + "Wu%d" % (ut % 2))
                    for mo in range(2):
                        ot = gi_ * 2 + mo
                        pfi[("pg", ot)] = pf.add(Wpg[ot % 2][:, :, :], w_in[:, C_PG + ot * 128: C_PG + (ot + 1) * 128], H + "Wpg%d" % (ot % 2))
                for m in range(8):
                    pfi[("gp", m)] = pf.add(Wgp[m % 2][:, :, :], w_in[:, C_GP + m * 128: C_GP + (m + 1) * 128], H + "Wgp%d" % (m % 2))
                    pfi[("bp", m)] = pf.add(Wbp[m % 2][:, :, :], w_bp[:, m * 128:(m + 1) * 128], H + "Wbp%d" % (m % 2))
                for gi_ in range(4):
                    w = POOL_W[gi_]
                    pwb = gi_ % 2
                    pf.need(pfi[("pw", gi_)])
                    for j in range(2):
                        ut = gi_ * 2 + j
                        ub_ = ut % 2
                        pf.need(pfi[("u", ut)])
                        pp, ppk = next_pA()
                        mm_group(pp[:, 0:16], ppk, [(Wu[ub_][:, k, :], nT[:, k, 0:16]) for k in range(8)], [nTk, H + "Wu%d" % ub_])
                        P.add("act", lambda e, pp=pp: e.activation(out=ubP[:, 0:16], in_=pp[:, 0:16], func=AF.Copy), reads=[ppk], writes=[H + "ubP"])
                        for bi in range(2):
                            pp, ppk = next_pA()
                            c0 = 16 + bi * 512
                            mm_group(pp[:, :], ppk, [(Wu[ub_][:, k, :], nT[:, k, c0:c0 + 512]) for k in range(8)], [nTk, H + "Wu%d" % ub_])
                            P.add("act", lambda e, pp=pp, c0=c0: e.activation(out=ubP[:, c0:c0 + 512], in_=pp[:, :], func=AF.Copy), reads=[ppk], writes=[H + "ubP"])
                        seqs = [(ubP, stP, LP, 16, 0, H + "ubP", H + "stP")]
                        if has_s:
                            pp, ppk = next_pA()
                            c0 = 16 + NPH
                            mm_group(pp[:, 0:64], ppk, [(Wu[ub_][:, k, :], nT[:, k, c0:c0 + 64]) for k in range(8)], [nTk, H + "Wu%d" % ub_])
                            P.add("act", lambda e, pp=pp: e.activation(out=ubS[:, 15:79], in_=pp[:, 0:64], func=AF.Copy), reads=[ppk], writes=[H + "ubS"])
                            P.add("pool", lambda e, ut=ut: e.tensor_copy(out=ubS[:, 0:15], in_=splf[:, ut, :]), reads=["splf"], writes=[H + "ubS"])
                            P.add("dve", lambda e, ut=ut: e.tensor_copy(out=plPs[:, ut, :], in_=ubP[:, LP - 15:LP]), reads=[H + "ubP"], writes=["plPs"])
                            P.add("dve", lambda e, ut=ut: e.tensor_copy(out=plSs[:, ut, :], in_=ubS[:, 64:79]), reads=[H + "ubS"], writes=["plSs"])
                            seqs.append((ubS, stS, 79, 15, NPH, H + "ubS", H + "stS"))
                        for (ub, st, L, v0, tk0, ubk, stk) in seqs:
                            cur, curk = ub, ubk
                            sh = 1
                            lo = 0
                            si = 0
                            while sh < w:
                                nlo = lo + sh
                                dstb, dstk = st[si], stk + "%d" % si
                                eng = "pool" if si == 0 else "dve"
                                P.add(eng, lambda e, cur=cur, dstb=dstb, nlo=nlo, sh=sh, L=L: e.tensor_tensor(out=dstb[:, nlo:L], in0=cur[:, nlo:L], in1=cur[:, nlo - sh:L - sh], op=ALU.add),
                                      reads=[curk], writes=[dstk])
                                cur, curk = dstb, dstk
                                lo = nlo
                                sh *= 2
                                si ^= 1
                            nv = L - v0
                            P.add("dve", lambda e, cur=cur, ub=ub, v0=v0, L=L, j=j, tk0=tk0, nv=nv, w=w: e.scalar_tensor_tensor(
                                out=dT[:, j, tk0:tk0 + nv], in0=cur[:, v0:L], scalar=1.0 / w, in1=ub[:, v0:L], op0=ALU.mult, op1=ALU.subtract),
                                reads=[curk, ubk], writes=[H + "dT%d" % j])
                            if hf == 0 and tk0 == 0:
                                P.add("dve", lambda e, cur=cur, gi_=gi_: e.tensor_tensor(out=tmpc[0][:, 0:16], in0=cur[:, 16:32], in1=pinv[:, gi_ * 16:(gi_ + 1) * 16], op=ALU.mult),
                                      reads=[curk, "pinv"], writes=[H + "tmpc0"])
                                P.add("dve", lambda e, ub=ub, j=j: e.tensor_tensor(out=dT[:, j, 0:16], in0=tmpc[0][:, 0:16], in1=ub[:, 16:32], op=ALU.subtract),
                                      reads=[H + "tmpc0", ubk], writes=[H + "dT%d" % j])
                    for mo in range(2):
                        ot = gi_ * 2 + mo
                        wb = ot % 2
                        pf.need(pfi[("pg", ot)])
                        for (c0, N, tok0) in fblocks:
                            gi = ctr % 2
                            ctr += 1
                            pp, ppk = next_pA()
                            mm_group(pp[:, :N], ppk, [(Wpg[wb][:, k, :], nT[:, k, c0:c0 + N]) for k in range(8)], [nTk, H + "Wpg%d" % wb])
                            P.add("act", lambda e, pp=pp, N=N, gi=gi: e.activation(out=spg[gi][:, :N], in_=pp[:, :N], func=AF.Silu), reads=[ppk], writes=[H + "spg%d" % gi])
                            pp2, ppk2 = next_pA()
                            mm_group(pp2[:, :N], ppk2, [(PW[pwb][:, kt, mo * 128:(mo + 1) * 128], dT[:, kt, tok0:tok0 + N]) for kt in range(2)], [H + "dT0", H + "dT1", H + "PW%d" % pwb])
                            P.add("dve", lambda e, pp2=pp2, N=N, gi=gi, ot=ot, tok0=tok0: e.scalar_tensor_tensor(out=PGT[:, ot, tok0:tok0 + N], in0=pp2[:, :N], scalar=psc[:, ot:ot + 1],
                                                                                                               in1=spg[gi][:, :N], op0=ALU.mult, op1=ALU.mult),
                                  reads=[ppk2, H + "spg%d" % gi, "psc"], writes=[H + "PGT"])
                for m in range(8):
                    wb = m % 2
                    pf.need(pfi[("bp", m)])
                    for (c0, N, tok0) in fblocks:
                        gi = ctr % 2
                        ctr += 1
                        pp, ppk = next_pA()
                        mm_group(pp[:, :N], ppk, [(Wgp[wb][:, k, :], nT[:, k, c0:c0 + N]) for k in range(8)], [nTk, H + "Wgp%d" % wb])
                        P.add("act", lambda e, pp=pp, N=N, gi=gi, m=m: e.activation(out=spg[gi][:, :N], in_=pp[:, :N], func=AF.Sigmoid, bias=gb[:, 8 + m:9 + m], scale=1.0),
                              reads=[ppk, "gb"], writes=[H + "spg%d" % gi])
                        pp2, ppk2 = next_pA()
                        mm_group(pp2[:, :N], ppk2, [(Wbp[wb][:, k, :], PGT[:, k, tok0:tok0 + N]) for k in range(8)], [H + "PGT", H + "Wbp%d" % wb])
                        P.add("dve", lambda e, pp2=pp2, N=N, gi=gi: e.tensor_tensor(out=tmpc[gi][:, :N], in0=pp2[:, :N], in1=spg[gi][:, :N], op=ALU.mult),
                              reads=[ppk2, H + "spg%d" % gi], writes=[H + "tmpc%d" % gi])
                        P.add("pool", lambda e, N=N, gi=gi, m=m, tok0=tok0: e.tensor_tensor(out=MG[:, m, tok0:tok0 + N], in0=MG[:, m, tok0:tok0 + N], in1=tmpc[gi][:, :N], op=ALU.add),
                              reads=[H + "tmpc%d" % gi, H + "MG%d" % m], writes=[H + "MG%d" % m])
                chk("S3_%d" % hf)
                mgk = [H + "MG%d" % m for m in range(8)]
                for (c, col0, T, tok0) in chunks:
                    b = c % 2
                    samp = (c == 8)
                    if samp:
                        P.dma(lambda e, b=b: e.dma_start(out=xr[b][:64, :], in_=xin[2064:2128, :]), writes=[H + "xr%d" % b])
                    else:
                        r0 = 16 + hf * NPH + c * 128
                        P.dma(lambda e, b=b, r0=r0: e.dma_start(out=xr[b][:, :], in_=xin[r0:r0 + 128, :]), writes=[H + "xr%d" % b])
                    for half in range(2):
                        pp, ppk = next_pA()
                        mm_group(pp[:T, :], ppk, [(MG[:, k, tok0:tok0 + T], Wo[:, k, half * 512:(half + 1) * 512]) for k in range(8)], mgk + [H + "Wo"])
                        P.add("dve", lambda e, pp=pp, T=T, b=b, half=half: e.tensor_tensor(out=xr[b][:T, half * 512:(half + 1) * 512], in0=pp[:T, :], in1=xr[b][:T, half * 512:(half + 1) * 512], op=ALU.add),
                              reads=[ppk, H + "xr%d" % b], writes=[H + "xr%d" % b])
                    P.add("act", lambda e, T=T, b=b: e.activation(out=junkc[:T, :], in_=xr[b][:T, :], func=AF.Square, accum_out=ssq4[b][:T, :]),
                          reads=[H + "xr%d" % b], writes=[H + "junkc", H + "ssq4%d" % b])
                    rstd_from_ssq(ssq4[b][:T, :], H + "ssq4%d" % b, 1.0 / D)
                    P.add("dve", lambda e, T=T, b=b: e.scalar_tensor_tensor(out=yo[b][:T, :], in0=xr[b][:T, :], scalar=ssq4[b][:T, 0:1], in1=fg[:T, :], op0=ALU.mult, op1=ALU.mult),
                          reads=[H + "xr%d" % b, H + "ssq4%d" % b, "fg"], writes=[H + "yo%d" % b])
                    if samp:
                        P.dma(lambda e, b=b: e.dma_start(out=yS[:, :], in_=yo[b][:64, :]), reads=[H + "yo%d" % b])
                    else:
                        r0 = hf * NPH + c * 128
                        P.dma(lambda e, b=b, r0=r0: e.dma_start(out=yP[r0:r0 + 128, :], in_=yo[b][:, :]), reads=[H + "yo%d" % b])
            P.barrier()
        except _Stop:
            pass
        P.muted = False
        P.dma(lambda e: e.dma_start(out=cvP, in_=cvPs[:]), reads=["cvPs"])
        P.dma(lambda e: e.dma_start(out=cvS, in_=cvSs[:]), reads=["cvSs"])
        P.dma(lambda e: e.dma_start(out=plP, in_=plPs[:]), reads=["plPs"])
        P.dma(lambda e: e.dma_start(out=plS, in_=plSs[:]), reads=["plSs"])
        P.emit(nc)
    P.stats["dbg"] = dbg_list
    return nc, P.stats


_CACHE = {}
STOP = None


def kernel(x_prompt, x_sample, state_ssd, state_conv, state_pool, norm_g, w_in, conv_w, conv_b,
           dt_bias, a_log, d_skip, ssd_norm_g, w_branch_ssd, pool_w, pool_scale, w_branch_pool,
           gate_b, w_out, final_g):
    f = lambda a: np.ascontiguousarray(np.asarray(a, dtype=np.float32))
    x_prompt, x_sample = f(x_prompt), f(x_sample)
    if "nc" not in _CACHE:
        _CACHE["nc"] = build_program(STOP)
    nc, stats = _CACHE["nc"]
    shared = {
        "w_in": f(w_in[0]), "w_bs": f(w_branch_ssd[0]), "w_bp": f(w_branch_pool[0]), "w_out": f(w_out[0]),
        "pool_w": f(pool_w[0]),
        "ng": f(np.asarray(norm_g[0]).reshape(8, 128).T),
        "cw": f(np.asarray(conv_w[0]).reshape(4, 32, 128).transpose(2, 1, 0)),
        "cb": f(np.asarray(conv_b[0]).reshape(32, 128).T),
        "sng": f(np.asarray(ssd_norm_g[0]).reshape(16, 128).T),
        "psc": f(np.asarray(pool_scale[0]).reshape(8, 128).T),
        "gb": f(np.asarray(gate_b[0]).reshape(16, 128).T),
        "dtb": f(np.asarray(dt_bias[0]).reshape(1, 32)), "alog": f(np.asarray(a_log[0]).reshape(1, 32)),
        "dsk": f(np.asarray(d_skip[0]).reshape(1, 32)), "fg": f(np.asarray(final_g).reshape(1, 1024)),
    }
    in_maps = []
    for q in range(8):
        seq, part = q // 4, q % 4
        s0 = part * 2048
        halo = x_prompt[seq, s0 - 16:s0] if part > 0 else np.zeros((16, D), np.float32)
        xin = np.concatenate([halo, x_prompt[seq, s0:s0 + 2048], x_sample[q]], axis=0)
        pinv = np.ones((4, 16), np.float32)
        for gi, w in enumerate(POOL_W):
            for t in range(16):
                pinv[gi, t] = 1.0 / min(w, s0 + t + 1)
        m = dict(shared)
        m["xin"] = f(xin)
        m["sst"] = f(np.asarray(state_ssd[0, q]).reshape(2048, 128).T)
        scv3 = np.asarray(state_conv[0, q]).reshape(3, 32, 128).transpose(2, 1, 0)
        m["scv"] = f(np.concatenate([np.zeros((128, 32, 1), np.float32), scv3], axis=2))
        m["spl"] = f(np.asarray(state_pool[0, q]).reshape(15, 8, 128).transpose(2, 1, 0))
        m["pinv"] = f(pinv.reshape(1, 64))
        npre_rows = NPRE * NPH + 16
        xp = np.zeros((npre_rows, D), np.float32)
        have = min(s0, npre_rows)
        if have > 0:
            xp[npre_rows - have:] = x_prompt[seq, s0 - have:s0]
        pmv = np.zeros((1, 8), np.float32)
        for k in range(NPRE):
            if s0 - NPRE * NPH + k * NPH >= 0:
                pmv[0, k] = 1.0
        m["xpre"] = xp
        m["pm"] = pmv
        in_maps.append(m)
    res = run_bass_kernel_spmd(nc, in_maps, core_ids=list(range(8)))
    R = res.results
    global LAST_RESULTS
    LAST_RESULTS = R
    y_prompt = np.zeros((2, 8192, D), np.float32)
    y_sample = np.zeros((8, 64, D), np.float32)
    ssd_p = np.zeros((1, 2, 32, 64, 128), np.float32)
    ssd_s = np.zeros((1, 8, 32, 64, 128), np.float32)
    conv_p = np.zeros((1, 2, 3, 4096), np.float32)
    conv_s = np.zeros((1, 8, 3, 4096), np.float32)
    pool_p = np.zeros((1, 2, 15, 1024), np.float32)
    pool_s = np.zeros((1, 8, 15, 1024), np.float32)
    for q in range(8):
        seq, part = q // 4, q % 4
        r = R[q]
        y_prompt[seq, part * 2048:(part + 1) * 2048] = r["yP"]
        y_sample[q] = r["yS"]
        ssd_s[0, q] = np.asarray(r["hSo"]).T.reshape(32, 64, 128)
        conv_s[0, q] = np.asarray(r["cvS"]).transpose(2, 1, 0).reshape(3, 4096)
        pool_s[0, q] = np.asarray(r["plS"]).transpose(2, 1, 0).reshape(15, 1024)
        if part == 3:
            ssd_p[0, seq] = np.asarray(r["hPo"]).T.reshape(32, 64, 128)
            conv_p[0, seq] = np.asarray(r["cvP"]).transpose(2, 1, 0).reshape(3, 4096)
            pool_p[0, seq] = np.asarray(r["plP"]).transpose(2, 1, 0).reshape(15, 1024)
    return (y_prompt, y_sample, ssd_p, ssd_s, conv_p, conv_s, pool_p, pool_s)
```
